# Optimizing a Trainium2 kernel written in Bass

```python
import jax, jax.numpy as jnp
from jax import lax
import numpy as np

D_MODEL = 1024
BATCH = 8
SEQ = 4096
DEPTH = 2

GRID_W = 64
CTX_LEN = 256
EPS = 1e-6
A_WIDTH = 384
A_HEADS = 8
A_HD = A_WIDTH // A_HEADS
CONV_W = 4
RG_C = 8.0
B_HEADS = 4
B_DK = 96
B_DV = 96
B_WIDTH = B_HEADS * B_DK
B_CHUNK = 64
C_WIDTH = 256
C_HEADS = 4
C_HD = C_WIDTH // C_HEADS
C_CHUNK = 128
MIX_WIDTH = A_WIDTH + B_WIDTH + C_WIDTH
IN_SIZES = (A_WIDTH, A_WIDTH, B_WIDTH, B_WIDTH, B_WIDTH, B_WIDTH, B_WIDTH, C_WIDTH, C_WIDTH, C_WIDTH)
IN_COLS = sum(IN_SIZES)

kernel_name = "hybrid_rglru_hgrn2_chunkmlp_prefix_dit"


def _rmsnorm(x, w):
    xf = x.astype(jnp.float32)
    return xf * lax.rsqrt(jnp.mean(xf * xf, axis=-1, keepdims=True) + EPS) * w.astype(jnp.float32)


def _sincos_2d(rows, dim, dtype):
    r, col = jnp.meshgrid(jnp.arange(rows, dtype=jnp.float32), jnp.arange(GRID_W, dtype=jnp.float32), indexing='ij')
    r = r.reshape(-1)
    col = col.reshape(-1)
    q = dim // 4
    omega = 1.0 / (10000.0 ** (jnp.arange(q, dtype=jnp.float32) / q))
    ar = r[:, None] * omega
    ac = col[:, None] * omega
    return jnp.concatenate([jnp.sin(ar), jnp.cos(ar), jnp.sin(ac), jnp.cos(ac)], axis=-1).astype(dtype)


def _dwconv(x, w, b):
    out = lax.conv_general_dilated(
        x, w[:, None, :].astype(jnp.float32), window_strides=(1,),
        padding=[(CONV_W // 2, CONV_W - 1 - CONV_W // 2)],
        dimension_numbers=('NWC', 'WIO', 'NWC'), feature_group_count=x.shape[-1])
    return out + b.astype(jnp.float32)


def _rglru_coeffs(x, w_r, b_r, w_i, b_i, lam):
    bn, L, C = x.shape
    xh = x.reshape(bn, L, A_HEADS, A_HD)
    r = jax.nn.sigmoid(jnp.einsum('blhi,hij->blhj', xh, w_r.astype(jnp.float32)).reshape(bn, L, C) + b_r.astype(jnp.float32))
    i = jax.nn.sigmoid(jnp.einsum('blhi,hij->blhj', xh, w_i.astype(jnp.float32)).reshape(bn, L, C) + b_i.astype(jnp.float32))
    log_a = -RG_C * r * jax.nn.softplus(-lam.astype(jnp.float32))
    a = jnp.exp(log_a)
    u = jnp.sqrt(-jnp.expm1(2.0 * log_a)) * (i * x)
    return a, u


def _linear_scan(a, u, h0):
    u = u.at[:, 0].add(a[:, 0] * h0)

    def comb(lhs, rhs):
        a1, b1 = lhs
        a2, b2 = rhs
        return a1 * a2, a2 * b1 + b2

    _, h = lax.associative_scan(comb, (a, u), axis=1)
    return h


def _rglru_direction(xc_ctx, xc_lat, w_r, b_r, w_i, b_i, lam):
    a, u = _rglru_coeffs(xc_ctx, w_r, b_r, w_i, b_i, lam)
    h_ctx = _linear_scan(a, u, jnp.zeros_like(u[:, 0]))
    a, u = _rglru_coeffs(xc_lat, w_r, b_r, w_i, b_i, lam)
    h_lat = _linear_scan(a, u, h_ctx[:, -1])
    return h_ctx, h_lat


def _rglru_mixer(xa_ctx, xa_lat, conv_w, conv_b, w_r, b_r, w_i, b_i, lam, need_ctx_out):
    xc_c = _dwconv(xa_ctx.astype(jnp.float32), conv_w, conv_b)
    xc_l = _dwconv(xa_lat.astype(jnp.float32), conv_w, conv_b)
    hc_f, hl_f = _rglru_direction(xc_c, xc_l, w_r[0], b_r[0], w_i[0], b_i[0], lam[0])
    hc_b, hl_b = _rglru_direction(jnp.flip(xc_c, 1), jnp.flip(xc_l, 1), w_r[1], b_r[1], w_i[1], b_i[1], lam[1])
    y_lat = hl_f + jnp.flip(hl_b, 1)
    y_ctx = hc_f + jnp.flip(hc_b, 1) if need_ctx_out else None
    return y_ctx, y_lat


def _gla_chunked(q, k, v, logf, s0, want_out):
    bn, L, H, DK = q.shape
    DV = v.shape[-1]
    n = L // B_CHUNK
    rs = lambda t: t.reshape(bn, n, B_CHUNK, H, t.shape[-1])
    q, k, v, logf = rs(q), rs(k), rs(v), rs(logf)
    G = jnp.cumsum(logf, axis=2)
    G_last = G[:, :, -1]
    k_end = k * jnp.exp(G_last[:, :, None] - G)
    chunk_kv = jnp.einsum('bnchd,bnchv->bnhdv', k_end, v)
    decay = jnp.exp(G_last)

    def step(S, inp):
        dec, kv = inp
        return dec[..., None] * S + kv, S

    S_fin, S_starts = lax.scan(step, s0, (jnp.moveaxis(decay, 1, 0), jnp.moveaxis(chunk_kv, 1, 0)))
    if not want_out:
        return None, S_fin
    S_starts = jnp.moveaxis(S_starts, 0, 1)
    qg = q * jnp.exp(G)
    kg = k * jnp.exp(-G)
    scores = jnp.einsum('bnchd,bnshd->bnhcs', qg, kg)
    mask = jnp.tril(jnp.ones((B_CHUNK, B_CHUNK), dtype=bool))
    scores = jnp.where(mask, scores, 0.0)
    o = jnp.einsum('bnhcs,bnshv->bnchv', scores, v) + jnp.einsum('bnchd,bnhdv->bnchv', qg, S_starts)
    return o.reshape(bn, L, H, DV), S_fin


def _hgrn2_feats(q_raw, ff_raw, fb_raw, i_raw, lb):
    heads = lambda t: t.astype(jnp.float32).reshape(t.shape[0], t.shape[1], B_HEADS, -1)
    lbh = lb.reshape(B_HEADS, B_DK)
    q = jax.nn.silu(heads(q_raw))
    v = heads(i_raw)
    dirs = []
    for fr in (ff_raw, fb_raw):
        f = lbh + (1.0 - lbh) * jax.nn.sigmoid(heads(fr))
        dirs.append((1.0 - f, jnp.log(f)))
    return q, v, dirs


def _hgrn2_mixer(feats_ctx, feats_lat, norm_w, need_ctx_out):
    qc, vc, dc = feats_ctx
    ql, vl, dl = feats_lat
    bn = ql.shape[0]
    s0 = jnp.zeros((bn, B_HEADS, B_DK, B_DV), jnp.float32)
    fl = lambda t: jnp.flip(t, 1)
    oc_f, sc_f = _gla_chunked(qc, dc[0][0], vc, dc[0][1], s0, need_ctx_out)
    ol_f, _ = _gla_chunked(ql, dl[0][0], vl, dl[0][1], sc_f, True)
    oc_b, sc_b = _gla_chunked(fl(qc), fl(dc[1][0]), fl(vc), fl(dc[1][1]), s0, need_ctx_out)
    ol_b, _ = _gla_chunked(fl(ql), fl(dl[1][0]), fl(vl), fl(dl[1][1]), sc_b, True)
    g = norm_w.astype(jnp.float32).reshape(B_HEADS, B_DV)

    def head_norm(o):
        o = o * lax.rsqrt(jnp.mean(o * o, axis=-1, keepdims=True) + EPS) * g
        return o.reshape(o.shape[0], o.shape[1], B_WIDTH)

    y_lat = head_norm(ol_f + fl(ol_b))
    y_ctx = head_norm(oc_f + fl(oc_b)) if need_ctx_out else None
    return y_ctx, y_lat


def _chunk_mlp(u_raw, v_raw, norm_w, ws, bs):
    u = jax.nn.gelu(u_raw.astype(jnp.float32))
    v = jax.nn.gelu(v_raw.astype(jnp.float32))
    mu = jnp.mean(v, axis=-1, keepdims=True)
    var = jnp.mean(jnp.square(v - mu), axis=-1, keepdims=True)
    v = (v - mu) * lax.rsqrt(var + EPS) * norm_w.astype(jnp.float32)
    bn, L, _ = v.shape
    n = L // C_CHUNK
    vh = v.reshape(bn, n, C_CHUNK, C_HEADS, C_HD)
    mixed = jnp.einsum('hst,bnthe->bnshe', ws.astype(jnp.float32), vh) + bs.astype(jnp.float32).T[None, None, :, :, None]
    return u * mixed.reshape(bn, L, C_WIDTH)


def setup_inputs(seed: int = 0) -> dict:
    key = jax.random.key(seed)
    ks = jax.random.split(key, 24)
    f32 = jnp.float32
    nrm = lambda k, shape, s: jax.random.normal(k, shape, f32) * s
    u = jax.random.uniform(ks[14], (DEPTH, 2, A_WIDTH), f32, 0.9, 0.999)
    p = u ** (1.0 / RG_C)
    a_lambda = jnp.log(p) - jnp.log1p(-p)
    return {
        "x": nrm(ks[0], (BATCH, SEQ, D_MODEL), 1.0),
        "c": nrm(ks[1], (BATCH, D_MODEL), 1.0),
        "ctx": nrm(ks[2], (BATCH, CTX_LEN, D_MODEL), 1.0),
        "c_ctx": nrm(ks[3], (D_MODEL,), 1.0),
        "norm_w": 1.0 + nrm(ks[4], (DEPTH, D_MODEL), 0.02),
        "w_mod": nrm(ks[5], (DEPTH, D_MODEL, 3 * D_MODEL), D_MODEL ** -0.5),
        "b_mod": nrm(ks[6], (DEPTH, 3 * D_MODEL), 0.02),
        "w_in": nrm(ks[7], (DEPTH, D_MODEL, IN_COLS), D_MODEL ** -0.5),
        "a_conv_w": nrm(ks[8], (DEPTH, CONV_W, A_WIDTH), CONV_W ** -0.5),
        "a_conv_b": nrm(ks[9], (DEPTH, A_WIDTH), 0.02),
        "a_wr": nrm(ks[10], (DEPTH, 2, A_HEADS, A_HD, A_HD), A_HD ** -0.5),
        "a_br": nrm(ks[11], (DEPTH, 2, A_WIDTH), 0.02),
        "a_wi": nrm(ks[12], (DEPTH, 2, A_HEADS, A_HD, A_HD), A_HD ** -0.5),
        "a_bi": nrm(ks[13], (DEPTH, 2, A_WIDTH), 0.02),
        "a_lambda": a_lambda,
        "b_lb_logits": nrm(ks[15], (DEPTH + 1, B_WIDTH), 0.1),
        "b_norm_w": 1.0 + nrm(ks[16], (DEPTH, B_WIDTH), 0.02),
        "c_norm_w": 1.0 + nrm(ks[17], (DEPTH, C_WIDTH), 0.02),
        "c_ws": nrm(ks[18], (DEPTH, C_HEADS, C_CHUNK, C_CHUNK), C_CHUNK ** -0.5),
        "c_bs": nrm(ks[19], (DEPTH, C_HEADS, C_CHUNK), 0.02),
        "w_out": nrm(ks[20], (DEPTH, MIX_WIDTH, D_MODEL), MIX_WIDTH ** -0.5),
        "final_norm_w": 1.0 + nrm(ks[21], (D_MODEL,), 0.02),
    }


def reference(x, c, ctx, c_ctx, norm_w, w_mod, b_mod, w_in, a_conv_w, a_conv_b, a_wr, a_br, a_wi, a_bi,
              a_lambda, b_lb_logits, b_norm_w, c_norm_w, c_ws, c_bs, w_out, final_norm_w):
    n_lat = x.shape[1]
    ROWS = n_lat // GRID_W
    x = x + _sincos_2d(ROWS, D_MODEL, x.dtype)[None]
    silu_c = jax.nn.silu(c.astype(jnp.float32))
    silu_cc = jax.nn.silu(c_ctx.astype(jnp.float32))
    lbs = jnp.cumsum(jax.nn.softmax(b_lb_logits.astype(jnp.float32), axis=0), axis=0)
    cuts = np.cumsum(IN_SIZES)[:-1].tolist()
    for l in range(DEPTH):
        need_ctx_out = l < DEPTH - 1
        mod = silu_c @ w_mod[l] + b_mod[l]
        mod_c = silu_cc @ w_mod[l] + b_mod[l]
        sh, sc, g = jnp.split(mod, 3, axis=-1)
        sh_c, sc_c, g_c = jnp.split(mod_c, 3, axis=-1)
        h = _rmsnorm(x, norm_w[l]) * (1.0 + sc[:, None]) + sh[:, None]
        hc = _rmsnorm(ctx, norm_w[l]) * (1.0 + sc_c) + sh_c
        p = jnp.split(h @ w_in[l], cuts, axis=-1)
        pc = jnp.split(hc @ w_in[l], cuts, axis=-1)
        ya_c, ya_l = _rglru_mixer(pc[0], p[0], a_conv_w[l], a_conv_b[l], a_wr[l], a_br[l], a_wi[l], a_bi[l],
                                  a_lambda[l], need_ctx_out)
        yb_c, yb_l = _hgrn2_mixer(_hgrn2_feats(pc[2], pc[3], pc[4], pc[5], lbs[l]),
                                  _hgrn2_feats(p[2], p[3], p[4], p[5], lbs[l]), b_norm_w[l], need_ctx_out)
        yc_l = _chunk_mlp(p[7], p[8], c_norm_w[l], c_ws[l], c_bs[l])
        y = jnp.concatenate([ya_l * jax.nn.silu(p[1]), yb_l * jax.nn.silu(p[6]), yc_l * jax.nn.silu(p[9])], axis=-1)
        x = x + (g[:, None] * (y.astype(x.dtype) @ w_out[l])).astype(x.dtype)
        if need_ctx_out:
            yc_c = _chunk_mlp(pc[7], pc[8], c_norm_w[l], c_ws[l], c_bs[l])
            yctx = jnp.concatenate([ya_c * jax.nn.silu(pc[1]), yb_c * jax.nn.silu(pc[6]), yc_c * jax.nn.silu(pc[9])], axis=-1)
            ctx = ctx + (g_c * (yctx.astype(ctx.dtype) @ w_out[l])).astype(ctx.dtype)
    return _rmsnorm(x, final_norm_w).astype(x.dtype)
```

```python
import numpy as np
from contextlib import ExitStack
import concourse.bass as bass
import concourse.mybir as mybir
from concourse.ap import AP
from concourse.bass_utils import run_bass_kernel_spmd

F32 = mybir.dt.float32
BF16 = mybir.dt.bfloat16
I32 = mybir.dt.int32
AF = mybir.ActivationFunctionType
ALU = mybir.AluOpType

D = 1024
TC = 256
TL = 4096
T = TC + TL
NCOLS = 3456
EPS = 1e-6
PI = float(np.pi)
BLOCKS = [(0, 2)] + [(256 + 512 * i, 4) for i in range(8)]

P_NORMW, P_CONVW, P_CONVB, P_BR, P_BI, P_LAM, P_CNW, P_BNW, P_LBL = 0, 8, 20, 23, 29, 35, 41, 43, 47
NPRM = 64
C_ID, C_MF, C_MB, C_ONE, C_JROW, C_PH, C_RCOL, C_CCOL, C_SMASK = 0, 128, 256, 384, 512, 1024, 1536, 1568, 1569
NCST = C_SMASK + 2049


class Tk:
    __slots__ = ("w", "r", "multi", "ws")

    def __init__(self):
        self.w = None
        self.r = []
        self.multi = False
        self.ws = []


class Sched:
    ENGS = ("pe", "act", "dve", "pool", "sp")

    def __init__(self, nc):
        self.nc = nc
        self.ops = {e: [] for e in self.ENGS}
        self.semcount = {}
        self.known = {e: {} for e in self.ENGS}
        self.epoch = 0
        self.rr = 0
        self.rrq = {}

    def new_epoch(self):
        self.epoch += 1

    def _bump(self, key, inc):
        v = self.semcount.get(key, 0) + inc
        self.semcount[key] = v
        return (key, v)

    def _collect(self, eng, reads, writes):
        waits = {}
        kn = self.known[eng]

        def need(ev):
            if ev is None:
                return
            k, v = ev
            if kn.get(k, 0) >= v:
                return
            if waits.get(k, 0) < v:
                waits[k] = v
        for t in reads:
            if t.multi:
                for ev in t.ws:
                    need(ev)
            else:
                need(t.w)
        for t in writes:
            if not t.multi:
                need(t.w)
            for ev in t.r:
                need(ev)
        for k, v in waits.items():
            kn[k] = v
        return waits

    def _commit(self, ev, reads, writes):
        for t in reads:
            t.r.append(ev)
            if len(t.r) > 16:
                best = {}
                for k, v in t.r:
                    if best.get(k, 0) < v:
                        best[k] = v
                t.r = list(best.items())
        for t in writes:
            if t.multi:
                t.ws.append(ev)
                if len(t.ws) > 16:
                    best = {}
                    for k, v in t.ws:
                        if best.get(k, 0) < v:
                            best[k] = v
                    t.ws = list(best.items())
            else:
                t.w = ev
                t.r = []

    def op(self, eng, fn, reads=(), writes=()):
        reads = [b.tk for b in reads]
        writes = [b.tk for b in writes]
        waits = self._collect(eng, reads, writes)
        ev = self._bump(("E", eng, self.epoch), 1)
        self.ops[eng].append((list(waits.items()), fn, ev, 1))
        self._commit(ev, reads, writes)

    def dma(self, out, in_, reads=(), writes=(), eng="sp", key=None):
        reads = [b.tk for b in reads]
        writes = [b.tk for b in writes]
        if key is None:
            n = self.rrq.get(eng, 0)
            self.rrq[eng] = n + 1
            key = ("D", eng, n % (32 if eng == "sp" else 8))
        else:
            key = ("D", key)
        waits = self._collect(eng, reads, writes)
        prev = self.semcount.get(key, 0)
        if prev and self.known[eng].get(key, 0) < prev:
            waits[key] = max(waits.get(key, 0), prev)
            self.known[eng][key] = prev
        ev = self._bump(key, 16)

        def fn(e, out=out, in_=in_):
            return e.dma_start(out=out, in_=in_)
        self.ops[eng].append((list(waits.items()), fn, ev, 16))
        self._commit(ev, reads, writes)

    def barrier(self):
        evs = list(self.semcount.items())
        for e in self.ENGS:
            waits = []
            for k, v in evs:
                if self.known[e].get(k, 0) < v:
                    waits.append((k, v))
                    self.known[e][k] = v
            if waits:
                self.ops[e].append((waits, None, None, 0))

    def emit(self):
        nc = self.nc
        keys = list(self.semcount.keys())
        with ExitStack() as es:
            sems = {}
            for i, k in enumerate(keys):
                sems[k] = es.enter_context(nc.semaphore("s%d" % i))
            block = es.enter_context(nc.Block())

            def run(engname):
                def body(e):
                    for waits, fn, ev, inc in self.ops[engname]:
                        for k, v in waits:
                            e.wait_ge(sems[k], v)
                        if fn is not None:
                            fn(e).then_inc(sems[ev[0]], inc)
                return body
            block.tensor(run("pe"))
            block.scalar(run("act"))
            block.vector(run("dve"))
            block.gpsimd(run("pool"))
            block.sync(run("sp"))


class Buf:
    __slots__ = ("ap", "tk")

    def __init__(self, ap):
        self.ap = ap
        self.tk = Tk()

    def __getitem__(self, key):
        return self.ap[key]


def rev(ap):
    a = [list(x) for x in ap.ap]
    off = ap.offset + a[-1][0] * (a[-1][1] - 1)
    a[-1][0] = -a[-1][0]
    return AP(ap.tensor, off, a)


def bcast_mid(ap2, n):
    a = [list(x) for x in ap2.ap]
    return AP(ap2.tensor, ap2.offset, [a[0], [0, n], a[1]])


def bcast_last(ap2, n):
    a = [list(x) for x in ap2.ap]
    return AP(ap2.tensor, ap2.offset, [a[0], a[1], [0, n]])


class Builder:
    def __init__(self, debug=None, stop_after=None):
        self.debug = debug or ()
        self.stop_after = stop_after
        self.nc = nc = bass.Bass("TRN2", target_bir_lowering=False)
        self.S = Sched(nc)
        self.es = ExitStack()
        self.dbg_out = {}
        self.halt = False

    def dram_in(self, name, shape, dt=F32):
        return self.nc.dram_tensor(name, list(shape), dt, kind="ExternalInput").ap()

    def dram_scratch(self, name, shape, dt):
        kind = "ExternalOutput" if name in self.debug else "Internal"
        ap = self.nc.dram_tensor(name, list(shape), dt, kind=kind).ap()
        if name in self.debug:
            self.dbg_out[name] = ap
        return ap

    def setup_arena(self):
        nc = self.nc
        self.NW = 52500
        ar = self.es.enter_context(nc.sbuf_tensor("arena", [128, self.NW], F32))
        self.arena = ar
        self.top = 0
        self.top_hi = self.NW
        self.psum = []
        for i in range(8):
            p = self.es.enter_context(nc.psum_tensor("ps%d" % i, [128, 512], F32))
            self.psum.append(Buf(p[:]))
        self.ps_i = 0

    def alloc(self, nelem, dt=F32, parts=128):
        size = 4 if dt in (F32, I32) else 2
        words = (nelem * size + 3) // 4
        assert self.top + words <= self.top_hi, "SBUF arena overflow %d > %d" % (self.top + words, self.top_hi)
        v = self.arena[0:parts, self.top:self.top + words]
        self.top += words
        self.peak = max(getattr(self, "peak", 0), self.top)
        if dt != F32:
            v = v.bitcast(dt)
            v = v[:, 0:nelem]
        return Buf(v)

    def mark(self):
        return self.top

    def alloc_win(self):
        words = 8 * NCOLS // 2
        self.top_hi = self.NW - words
        assert self.top <= self.top_hi, "SBUF arena overflow (win)"
        v = self.arena[:, self.top_hi:self.NW].bitcast(BF16)
        self.win = Buf(v)
        self.winv = v.rearrange("p (k n) -> p k n", n=NCOLS)

    def win_steps(self, l, stage, engs, nsplit=2):
        steps = []
        H = NCOLS // nsplit
        for k in range(8):
            for hf in range(nsplit):
                idx = nsplit * k + hf

                def step(k=k, hf=hf, idx=idx):
                    sb = stage[idx % len(stage)]
                    self.dma(sb.ap, self.win_d[l, k * 128:(k + 1) * 128, hf * H:(hf + 1) * H], writes=[sb])
                    self.copy(engs[idx % len(engs)], self.winv[:, k, hf * H:(hf + 1) * H], sb.ap, [sb], [self.win])
                steps.append(step)
        return steps

    def release(self, m):
        self.S.barrier()
        self.peaks = getattr(self, "peaks", [])
        self.peaks.append((self.peak, m))
        self.peak = m
        self.top = m

    def ps(self):
        b = self.psum[self.ps_i % 8]
        self.ps_i += 1
        return b

    def act(self, out, in_, func, reads, writes, scale=1.0, bias=0.0, accum_out=None):
        kw = dict(out=out, in_=in_, func=func, scale=scale, bias=bias)
        if accum_out is not None:
            kw["accum_out"] = accum_out
        self.S.op("act", lambda e: e.activation(**kw), reads, writes)

    def tt(self, eng, out, in0, in1, op, reads, writes):
        self.S.op(eng, lambda e: e.tensor_tensor(out=out, in0=in0, in1=in1, op=op), reads, writes)

    def ts(self, eng, out, in0, s1, s2, op0, op1, reads, writes):
        if s2 is None:
            self.S.op(eng, lambda e: e.tensor_scalar(out=out, in0=in0, scalar1=s1, scalar2=None, op0=op0), reads, writes)
        else:
            self.S.op(eng, lambda e: e.tensor_scalar(out=out, in0=in0, scalar1=s1, scalar2=s2, op0=op0, op1=op1), reads, writes)

    def stt(self, eng, out, in0, scalar, in1, op0, op1, reads, writes):
        self.S.op(eng, lambda e: e.scalar_tensor_tensor(out=out, in0=in0, scalar=scalar, in1=in1, op0=op0, op1=op1), reads, writes)

    def copy(self, eng, out, in_, reads, writes):
        if eng == "act":
            self.act(out, in_, AF.Copy, reads, writes)
        else:
            self.S.op(eng, lambda e: e.tensor_copy(out=out, in_=in_), reads, writes)

    def memset(self, eng, out, val, writes):
        self.S.op(eng, lambda e: e.memset(out, val), (), writes)

    def mm(self, out, pairs, reads, writes, start=True, stop=True):
        def fn(e):
            n = len(pairs)
            ins = None
            for i, (l, r) in enumerate(pairs):
                ins = e.matmul(out, lhsT=l, rhs=r, start=(start and i == 0), stop=(stop and i == n - 1))
            return ins
        self.S.op("pe", fn, reads, writes)

    def dma(self, out, in_, reads=(), writes=(), key=None, eng="sp"):
        self.S.dma(out, in_, reads, writes, eng=eng, key=key)

    def build(self):
        nc = self.nc
        S = self.S
        di = self.dram_in
        self.x_d = di("x", [TL, D])
        self.ctx_d = di("ctx", [TC, D])
        self.cv_d = di("cvec", [128, 16])
        self.wmod_d = di("w_mod", [2, D, 3 * D])
        self.bmod_d = di("b_mod", [2, 3 * D])
        self.win_d = di("w_in", [2, D, NCOLS])
        self.wout_d = di("w_out", [2, D, D])
        self.prm_d = di("prm", [2, 128, NPRM])
        self.agw_d = di("agw", [2, 4, 384, 384])
        self.cws_d = di("cwsT", [2, 128, 512])
        self.cbias_d = di("cbias", [2, 128, 256])
        self.fnw_d = di("fnw", [D])
        self.cst_d = di("cst", [128, NCST])
        self.out_d = nc.dram_tensor("out", [TL, D], F32, kind="ExternalOutput").ap()
        ds = self.dram_scratch
        self.AX = ds("AX", [384, T], F32)
        self.SGA = ds("SGA", [384, T], BF16)
        self.Q = ds("Q", [384, T], BF16)
        self.SGF = ds("SGF", [384, T], F32)
        self.SGB = ds("SGB", [384, T], F32)
        self.GB = ds("GB", [384, T], BF16)
        self.VT = ds("VT", [T, 384], BF16)
        self.OF = ds("OF", [384, T], F32)
        self.Y = ds("Y", [D, T], BF16)
        self.X1 = ds("X1", [T, D], F32)
        self.X0 = ds("X0", [TL, D], F32)
        self.tkX0 = Buf(None)
        self.tkAX, self.tkSGA, self.tkQ, self.tkSGF, self.tkSGB, self.tkGB, self.tkVT, self.tkOF, self.tkY, self.tkX1 = [Buf(None) for _ in range(10)]
        for b in (self.tkAX, self.tkSGA, self.tkQ, self.tkSGF, self.tkSGB, self.tkGB, self.tkVT, self.tkOF, self.tkY, self.tkX1, self.tkX0):
            b.tk.multi = True
        self.setup_arena()
        self.phase_setup()
        for l in range(2):
            if self.stop_after == ("setup", l):
                break
            S.new_epoch()
            self.phase_weights(l)
            self.phase_p1(l)
            if self.stop_after == ("p1", l):
                break
            self.phase_a(l)
            if self.halt or self.stop_after == ("a", l):
                break
            self.phase_b(l)
            if self.halt or self.stop_after == ("b", l):
                break
            self.phase_o(l)
            if self.stop_after == ("o", l):
                break
        S.barrier()
        S.emit()
        self.es.close()
        return nc

    def phase_setup(self):
        S = self.S
        al = self.alloc
        self.cst = al(NCST)
        self.dma(self.cst.ap, self.cst_d, writes=[self.cst])
        self.identb = al(128, BF16)
        self.maskf = al(128, BF16)
        self.maskb = al(128, BF16)
        self.copy("dve", self.identb.ap, self.cst[:, C_ID:C_ID + 128], [self.cst], [self.identb])
        self.copy("dve", self.maskf.ap, self.cst[:, C_MF:C_MF + 128], [self.cst], [self.maskf])
        self.copy("dve", self.maskb.ap, self.cst[:, C_MB:C_MB + 128], [self.cst], [self.maskb])
        self.ones = self.cst[:, C_ONE:C_ONE + 128]
        self.smask = self.cst[0:96, C_SMASK:C_SMASK + 2049]
        self.prm = [al(NPRM), al(NPRM)]
        for l in range(2):
            self.dma(self.prm[l].ap, self.prm_d[l], writes=[self.prm[l]])
        self.negpi = al(1)
        self.memset("pool", self.negpi.ap, -PI, [self.negpi])
        self.epsb = al(1)
        self.memset("pool", self.epsb.ap, EPS, [self.epsb])
        self.pc = al(512)
        mpos = self.mark()
        self.omega = al(512)
        self.copy("dve", self.omega.ap, self.cst[:, C_JROW:C_JROW + 512], [self.cst], [self.omega])
        self._pt = (al(512), al(512, I32), al(512))
        self.pos_half_tile(self.pc, None, rcol_ap=self.cst[:, C_CCOL:C_CCOL + 1])
        self.E_d = self.dram_scratch("Etab", [64, 512], F32)
        self.tkE = Buf(None)
        self.dma(self.E_d, self.pc[0:64, :], reads=[self.pc], writes=[self.tkE])
        self.release(mpos)
        p0 = self.prm[0]
        lg = [p0[0:96, P_LBL + 4 * i:P_LBL + 4 * i + 4] for i in range(3)]
        mx = al(4)
        self.tt("dve", mx[0:96, :], lg[0], lg[1], ALU.max, [p0], [mx])
        self.tt("dve", mx[0:96, :], mx[0:96, :], lg[2], ALU.max, [p0, mx], [mx])
        ee = al(12)
        for i in range(3):
            self.tt("dve", ee[0:96, 4 * i:4 * i + 4], lg[i], mx[0:96, :], ALU.subtract, [p0, mx], [ee])
        self.act(ee[0:96, :], ee[0:96, :], AF.Exp, [ee], [ee])
        ssum = al(4)
        self.tt("dve", ssum[0:96, :], ee[0:96, 0:4], ee[0:96, 4:8], ALU.add, [ee], [ssum])
        e01 = al(4)
        self.copy("dve", e01[0:96, :], ssum[0:96, :], [ssum], [e01])
        self.tt("dve", ssum[0:96, :], ssum[0:96, :], ee[0:96, 8:12], ALU.add, [ee, ssum], [ssum])
        S.op("dve", lambda e: e.reciprocal(out=ssum[0:96, :], in_=ssum[0:96, :]), [ssum], [ssum])
        self.lb = [al(4), al(4)]
        self.oml = [al(4), al(4)]
        self.noml = [al(4), al(4)]
        self.tt("dve", self.lb[0][0:96, :], ee[0:96, 0:4], ssum[0:96, :], ALU.mult, [ee, ssum], [self.lb[0]])
        self.tt("dve", self.lb[1][0:96, :], e01[0:96, :], ssum[0:96, :], ALU.mult, [e01, ssum], [self.lb[1]])
        for l in range(2):
            self.ts("dve", self.oml[l][0:96, :], self.lb[l][0:96, :], -1.0, 1.0, ALU.mult, ALU.add, [self.lb[l]], [self.oml[l]])
            self.ts("dve", self.noml[l][0:96, :], self.lb[l][0:96, :], 1.0, -1.0, ALU.mult, ALU.add, [self.lb[l]], [self.noml[l]])
        cv = al(16)
        self.dma(cv.ap, self.cv_d, writes=[cv])
        self.act(cv.ap, cv.ap, AF.Silu, [cv], [cv])
        crep = al(8 * 128)
        crv = crep.ap.rearrange("p (k m) -> p k m", m=128)
        self.copy("dve", crv[:, :, 0:64], bcast_last(cv[:, 0:8], 64), [cv], [crep])
        self.copy("dve", crv[:, :, 64:128], bcast_last(cv[:, 8:16], 64), [cv], [crep])
        self.scw = [[None, None], [None, None]]
        self.shT = [[None, None], [None, None]]
        self.gb = [[None, None], [None, None]]
        for l in range(2):
            for s in range(2):
                self.scw[l][s] = al(8)
                self.shT[l][s] = al(8)
                self.gb[l][s] = al(D) if (l, s) != (1, 1) else None
        m0 = self.mark()
        self.alloc_win()
        wst = [al(NCOLS // 2) for _ in range(3)]
        wsteps0 = self.win_steps(0, wst, ["act", "dve"])
        bm = al(3 * D, parts=1)
        stage = [al(3 * D) for _ in range(3)]
        modsb = al(3 * D)
        for l in range(2):
            bmv = self.bmod_d[l]
            self.dma(bm.ap, AP(bmv.tensor, bmv.offset, [[0, 1], [1, 3 * D]]), writes=[bm])
            banks = [self.ps() for _ in range(6)]
            for k in range(8):
                st = stage[k % 3]
                self.dma(st.ap, self.wmod_d[l, k * 128:(k + 1) * 128, :], writes=[st])
                for n in range(6):
                    self.mm(banks[n].ap, [(crv[:, k, :], st[:, n * 512:(n + 1) * 512])], [crep, st], [banks[n]],
                            start=(k == 0), stop=False)
                for _ in range(2):
                    if wsteps0:
                        wsteps0.pop(0)()
            for n in range(6):
                self.mm(banks[n].ap, [(self.ones[0:1, :], bm[0:1, n * 512:(n + 1) * 512])], [self.cst, bm], [banks[n]],
                        start=False, stop=True)
                self.copy("act" if n % 2 else "dve", modsb[:, n * 512:(n + 1) * 512], banks[n].ap, [banks[n]], [modsb])
            for s in range(2):
                r0 = 64 * s
                for n in range(2):
                    if self.gb[l][s] is None:
                        continue
                    pb = self.ps()
                    self.mm(pb.ap, [(self.ones[r0:r0 + 1, :], modsb[r0:r0 + 1, 2 * D + n * 512:2 * D + (n + 1) * 512])],
                            [self.cst, modsb], [pb])
                    self.copy("dve", self.gb[l][s][:, n * 512:(n + 1) * 512], pb.ap, [pb], [self.gb[l][s]])
                pt = self.ps()
                for k in range(8):
                    self.mm(pt[:, k:k + 1], [(modsb[r0:r0 + 1, D + k * 128:D + (k + 1) * 128], self.ones[r0:r0 + 1, 0:1])],
                            [modsb, self.cst], [pt])
                    self.mm(pt[:, 8 + k:9 + k], [(modsb[r0:r0 + 1, k * 128:(k + 1) * 128], self.ones[r0:r0 + 1, 0:1])],
                            [modsb, self.cst], [pt])
                self.stt("dve", self.scw[l][s].ap, pt[:, 0:8], 1.0, self.prm[l][:, P_NORMW:P_NORMW + 8], ALU.add, ALU.mult,
                         [pt, self.prm[l]], [self.scw[l][s]])
                self.copy("dve", self.shT[l][s].ap, pt[:, 8:16], [pt], [self.shT[l][s]])
        while wsteps0:
            wsteps0.pop(0)()
        self.release(m0)

    def phase_weights(self, l):
        al = self.alloc
        self.m_layer = self.mark()
        self.agw = al(4 * 3 * 384, BF16)
        self.agwv = self.agw.ap.rearrange("p (g k n) -> p g k n", g=4, k=3)
        m = self.mark()
        st = [al(3 * 384), al(3 * 384)]
        for g in range(4):
            s = st[g % 2]
            self.dma(s.ap.rearrange("p (k n) -> p k n", k=3), self.agw_d[l, g].rearrange("(k p) n -> p k n", p=128), writes=[s])
            self.copy("act" if g % 2 else "dve", self.agwv[:, g, :, :], s.ap.rearrange("p (k n) -> p k n", k=3), [s], [self.agw])
        self.release(m)
        self.cws = al(512, BF16)
        self.cbias = al(256)
        m = self.mark()
        s = al(512)
        self.dma(s.ap, self.cws_d[l], writes=[s])
        self.copy("dve", self.cws.ap, s.ap, [s], [self.cws])
        self.dma(self.cbias.ap, self.cbias_d[l], writes=[self.cbias])
        self.release(m)
        self.halfc = al(6)
        prm = self.prm[l]
        self.act(self.halfc.ap, prm[:, P_LAM:P_LAM + 6], AF.Exp, [prm], [self.halfc], scale=-1.0)
        self.act(self.halfc.ap, self.halfc.ap, AF.Ln, [self.halfc], [self.halfc], bias=1.0)
        self.ts("dve", self.halfc.ap, self.halfc.ap, -4.0, None, ALU.mult, None, [self.halfc], [self.halfc])
        self.ahalf = al(18)
        self.ts("dve", self.ahalf[:, 0:6], self.halfc.ap, 0.5, None, ALU.mult, None, [self.halfc], [self.ahalf])
        self.ts("dve", self.ahalf[:, 6:12], prm[:, P_BR:P_BR + 6], 0.5, None, ALU.mult, None, [prm], [self.ahalf])
        self.ts("dve", self.ahalf[:, 12:18], prm[:, P_BI:P_BI + 6], 0.5, None, ALU.mult, None, [prm], [self.ahalf])

    def phase_p1(self, l):
        al = self.alloc
        S = self.S
        m_p1 = self.mark()
        prm = self.prm[l]
        win, winv = self.win, self.winv
        xt = [al(D) for _ in range(8)]
        xn = [al(D, BF16) for _ in range(4)]
        hT = [al(8 * 512, BF16) for _ in range(2)]
        ss = [al(4) for _ in range(2)]
        rstd = [al(4) for _ in range(2)]
        prt = [al(512) for _ in range(4 if l == 0 else 0)]
        sq_junk = al(D, BF16)
        NF32, NBF = (6, 9) if l == 0 else (8, 12)
        f32t = [al(512) for _ in range(NF32)]
        bft = [al(512, BF16) for _ in range(NBF)]
        vtt = [al(384, BF16) for _ in range(2)]
        cvt = [al(256) for _ in range(4)]
        vhat = [al(256, BF16) for _ in range(4)]
        stats = al(4 * 6)
        mv = al(8)
        cu = [al(1024, BF16) for _ in range(1)]
        cg = [al(1024, BF16) for _ in range(1)]
        ug = [al(1024, BF16) for _ in range(1)]
        ycb = [al(1024, BF16) for _ in range(2)]
        t1 = [al(128) for _ in range(2)]
        cnt = {"f": 0, "b": 0, "v": 0}
        NBLK = len(BLOCKS)

        def src_rows(t0):
            if l == 0:
                return self.ctx_d[t0:t0 + 128, :] if t0 < TC else self.x_d[t0 - TC:t0 - TC + 128, :]
            return self.X1[t0:t0 + 128, :]

        def xbuf(bi, i):
            return xt[(bi % 2) * 4 + i]

        def loads(bi):
            if bi >= NBLK:
                return
            c0, nt = BLOCKS[bi]
            for i in range(nt):
                rd = [self.tkX1] if l > 0 else []
                self.dma(xbuf(bi, i).ap, src_rows(c0 + i * 128), reads=rd, writes=[xbuf(bi, i)])
                if l == 0 and c0 >= TC:
                    tglob = (c0 - TC) // 128 + i
                    pr = prt[i]
                    for half in range(2):
                        src = AP(self.E_d.tensor, self.E_d.offset + (2 * tglob + half) * 512, [[0, 64], [1, 512]])
                        self.dma(pr[half * 64:(half + 1) * 64, :], src, reads=[self.tkE], writes=[pr])

        def part0(bi):
            if bi >= NBLK or l != 0:
                return
            c0, nt = BLOCKS[bi]
            if c0 < TC:
                return
            for i in range(nt):
                xb = xbuf(bi, i)
                pr = prt[i]
                self.tt("dve", xb[:, 0:512], xb[:, 0:512], pr.ap, ALU.add, [xb, pr], [xb])
                self.tt("dve", xb[:, 512:1024], xb[:, 512:1024], self.pc.ap, ALU.add, [xb, self.pc], [xb])
                self.dma(self.X0[c0 - TC + i * 128:c0 - TC + (i + 1) * 128, :], xb.ap, reads=[xb], writes=[self.tkX0])

        def part1(bi):
            if bi >= NBLK:
                return
            c0, nt = BLOCKS[bi]
            sb, rb = ss[bi % 2], rstd[bi % 2]
            self.memset("pool", sb.ap, 0.0, [sb])
            for i in range(nt):
                xb = xbuf(bi, i)
                self.act(sq_junk.ap, xb.ap, AF.Square, [xb], [sq_junk, sb], accum_out=sb[:, i:i + 1])
            self.act(rb[:, 0:nt], sb[:, 0:nt], AF.Sqrt, [sb, self.epsb], [rb], scale=1.0 / D, bias=self.epsb.ap)
            S.op("dve", lambda e, rb=rb, nt=nt: e.reciprocal(out=rb[:, 0:nt], in_=rb[:, 0:nt]), [rb], [rb])
            for i in range(nt):
                self.ts("dve", xn[i].ap, xbuf(bi, i).ap, rb[:, i:i + 1], None, ALU.mult, None, [xbuf(bi, i), rb], [xn[i]])

        def part2(bi):
            if bi >= NBLK:
                return
            c0, nt = BLOCKS[bi]
            s_ctx = 1 if c0 < TC else 0
            h = hT[bi % 2]
            hv = h.ap.rearrange("p (k t) -> p k t", t=512)
            scw, shT = self.scw[l][s_ctx], self.shT[l][s_ctx]
            for i in range(nt):
                xnb = xn[i]
                pb = self.ps()
                pbv = pb.ap.bitcast(BF16)
                S.op("pe", lambda e, pbv=pbv, xnb=xnb: [e.transpose(out=pbv[:, k * 128:(k + 1) * 128], in_=xnb[:, k * 128:(k + 1) * 128],
                                                                       identity=self.identb.ap) for k in range(8)][-1],
                     [xnb, self.identb], [pb])
                pv = pbv.rearrange("p (k t) -> p k t", t=128)
                for k in range(8):
                    if k % 2 == 0:
                        self.ts("dve", hv[:, k, i * 128:(i + 1) * 128], pv[:, k, :], scw[:, k:k + 1], shT[:, k:k + 1], ALU.mult, ALU.add,
                                [pb, scw, shT], [h])
                    else:
                        self.act(hv[:, k, i * 128:(i + 1) * 128], pv[:, k, :], AF.Identity, [pb, scw, shT], [h],
                                 scale=scw[:, k:k + 1], bias=shT[:, k:k + 1])
            if "hT" in self.debug and bi == 1 and l == 0:
                d = self.dram_scratch("hT", [128, 8 * 512], BF16)
                self.dma(d, h.ap, reads=[h])

        def f32tile():
            b = f32t[cnt["f"] % NF32]
            cnt["f"] += 1
            return b

        def bftile():
            b = bft[cnt["b"] % NBF]
            cnt["b"] += 1
            return b

        def inproj(bi, mid_hook):
            c0, nt = BLOCKS[bi]
            ncol = nt * 128
            h = hT[bi % 2]
            hv = h.ap.rearrange("p (k t) -> p k t", t=512)
            cub, cgb, ugb, ycbb = cu[0], cg[0], ug[0], ycb[bi % 2]

            def proj(col0, M):
                pb = self.ps()
                self.mm(pb[0:M, 0:ncol], [(winv[:, k, col0:col0 + M], hv[:, k, 0:ncol]) for k in range(8)], [win, h], [pb])
                return pb
            for j in range(3):
                pb = proj(384 + j * 128, 128)
                o = bftile()
                self.act(o[:, 0:ncol], pb[:, 0:ncol], AF.Silu, [pb], [o])
                self.dma(self.SGA[j * 128:(j + 1) * 128, c0:c0 + ncol], o[:, 0:ncol], reads=[o], writes=[self.tkSGA], eng="act")
            def bsplit(colb, dst, tk, func, tile_fn):
                for s3 in range(3):
                    pb = proj(colb + s3 * 128, 128)
                    o = tile_fn()
                    self.act(o[:, 0:ncol], pb[:, 0:ncol], func, [pb], [o])
                    self.dma(dst[s3 * 128:(s3 + 1) * 128, c0:c0 + ncol], o[:, 0:ncol], reads=[o], writes=[tk], eng="act")
            bsplit(768, self.Q, self.tkQ, AF.Silu, bftile)
            bsplit(2304, self.GB, self.tkGB, AF.Silu, bftile)
            for j in range(2):
                pb = proj(3200 + j * 128, 128)
                self.act(cgb[:, j * 512:j * 512 + ncol], pb[:, 0:ncol], AF.Silu, [pb], [cgb])
            mid_hook()
            bsplit(1152, self.SGF, self.tkSGF, AF.Sigmoid, f32tile)
            bsplit(1536, self.SGB, self.tkSGB, AF.Sigmoid, f32tile)
            for j in range(2):
                pb = proj(2688 + j * 128, 128)
                self.act(cub[:, j * 512:j * 512 + ncol], pb[:, 0:ncol], AF.Gelu_apprx_tanh, [pb], [cub])
            w3 = lambda b: b.ap.rearrange("p (j t) -> p j t", t=512)[:, :, 0:ncol]
            self.tt("dve", w3(ugb), w3(cub), w3(cgb), ALU.mult, [cub, cgb], [ugb])
            stv = stats.ap.rearrange("p (i s) -> p i s", s=6)
            mvv = mv.ap.rearrange("p (i s) -> p i s", s=2)
            for i in range(nt):
                hs = [hv[:, k, i * 128:(i + 1) * 128] for k in range(8)]
                pb = self.ps()
                self.mm(pb[:, 0:256], [(hs[k], winv[:, k, 2944:3200]) for k in range(8)], [win, h], [pb])
                cvb = cvt[i]
                self.act(cvb.ap, pb[:, 0:256], AF.Gelu_apprx_tanh, [pb], [cvb])
                S.op("dve", lambda e, i=i, cvb=cvb: e.bn_stats(out=stv[:, i, :], in_=cvb.ap), [cvb], [stats])
                S.op("dve", lambda e, i=i: e.bn_aggr(out=mvv[:, i, :], in_=stv[:, i, :]), [stats], [mv])
            for i in range(nt):
                hs = [hv[:, k, i * 128:(i + 1) * 128] for k in range(8)]
                pb = self.ps()
                self.mm(pb[:, 0:384], [(hs[k], winv[:, k, 1920:2304]) for k in range(8)], [win, h], [pb])
                vb = vtt[cnt["v"] % 2]
                cnt["v"] += 1
                self.copy("dve", vb.ap, pb[:, 0:384], [pb], [vb])
                self.dma(self.VT[c0 + i * 128:c0 + (i + 1) * 128, :], vb.ap, reads=[vb], writes=[self.tkVT])
            self.act(mvv[:, 0:nt, 1], mvv[:, 0:nt, 1], AF.Sqrt, [mv, self.epsb], [mv], bias=self.epsb.ap)
            S.op("dve", lambda e: e.reciprocal(out=mvv[:, 0:nt, 1], in_=mvv[:, 0:nt, 1]), [mv], [mv])
            for j in range(3):
                pb = proj(j * 128, 128)
                o = f32tile()
                self.act(o[:, 0:ncol], pb[:, 0:ncol], AF.Copy, [pb], [o])
                self.dma(self.AX[j * 128:(j + 1) * 128, c0:c0 + ncol], o[:, 0:ncol], reads=[o], writes=[self.tkAX], eng="act")
            cwv = self.cws.ap.rearrange("p (h s) -> p h s", s=128)
            for i in range(nt):
                self.ts("dve", vhat[i].ap, cvt[i].ap, mvv[:, i, 0:1], mvv[:, i, 1:2], ALU.subtract, ALU.mult, [cvt[i], mv], [vhat[i]])
            pcs = []
            for i in range(nt):
                vh = vhat[i]
                pc_ = self.ps()
                pcv = pc_.ap.rearrange("p (j s) -> p j s", s=256)

                def cmix(e, pcv=pcv, vh=vh):
                    ins = None
                    for j in range(2):
                        for hl in range(2):
                            hh = 2 * j + hl
                            ins = e.matmul(pcv[hl * 64:(hl + 1) * 64, j, 0:128], lhsT=vh[:, hh * 64:(hh + 1) * 64], rhs=cwv[:, hh, :],
                                           start=True, stop=True)
                    return ins
                S.op("pe", cmix, [vh, self.cws], [pc_])
                pcs.append((pc_, pcv))
            for i in range(nt):
                pc_, pcv = pcs[i]
                for j in range(2):
                    tb = t1[j]
                    self.stt("dve", tb.ap, pcv[:, j, 0:128], prm[:, P_CNW + j:P_CNW + j + 1], self.cbias[:, j * 128:(j + 1) * 128],
                             ALU.mult, ALU.add, [pc_, prm, self.cbias], [tb])
                    self.tt("dve", ycbb[:, j * 512 + i * 128:j * 512 + (i + 1) * 128], tb.ap,
                            ugb[:, j * 512 + i * 128:j * 512 + (i + 1) * 128], ALU.mult, [tb, ugb], [ycbb])
            for j in range(2):
                self.dma(self.Y[768 + j * 128:768 + (j + 1) * 128, c0:c0 + ncol], ycbb[:, j * 512:j * 512 + ncol], reads=[ycbb], writes=[self.tkY])

        loads(0)
        part0(0)
        part1(0)
        loads(1)
        part2(0)
        part0(1)
        for bi in range(NBLK):
            part1(bi + 1)
            loads(bi + 2)
            inproj(bi, lambda bi=bi: part2(bi + 1))
            part0(bi + 2)
        self.release(m_p1)
        self.top_hi = self.NW

    def pos_half_tile(self, dst, tglob, rcol_ap=None):
        arg, ni, nf = self._pt
        if rcol_ap is None:
            rcol_ap = self.cst[:, C_RCOL + tglob:C_RCOL + tglob + 1]
        self.stt("dve", arg.ap, self.omega.ap, rcol_ap, self.cst[:, C_PH:C_PH + 512],
                 ALU.mult, ALU.add, [self.omega, self.cst], [arg])
        self.ts("dve", ni.ap, arg.ap, 1.0 / (2 * PI), None, ALU.mult, None, [arg], [ni])
        self.copy("dve", nf.ap, ni.ap, [ni], [nf])
        self.stt("dve", arg.ap, nf.ap, -2 * PI, arg.ap, ALU.mult, ALU.add, [nf, arg], [arg])
        self.act(dst.ap, arg.ap, AF.Sin, [arg], [dst])

    def phase_a(self, l):
        al = self.alloc
        S = self.S
        m_a = self.mark()
        prm = self.prm[l]
        NP = 2 + TC + 1 + 2 + TL + 1
        xcb = al(3 * T, BF16)
        xcbv = xcb.ap.rearrange("p (j t) -> p j t", t=T)
        XC = self.dram_scratch("XC%d" % l, [384, T], F32)
        tkXC = Buf(None)
        tkXC.tk.multi = True
        m0 = self.mark()
        axp = [al(NP), al(NP)]
        xcj0 = [al(T), al(T)]
        for b in axp:
            self.memset("pool", b[:, 0:2], 0.0, [b])
            self.memset("pool", b[:, 2 + TC:2 + TC + 3], 0.0, [b])
            self.memset("pool", b[:, NP - 1:NP], 0.0, [b])
        for j in range(3):
            ax, xj = axp[j % 2], xcj0[j % 2]
            self.dma(ax[:, 2:2 + TC], self.AX[j * 128:(j + 1) * 128, 0:TC], reads=[self.tkAX], writes=[ax])
            self.dma(ax[:, 5 + TC:5 + TC + TL], self.AX[j * 128:(j + 1) * 128, TC:T], reads=[self.tkAX], writes=[ax])
            eng = "dve"
            cw = [prm[:, P_CONVW + 4 * j + k:P_CONVW + 4 * j + k + 1] for k in range(4)]
            for (o0, n, b0) in ((0, TC, 0), (TC, TL, 3 + TC)):
                self.ts(eng, xj[:, o0:o0 + n], ax[:, b0:b0 + n], cw[0], prm[:, P_CONVB + j:P_CONVB + j + 1], ALU.mult, ALU.add,
                        [ax, prm], [xj])
                for k in range(1, 4):
                    self.stt(eng, xj[:, o0:o0 + n], ax[:, b0 + k:b0 + k + n], cw[k], xj[:, o0:o0 + n], ALU.mult, ALU.add,
                             [ax, prm, xj], [xj])
            self.copy("act", xcbv[:, j, :], xj.ap, [xj], [xcb])
            self.dma(XC[j * 128:(j + 1) * 128, :], xj.ap, reads=[xj], writes=[tkXC])
        self.release(m0)
        if self.stop_after == ("a0", l):
            self.halt = True
            return
        xcj = al(T)
        a_row = al(T)
        nw_row = al(T)
        ix_row = al(T)
        hf_j = al(T)
        sga_row = al(T, BF16)
        y_row = a_row.ap.bitcast(BF16)[:, 0:T]
        NR = 5
        rt = [al(512) for _ in range(NR)]
        it = [al(512) for _ in range(NR)]
        tt_ = [al(512) for _ in range(NR)]
        dt_ = [al(512) for _ in range(NR)]
        kch = {0: (0, 1), 1: (0, 1, 2), 2: (1, 2)}
        itn = 0
        pend_a = []
        for j in range(3):
            self.dma(xcj.ap, XC[j * 128:(j + 1) * 128, :], reads=[tkXC], writes=[xcj])
            self.dma(sga_row.ap, self.SGA[j * 128:(j + 1) * 128, :], reads=[self.tkSGA], writes=[sga_row])
            for d in range(2):
                for (c0, nt) in BLOCKS:
                    n = nt * 128
                    q = itn % NR
                    itn += 1
                    r_, i_, t_, d_ = rt[q], it[q], tt_[q], dt_[q]
                    pr, pi = self.ps(), self.ps()
                    self.mm(pr[:, 0:n], [(self.agwv[:, 2 * d, k, j * 128:(j + 1) * 128], xcbv[:, k, c0:c0 + n]) for k in kch[j]], [self.agw, xcb], [pr])
                    self.mm(pi[:, 0:n], [(self.agwv[:, 2 * d + 1, k, j * 128:(j + 1) * 128], xcbv[:, k, c0:c0 + n]) for k in kch[j]], [self.agw, xcb], [pi])
                    cidx = 3 * d + j
                    ah = self.ahalf
                    self.act(r_[:, 0:n], pr[:, 0:n], AF.Tanh, [pr, ah], [r_], scale=0.5, bias=ah[:, 6 + cidx:7 + cidx])
                    self.act(i_[:, 0:n], pi[:, 0:n], AF.Tanh, [pi, ah], [i_], scale=0.5, bias=ah[:, 12 + cidx:13 + cidx])
                    self.act(t_[:, 0:n], r_[:, 0:n], AF.Tanh, [r_, ah], [t_], scale=ah[:, cidx:cidx + 1], bias=ah[:, cidx:cidx + 1])
                    self.act(d_[:, 0:n], r_[:, 0:n], AF.Exp, [r_, self.halfc], [d_], scale=self.halfc[:, cidx:cidx + 1],
                             bias=self.halfc[:, cidx:cidx + 1])
                    if len(pend_a) >= 2:
                        pd, pc0, pn = pend_a.pop(0)
                        self.act(a_row[:, pc0:pc0 + pn], pd[:, 0:pn], AF.Identity, [pd], [a_row], bias=1.0)
                    self.stt("dve", d_[:, 0:n], d_[:, 0:n], 1.0, t_[:, 0:n], ALU.add, ALU.mult, [d_, t_], [d_])
                    pend_a.append((d_, c0, n))
                    self.stt("dve", ix_row[:, c0:c0 + n], i_[:, 0:n], 1.0, xcj[:, c0:c0 + n], ALU.add, ALU.mult, [i_, xcj], [ix_row])
                while pend_a:
                    pd, pc0, pn = pend_a.pop(0)
                    self.act(a_row[:, pc0:pc0 + pn], pd[:, 0:pn], AF.Identity, [pd], [a_row], bias=1.0)
                if self.stop_after == ("a1", l):
                    self.halt = True
                    return
                self.act(nw_row.ap, a_row.ap, AF.Square, [a_row], [nw_row])
                self.act(nw_row.ap, nw_row.ap, AF.Sqrt, [nw_row], [nw_row], scale=-0.25, bias=0.25)
                self.tt("dve", ix_row.ap, ix_row.ap, nw_row.ap, ALU.mult, [ix_row, nw_row], [ix_row])
                if self.stop_after == ("a2", l):
                    self.halt = True
                    return
                if d == 0:
                    S.op("dve", lambda e: e.tensor_tensor_scan(out=hf_j.ap, data0=a_row.ap, data1=ix_row.ap, initial=0.0,
                                                               op0=ALU.mult, op1=ALU.add), [a_row, ix_row], [hf_j])
                    if self.stop_after == ("a3", l):
                        self.halt = True
                        return
                else:
                    S.op("dve", lambda e: e.tensor_tensor_scan(out=rev(nw_row[:, 0:TC]), data0=rev(a_row[:, 0:TC]), data1=rev(ix_row[:, 0:TC]),
                                                               initial=0.0, op0=ALU.mult, op1=ALU.add), [a_row, ix_row], [nw_row])
                    S.op("dve", lambda e: e.tensor_tensor_scan(out=rev(nw_row[:, TC:T]), data0=rev(a_row[:, TC:T]), data1=rev(ix_row[:, TC:T]),
                                                               initial=nw_row[:, 0:1], op0=ALU.mult, op1=ALU.add), [a_row, ix_row, nw_row], [nw_row])
                    if self.stop_after == ("a4", l):
                        self.halt = True
                        return
                    self.tt("dve", nw_row.ap, nw_row.ap, hf_j.ap, ALU.add, [nw_row, hf_j], [nw_row])
                    self.tt("dve", y_row, nw_row.ap, sga_row.ap, ALU.mult, [nw_row, sga_row, a_row], [a_row])
                    if self.stop_after == ("a5", l):
                        self.halt = True
                        return
                    self.dma(self.Y[j * 128:(j + 1) * 128, :], y_row, reads=[a_row], writes=[self.tkY])
                    if self.stop_after == ("a6", l):
                        self.halt = True
                        return
                if self.stop_after == ("aj%dd%d" % (j, d), l):
                    self.halt = True
                    return
        self.release(m_a)

    def phase_b(self, l):
        al = self.alloc
        S = self.S
        m_b = self.mark()
        prm = self.prm[l]
        lb, oml, noml = self.lb[l], self.oml[l], self.noml[l]
        P = 96
        W = 2048
        Sst = al(384, parts=P)
        Sbf2 = [al(384, BF16, parts=P) for _ in range(3)]
        st = {"cur": 0}
        qb = [al(W, BF16, parts=P) for _ in range(2)]
        sgb = [al(W, parts=P) for _ in range(2)]
        vtb = [al(4 * 384, BF16) for _ in range(2)]
        ofb = [al(W, parts=P) for _ in range(2)]
        gbb = [al(W, BF16, parts=P) for _ in range(2)]
        lf = al(W, parts=P)
        G = al(W, parts=P)
        eG = al(W, BF16, parts=P)
        enG = al(W, BF16, parts=P)
        kk = al(W, BF16, parts=P)
        kgs = [al(W, BF16, parts=P) for _ in range(2)]
        qgs = [al(W, BF16, parts=P) for _ in range(2)]
        decs = [al(32, parts=P) for _ in range(2)]
        kets = [al(4 * 384, BF16) for _ in range(2)]
        scm = [al(512, BF16) for _ in range(2)]
        ob = [al(W, parts=P) for _ in range(2)]
        o2 = al(512, parts=P)
        rs = al(512, parts=P)
        yb = [al(W, BF16, parts=P) for _ in range(2)]
        tmpS = al(384, parts=P)
        xtk = {id(b): [Buf(None) for _ in range(3)] for b in qb + sgb + gbb}
        Qv = self.Q.rearrange("(h d) t -> d h t", d=96)
        GBv = self.GB.rearrange("(h d) t -> d h t", d=96)
        OFv = self.OF.rearrange("(h d) t -> d h t", d=96)
        Yv = self.Y[384:768, :].rearrange("(h d) t -> d h t", d=96)
        v3 = lambda b: b.ap.rearrange("p (h t) -> p h t", t=512)
        S3 = Sst.ap.rearrange("p (h v) -> p h v", v=96)
        T3 = tmpS.ap.rearrange("p (h v) -> p h v", v=96)
        for d in range(2):
            SG = self.SGF if d == 0 else self.SGB
            tkSG = self.tkSGF if d == 0 else self.tkSGB
            SGv = SG.rearrange("(h d) t -> d h t", d=96)
            order = list(range(len(BLOCKS))) if d == 0 else [0] + list(range(len(BLOCKS) - 1, 0, -1))
            N = len(order)
            mask = self.maskf if d == 0 else self.maskb
            self.memset("pool", Sst.ap, 0.0, [Sst])
            self.memset("pool", Sbf2[0].ap, 0.0, [Sbf2[0]])
            self.memset("pool", Sbf2[1].ap, 0.0, [Sbf2[1]])
            self.memset("pool", Sbf2[2].ap, 0.0, [Sbf2[2]])

            def geom(oi):
                c0, nt = BLOCKS[order[oi]]
                return c0, nt, nt * 128

            def load_heads(buf, src, c0, n, tk):
                comp = src.rearrange("(s p) t -> p s t", p=128)
                self.dma(v3(buf)[:, 0:3, 0:n], comp[0:96, :, c0:c0 + n], reads=[tk], writes=[buf])
                for s3 in range(3):
                    self.dma(v3(buf)[32 * s3:32 * (s3 + 1), 3, 0:n], comp[96:128, s3, c0:c0 + n], reads=[tk], writes=[xtk[id(buf)][s3]])

            def RD(buf):
                return [buf] + xtk[id(buf)]

            def loadsA(oi):
                if oi >= N:
                    return
                c0, nt, n = geom(oi)
                q_, s_ = qb[oi % 2], sgb[oi % 2]
                load_heads(q_, self.Q, c0, n, self.tkQ)
                load_heads(s_, SG, c0, n, tkSG)

            def loadsB(oi):
                if oi >= N:
                    return
                c0, nt, n = geom(oi)
                v_ = vtb[oi % 2]
                self.dma(v_.ap.rearrange("p (n c) -> p n c", c=384)[:, 0:nt, :],
                         self.VT[c0:c0 + n, :].rearrange("(n p) c -> p n c", p=128), reads=[self.tkVT], writes=[v_])
                if d == 1:
                    self.dma(v3(ofb[oi % 2])[:, :, 0:n], OFv[:, :, c0:c0 + n], reads=[self.tkOF], writes=[ofb[oi % 2]])
                    load_heads(gbb[oi % 2], self.GB, c0, n, self.tkGB)

            def pw_stages(oi):
                if oi >= N:
                    return []
                c0, nt, n = geom(oi)
                nch = nt * 2
                q_, s_ = qb[oi % 2], sgb[oi % 2]
                kg, qg, dec = kgs[oi % 2], qgs[oi % 2], decs[oi % 2]
                full = (n == 512)
                sl = (lambda b: b.ap) if full else (lambda b: v3(b)[:, :, 0:n])
                dv = dec.ap.rearrange("p (h c) -> p h c", c=8)

                def stA():
                    for hh in range(4):
                        self.act(v3(lf)[:, hh, 0:n], v3(s_)[:, hh, 0:n], AF.Ln, RD(s_) + [oml, lb], [lf],
                                 scale=oml[0:P, hh:hh + 1], bias=lb[0:P, hh:hh + 1])
                        self.act(v3(kk)[:, hh, 0:n], v3(s_)[:, hh, 0:n], AF.Identity, RD(s_) + [noml, oml], [kk],
                                 scale=noml[0:P, hh:hh + 1], bias=oml[0:P, hh:hh + 1])

                def stB():
                    if full:
                        views = [(lf.ap, G.ap, W)]
                    else:
                        views = [(v3(lf)[:, hh, 0:n], v3(G)[:, hh, 0:n], n) for hh in range(4)]
                    for (src, dst, mlen) in views:
                        if d == 0:
                            S.op("dve", lambda e, src=src, dst=dst, mlen=mlen: e.tensor_tensor_scan(
                                out=dst, data0=self.smask[:, 0:mlen], data1=src, initial=0.0, op0=ALU.mult, op1=ALU.add), [lf, self.cst], [G])
                        else:
                            S.op("dve", lambda e, src=src, dst=dst, mlen=mlen: e.tensor_tensor_scan(
                                out=rev(dst), data0=rev(self.smask[:, 1:mlen + 1]), data1=rev(src), initial=0.0, op0=ALU.mult, op1=ALU.add),
                                [lf, self.cst], [G])

                def stC():
                    self.act(sl(eG), sl(G), AF.Exp, [G], [eG])
                    self.act(sl(enG), sl(G), AF.Exp, [G], [enG], scale=-1.0)
                    g4 = G.ap.rearrange("p (h c s) -> p h c s", h=4, s=64)
                    pos = 63 if d == 0 else 0
                    self.act(dv[:, :, 0:nch], g4[:, :, 0:nch, pos], AF.Exp, [G], [dec])

                def stD():
                    self.tt("dve", sl(kg), sl(kk), sl(enG), ALU.mult, [kk, enG], [kg])
                    self.tt("dve", sl(qg), sl(q_), sl(eG), ALU.mult, RD(q_) + [eG], [qg])
                return [stA, stB, stC, stD]

            def transposes(oi):
                if oi >= N:
                    return
                c0, nt, n = geom(oi)
                ke, ket = kgs[oi % 2], kets[oi % 2]
                kev = ket.ap.rearrange("p (n c) -> p n c", c=384)
                for i in range(nt):
                    pb = self.ps()
                    pbv = pb.ap.bitcast(BF16)
                    S.op("pe", lambda e, pbv=pbv, i=i, ke=ke: [e.transpose(out=pbv[:, hh * 96:(hh + 1) * 96], in_=v3(ke)[:, hh, i * 128:(i + 1) * 128],
                                                                              identity=self.identb[0:96, 0:96]) for hh in range(4)][-1],
                         [ke, self.identb], [pb])
                    self.copy("act", kev[:, i, :], pbv[:, 0:384], [pb], [ket])

            def core_tile(oi, i):
                c0, nt, n = geom(oi)
                kg, qg, dec, ket, v_ = kgs[oi % 2], qgs[oi % 2], decs[oi % 2], kets[oi % 2], vtb[oi % 2]
                o_ = ob[oi % 2]
                vv = v_.ap.rearrange("p (n c) -> p n c", c=384)
                kev = ket.ap.rearrange("p (n c) -> p n c", c=384)
                dv = dec.ap.rearrange("p (h c) -> p h c", c=8)
                psc = self.ps()
                pscv = psc.ap.rearrange("p (h c) -> p h c", c=128)
                S.op("pe", lambda e: [e.matmul(pscv[:, hh, :], lhsT=v3(kg)[:, hh, i * 128:(i + 1) * 128],
                                               rhs=v3(qg)[:, hh, i * 128:(i + 1) * 128], start=True, stop=True)
                                      for hh in range(4)][-1], [kg, qg], [psc])
                sc = scm[i % 2]
                scv = sc.ap.rearrange("p (h c) -> p h c", c=128)
                self.tt("dve", scv, pscv, bcast_mid(mask.ap, 4), ALU.mult, [psc, mask], [sc])
                po = self.ps()
                pov = po.ap.rearrange("p (h c) -> p h c", c=128)
                chunks = (0, 1) if d == 0 else (1, 0)
                pks = []
                for c in chunks:
                    pk = self.ps()
                    pkv = pk.ap.rearrange("p (h v) -> p h v", v=128)
                    S.op("pe", lambda e, pkv=pkv, c=c: [e.matmul(
                        pkv[0:P, hh, 0:96], lhsT=kev[c * 64:(c + 1) * 64, i, hh * 96:(hh + 1) * 96],
                        rhs=vv[c * 64:(c + 1) * 64, i, hh * 96:(hh + 1) * 96], start=True, stop=True) for hh in range(4)][-1],
                        [ket, v_], [pk])
                    pks.append((pk, pkv))

                def update(ci):
                    c = chunks[ci]
                    pk, pkv = pks[ci]
                    dcol = dv[:, :, i * 2 + c]
                    dcb = AP(dcol.tensor, dcol.offset, [list(dcol.ap[0]), list(dcol.ap[1]), [0, 96]])
                    self.tt("dve", T3, S3, pkv[0:P, :, 0:96], ALU.add, [Sst, pk], [tmpS])
                    self.tt("dve", S3, T3, dcb, ALU.mult, [tmpS, dec], [Sst])
                    st["cur"] = (st["cur"] + 1) % 3
                    self.copy("dve", Sbf2[st["cur"]].ap, Sst.ap, [Sst], [Sbf2[st["cur"]]])
                s_in0 = Sbf2[st["cur"]]
                update(0)
                s_in1 = Sbf2[st["cur"]]
                sins = {chunks[0]: s_in0, chunks[1]: s_in1}

                def ogroup(e):
                    ins = None
                    for hh in range(4):
                        e.matmul(pov[0:P, hh, :], lhsT=vv[:, i, hh * 96:(hh + 1) * 96], rhs=scv[:, hh, :], start=True, stop=False)
                        for ci2, c in enumerate(chunks):
                            col = i * 128 + c * 64
                            ins = e.matmul(pov[0:P, hh, c * 64:(c + 1) * 64], lhsT=sins[c][:, hh * 96:(hh + 1) * 96],
                                           rhs=v3(qg)[:, hh, col:col + 64], start=False, stop=(ci2 == 1))
                    return ins
                S.op("pe", ogroup, [v_, sc, s_in0, s_in1, qg], [po])
                update(1)
                def fin():
                    if d == 0:
                        self.copy("act", v3(o_)[:, :, i * 128:(i + 1) * 128], pov[0:P, :, :], [po], [o_])
                    else:
                        self.tt("dve", v3(o_)[:, :, i * 128:(i + 1) * 128], pov[0:P, :, :], v3(ofb[oi % 2])[:, :, i * 128:(i + 1) * 128], ALU.add,
                                [po, ofb[oi % 2]], [o_])
                return fin

            def finish(oi):
                c0, nt, n = geom(oi)
                o_ = ob[oi % 2]
                if d == 0:
                    self.dma(OFv[:, :, c0:c0 + n], v3(o_)[:, :, 0:n], reads=[o_], writes=[self.tkOF])
                    return
                y_ = yb[oi % 2]
                g_ = gbb[oi % 2]
                for hh in range(4):
                    self.act(o2[:, 0:n], v3(o_)[:, hh, 0:n], AF.Square, [o_], [o2])
                    pn = self.ps()
                    self.mm(pn[0:P, 0:n], [(self.ones[0:P, 0:P], o2[:, 0:n])], [self.cst, o2], [pn])
                    self.act(rs[:, 0:n], pn[0:P, 0:n], AF.Ln, [pn, self.epsb], [rs], scale=1.0 / 96.0, bias=self.epsb[0:P, :])
                    self.act(rs[:, 0:n], rs[:, 0:n], AF.Exp, [rs], [rs], scale=-0.5)
                    self.tt("dve", rs[:, 0:n], rs[:, 0:n], v3(o_)[:, hh, 0:n], ALU.mult, [rs, o_], [rs])
                    self.stt("dve", v3(y_)[:, hh, 0:n], rs[:, 0:n], prm[0:P, P_BNW + hh:P_BNW + hh + 1], v3(g_)[:, hh, 0:n], ALU.mult, ALU.mult,
                             [rs, prm] + RD(g_), [y_])
                self.dma(Yv[:, :, c0:c0 + n], v3(y_)[:, :, 0:n], reads=[y_], writes=[self.tkY])

            loadsA(0)
            loadsB(0)
            for f in pw_stages(0):
                f()
            transposes(0)
            loadsA(1)
            for oi in range(N):
                loadsA(oi + 2)
                loadsB(oi + 1)
                stages = pw_stages(oi + 1)
                c0, nt, n = geom(oi)
                tiles = list(range(nt)) if d == 0 else list(range(nt - 1, -1, -1))
                per = (len(stages) + nt - 1) // nt if stages else 0
                pend = None
                for ti, i in enumerate(tiles):
                    fin = core_tile(oi, i)
                    if pend is not None:
                        pend()
                    pend = fin
                    for f in stages[ti * per:(ti + 1) * per]:
                        f()
                pend()
                transposes(oi + 1)
                finish(oi)
        self.release(m_b)

    def phase_o(self, l):
        al = self.alloc
        S = self.S
        m_o = self.mark()
        last = (l == 1)
        self.wout = al(9 * D, BF16)
        self.woutv = self.wout.ap.rearrange("p (c n) -> p c n", n=D)
        if not last:
            self.woutc = al(9 * D, BF16)
            self.woutcv = self.woutc.ap.rearrange("p (c n) -> p c n", n=D)
        else:
            self.fnw = al(D)
            fn = self.fnw_d
            self.dma(self.fnw.ap, AP(fn.tensor, fn.offset, [[0, 128], [1, D]]), writes=[self.fnw])
        rows = [(j * 128, 128) for j in range(3)] + [(384 + h * 96, 96) for h in range(4)] + [(768 + j * 128, 128) for j in range(2)]
        self.wout_rows = rows
        m = self.mark()
        st = [al(D), al(D)]
        for i, (r0, n) in enumerate(rows):
            s = st[i % 2]
            self.dma(s[0:n, :], self.wout_d[l, r0:r0 + n, :], writes=[s])
            self.tt("dve", self.woutv[0:n, i, :], s[0:n, :], self.gb[l][0][0:n, :], ALU.mult, [s, self.gb[l][0]], [self.wout])
            if not last:
                self.tt("dve", self.woutcv[0:n, i, :], s[0:n, :], self.gb[l][1][0:n, :], ALU.mult, [s, self.gb[l][1]], [self.woutc])
        self.release(m)
        yA = [al(3 * 512, BF16) for _ in range(2)]
        yB = [al(4 * 512, BF16, parts=96) for _ in range(2)]
        yC = [al(2 * 512, BF16) for _ in range(2)]
        xt = [al(D) for _ in range(8)]
        xo = [al(D) for _ in range(2 if not last else 6)]
        wsteps = []
        if not last:
            self.alloc_win()
            wst = [al(NCOLS // 4) for _ in range(2)]
            wsteps = self.win_steps(1, wst, ["act"], nsplit=4)
        sq_junk = al(D, BF16)
        ssb = [al(1) for _ in range(4)]
        blocks = BLOCKS[1:] if last else BLOCKS
        cnt = {"x": 0}

        def src_rows(t0):
            if l == 0:
                return self.ctx_d[t0:t0 + 128, :] if t0 < TC else self.X0[t0 - TC:t0 - TC + 128, :]
            return self.X1[t0:t0 + 128, :]

        def load(oi):
            c0, nt = blocks[oi]
            n = nt * 128
            a_, b_, c_ = yA[oi % 2], yB[oi % 2], yC[oi % 2]
            self.dma(a_.ap.rearrange("p (j t) -> p j t", t=512)[:, :, 0:n], self.Y[0:384, :].rearrange("(j p) t -> p j t", p=128)[:, :, c0:c0 + n],
                     reads=[self.tkY], writes=[a_])
            self.dma(b_.ap.rearrange("p (j t) -> p j t", t=512)[:, :, 0:n], self.Y[384:768, :].rearrange("(j p) t -> p j t", p=96)[:, :, c0:c0 + n],
                     reads=[self.tkY], writes=[b_])
            self.dma(c_.ap.rearrange("p (j t) -> p j t", t=512)[:, :, 0:n], self.Y[768:1024, :].rearrange("(j p) t -> p j t", p=128)[:, :, c0:c0 + n],
                     reads=[self.tkY], writes=[c_])
            xs = []
            for i in range(nt):
                b = xt[cnt["x"] % 8]
                cnt["x"] += 1
                rd = [self.tkX1] if l > 0 else [self.tkX0]
                self.dma(b.ap, src_rows(c0 + i * 128), reads=rd, writes=[b])
                xs.append(b)
            return xs
        nxt = load(0)
        oc = 0
        for oi, (c0, nt) in enumerate(blocks):
            xs = nxt
            if oi + 1 < len(blocks):
                nxt = load(oi + 1)
            s_ctx = 1 if c0 < TC else 0
            a_, b_, c_ = yA[oi % 2], yB[oi % 2], yC[oi % 2]
            av = a_.ap.rearrange("p (j t) -> p j t", t=512)
            bv = b_.ap.rearrange("p (j t) -> p j t", t=512)
            cv = c_.ap.rearrange("p (j t) -> p j t", t=512)
            for i in range(nt):
                xb = xs[i]
                lhs = [(av[:, j, i * 128:(i + 1) * 128], 128) for j in range(3)] + [(bv[:, j, i * 128:(i + 1) * 128], 96) for j in range(4)] + \
                      [(cv[:, j, i * 128:(i + 1) * 128], 128) for j in range(2)]
                o_ = xo[oc % len(xo)]
                oc += 1
                if wsteps:
                    wsteps.pop(0)()
                wv, wb = (self.woutcv, self.woutc) if s_ctx else (self.woutv, self.wout)
                for nb in range(2):
                    pb = self.ps()
                    self.mm(pb.ap, [(lh, wv[0:kn, ci, nb * 512:(nb + 1) * 512]) for ci, (lh, kn) in enumerate(lhs)], [a_, b_, c_, wb], [pb])
                    self.tt("dve", o_[:, nb * 512:(nb + 1) * 512], pb.ap, xb[:, nb * 512:(nb + 1) * 512], ALU.add, [pb, xb], [o_])
                t0 = c0 + i * 128
                if not last:
                    self.dma(self.X1[t0:t0 + 128, :], o_.ap, reads=[o_], writes=[self.tkX1])
                else:
                    sb = ssb[oc % 4]
                    self.memset("pool", sb.ap, 0.0, [sb])
                    self.act(sq_junk.ap, o_.ap, AF.Square, [o_], [sq_junk, sb], accum_out=sb.ap)
                    self.act(sb.ap, sb.ap, AF.Sqrt, [sb, self.epsb], [sb], scale=1.0 / D, bias=self.epsb.ap)
                    S.op("dve", lambda e, sb=sb: e.reciprocal(out=sb.ap, in_=sb.ap), [sb], [sb])
                    self.stt("dve", o_.ap, o_.ap, sb.ap, self.fnw.ap, ALU.mult, ALU.mult, [o_, sb, self.fnw], [o_])
                    self.dma(self.out_d[t0 - TC:t0 - TC + 128, :], o_.ap, reads=[o_])
        while wsteps:
            wsteps.pop(0)()
        self.release(m_o)
        self.release(self.m_layer)


def _consts():
    c = np.zeros((128, NCST), np.float32)
    c[:, C_ID:C_ID + 128] = np.eye(128)
    s = np.arange(128)[:, None]
    t = np.arange(128)[None, :]
    same = (s // 64) == (t // 64)
    c[:, C_MF:C_MF + 128] = (same & (s <= t))
    c[:, C_MB:C_MB + 128] = (same & (s >= t))
    c[:, C_ONE:C_ONE + 128] = 1.0
    om = (1.0 / (10000.0 ** (np.arange(256, dtype=np.float64) / 256.0))).astype(np.float32)
    c[:, C_JROW:C_JROW + 256] = om
    c[:, C_JROW + 256:C_JROW + 512] = om
    c[:, C_PH:C_PH + 256] = 0.0
    c[:, C_PH + 256:C_PH + 512] = np.pi / 2
    p = np.arange(128)
    for tile in range(32):
        c[:, C_RCOL + tile] = 2 * tile + p // 64
    c[:, C_CCOL] = p % 64
    m = np.ones(2049, np.float32)
    m[::64] = 0.0
    c[:, C_SMASK:C_SMASK + 2049] = m
    return c


def _pack_params(inp):
    prm = np.zeros((2, 128, NPRM), np.float32)
    for l in range(2):
        prm[l, :, P_NORMW:P_NORMW + 8] = inp["norm_w"][l].reshape(8, 128).T
        cw = inp["a_conv_w"][l]
        prm[l, :, P_CONVW:P_CONVW + 12] = cw.reshape(4, 3, 128).transpose(2, 1, 0).reshape(128, 12)
        prm[l, :, P_CONVB:P_CONVB + 3] = inp["a_conv_b"][l].reshape(3, 128).T
        for name, col in (("a_br", P_BR), ("a_bi", P_BI), ("a_lambda", P_LAM)):
            v = inp[name][l]
            prm[l, :, col:col + 6] = v.reshape(2, 3, 128).transpose(2, 0, 1).reshape(128, 6)
        prm[l, :, P_CNW:P_CNW + 2] = inp["c_norm_w"][l].reshape(2, 128).T
        prm[l, 0:96, P_BNW:P_BNW + 4] = inp["b_norm_w"][l].reshape(4, 96).T
        lg = inp["b_lb_logits"]
        prm[l, 0:96, P_LBL:P_LBL + 12] = lg.reshape(3, 4, 96).transpose(2, 0, 1).reshape(96, 12)
    agw = np.zeros((2, 4, 384, 384), np.float32)
    for l in range(2):
        for d in range(2):
            for gi, name in enumerate(("a_wr", "a_wi")):
                w = inp[name][l, d]
                for h in range(8):
                    agw[l, 2 * d + gi, h * 48:(h + 1) * 48, h * 48:(h + 1) * 48] = w[h]
    cwsT = np.ascontiguousarray(inp["c_ws"].transpose(0, 3, 1, 2)).reshape(2, 128, 512)
    cb = inp["c_bs"]
    cbias = np.zeros((2, 128, 2, 128), np.float32)
    for j in range(2):
        for hl in range(2):
            cbias[:, hl * 64:(hl + 1) * 64, j, :] = cb[:, 2 * j + hl, None, :]
    return prm, agw, cwsT, cbias.reshape(2, 128, 256)


def _win_col_perm():
    perm = np.arange(NCOLS)
    for base in (768, 1152, 1536, 2304):
        for s3 in range(3):
            perm[base + s3 * 128:base + s3 * 128 + 96] = base + s3 * 96 + np.arange(96)
            perm[base + s3 * 128 + 96:base + (s3 + 1) * 128] = base + 288 + 32 * s3 + np.arange(32)
    return perm


def make_in_maps(inp, cores):
    inp = {k: np.ascontiguousarray(np.asarray(v, dtype=np.float32)) for k, v in inp.items()}
    inp["w_in"] = np.ascontiguousarray(inp["w_in"][:, :, _win_col_perm()])
    prm, agw, cwsT, cbias = _pack_params(inp)
    cst = _consts()
    maps = []
    for b in cores:
        cv = np.concatenate([inp["c"][b].reshape(8, 128).T, inp["c_ctx"].reshape(8, 128).T], axis=1)
        maps.append({
            "x": inp["x"][b], "ctx": inp["ctx"][b], "cvec": np.ascontiguousarray(cv),
            "w_mod": inp["w_mod"], "b_mod": inp["b_mod"], "w_in": inp["w_in"], "w_out": inp["w_out"],
            "prm": prm, "agw": agw, "cwsT": cwsT, "cbias": cbias, "fnw": inp["final_norm_w"], "cst": cst,
        })
    return maps


def kernel(**inputs):
    bld = Builder()
    nc = bld.build()
    maps = make_in_maps(inputs, list(range(8)))
    res = run_bass_kernel_spmd(nc, maps, core_ids=list(range(8)))
    return np.stack([np.asarray(r["out"], dtype=np.float32) for r in res.results], axis=0)
```

```python
import numpy as np
from contextlib import ExitStack
import concourse.bass as bass
import concourse.mybir as mybir
from concourse.ap import AP
from concourse.bass_utils import run_bass_kernel_spmd

F32 = mybir.dt.float32
BF16 = mybir.dt.bfloat16
I32 = mybir.dt.int32
AF = mybir.ActivationFunctionType
ALU = mybir.AluOpType

D = 1024
TC = 256
TL = 4096
T = TC + TL
NCOLS = 3456
EPS = 1e-6
PI = float(np.pi)
BLOCKS = [(0, 2)] + [(256 + 512 * i, 4) for i in range(8)]

P_NORMW, P_CONVW, P_CONVB, P_BR, P_BI, P_LAM, P_CNW, P_BNW, P_LBL = 0, 8, 20, 23, 29, 35, 41, 43, 47
NPRM = 64
C_ID, C_MF, C_MB, C_ONE, C_JROW, C_PH, C_RCOL, C_CCOL, C_SMASK = 0, 128, 256, 384, 512, 1024, 1536, 1568, 1569
NCST = C_SMASK + 2049


class Tk:
    __slots__ = ("w", "r", "multi", "ws")

    def __init__(self):
        self.w = None
        self.r = []
        self.multi = False
        self.ws = []


class Sched:
    ENGS = ("pe", "act", "dve", "pool", "sp")

    def __init__(self, nc):
        self.nc = nc
        self.ops = {e: [] for e in self.ENGS}
        self.semcount = {}
        self.known = {e: {} for e in self.ENGS}
        self.epoch = 0
        self.rr = 0
        self.rrq = {}

    def new_epoch(self):
        self.epoch += 1

    def _bump(self, key, inc):
        v = self.semcount.get(key, 0) + inc
        self.semcount[key] = v
        return (key, v)

    def _collect(self, eng, reads, writes):
        waits = {}
        kn = self.known[eng]

        def need(ev):
            if ev is None:
                return
            k, v = ev
            if kn.get(k, 0) >= v:
                return
            if waits.get(k, 0) < v:
                waits[k] = v
        for t in reads:
            if t.multi:
                for ev in t.ws:
                    need(ev)
            else:
                need(t.w)
        for t in writes:
            if not t.multi:
                need(t.w)
            for ev in t.r:
                need(ev)
        for k, v in waits.items():
            kn[k] = v
        return waits

    def _commit(self, ev, reads, writes):
        for t in reads:
            t.r.append(ev)
            if len(t.r) > 16:
                best = {}
                for k, v in t.r:
                    if best.get(k, 0) < v:
                        best[k] = v
                t.r = list(best.items())
        for t in writes:
            if t.multi:
                t.ws.append(ev)
                if len(t.ws) > 16:
                    best = {}
                    for k, v in t.ws:
                        if best.get(k, 0) < v:
                            best[k] = v
                    t.ws = list(best.items())
            else:
                t.w = ev
                t.r = []

    def op(self, eng, fn, reads=(), writes=()):
        reads = [b.tk for b in reads]
        writes = [b.tk for b in writes]
        waits = self._collect(eng, reads, writes)
        ev = self._bump(("E", eng, self.epoch), 1)
        self.ops[eng].append((list(waits.items()), fn, ev, 1))
        self._commit(ev, reads, writes)

    def dma(self, out, in_, reads=(), writes=(), eng="sp", key=None):
        reads = [b.tk for b in reads]
        writes = [b.tk for b in writes]
        if key is None:
            n = self.rrq.get(eng, 0)
            self.rrq[eng] = n + 1
            key = ("D", eng, n % (32 if eng == "sp" else 8))
        else:
            key = ("D", key)
        waits = self._collect(eng, reads, writes)
        prev = self.semcount.get(key, 0)
        if prev and self.known[eng].get(key, 0) < prev:
            waits[key] = max(waits.get(key, 0), prev)
            self.known[eng][key] = prev
        ev = self._bump(key, 16)

        def fn(e, out=out, in_=in_):
            return e.dma_start(out=out, in_=in_)
        self.ops[eng].append((list(waits.items()), fn, ev, 16))
        self._commit(ev, reads, writes)

    def barrier(self):
        evs = list(self.semcount.items())
        for e in self.ENGS:
            waits = []
            for k, v in evs:
                if self.known[e].get(k, 0) < v:
                    waits.append((k, v))
                    self.known[e][k] = v
            if waits:
                self.ops[e].append((waits, None, None, 0))

    def emit(self):
        nc = self.nc
        keys = list(self.semcount.keys())
        with ExitStack() as es:
            sems = {}
            for i, k in enumerate(keys):
                sems[k] = es.enter_context(nc.semaphore("s%d" % i))
            block = es.enter_context(nc.Block())

            def run(engname):
                def body(e):
                    for waits, fn, ev, inc in self.ops[engname]:
                        for k, v in waits:
                            e.wait_ge(sems[k], v)
                        if fn is not None:
                            fn(e).then_inc(sems[ev[0]], inc)
                return body
            block.tensor(run("pe"))
            block.scalar(run("act"))
            block.vector(run("dve"))
            block.gpsimd(run("pool"))
            block.sync(run("sp"))


class Buf:
    __slots__ = ("ap", "tk")

    def __init__(self, ap):
        self.ap = ap
        self.tk = Tk()

    def __getitem__(self, key):
        return self.ap[key]


def rev(ap):
    a = [list(x) for x in ap.ap]
    off = ap.offset + a[-1][0] * (a[-1][1] - 1)
    a[-1][0] = -a[-1][0]
    return AP(ap.tensor, off, a)


def bcast_mid(ap2, n):
    a = [list(x) for x in ap2.ap]
    return AP(ap2.tensor, ap2.offset, [a[0], [0, n], a[1]])


def bcast_last(ap2, n):
    a = [list(x) for x in ap2.ap]
    return AP(ap2.tensor, ap2.offset, [a[0], a[1], [0, n]])


class Builder:
    def __init__(self, debug=None, stop_after=None):
        self.debug = debug or ()
        self.stop_after = stop_after
        self.nc = nc = bass.Bass("TRN2", target_bir_lowering=False)
        self.S = Sched(nc)
        self.es = ExitStack()
        self.dbg_out = {}
        self.halt = False

    def dram_in(self, name, shape, dt=F32):
        return self.nc.dram_tensor(name, list(shape), dt, kind="ExternalInput").ap()

    def dram_scratch(self, name, shape, dt):
        kind = "ExternalOutput" if name in self.debug else "Internal"
        ap = self.nc.dram_tensor(name, list(shape), dt, kind=kind).ap()
        if name in self.debug:
            self.dbg_out[name] = ap
        return ap

    def setup_arena(self):
        nc = self.nc
        self.NW = 52500
        ar = self.es.enter_context(nc.sbuf_tensor("arena", [128, self.NW], F32))
        self.arena = ar
        self.top = 0
        self.top_hi = self.NW
        self.psum = []
        for i in range(8):
            p = self.es.enter_context(nc.psum_tensor("ps%d" % i, [128, 512], F32))
            self.psum.append(Buf(p[:]))
        self.ps_i = 0

    def alloc(self, nelem, dt=F32, parts=128):
        size = 4 if dt in (F32, I32) else 2
        words = (nelem * size + 3) // 4
        assert self.top + words <= self.top_hi, "SBUF arena overflow %d > %d" % (self.top + words, self.top_hi)
        v = self.arena[0:parts, self.top:self.top + words]
        self.top += words
        self.peak = max(getattr(self, "peak", 0), self.top)
        if dt != F32:
            v = v.bitcast(dt)
            v = v[:, 0:nelem]
        return Buf(v)

    def mark(self):
        return self.top

    def alloc_win(self):
        words = 8 * NCOLS // 2
        self.top_hi = self.NW - words
        assert self.top <= self.top_hi, "SBUF arena overflow (win)"
        v = self.arena[:, self.top_hi:self.NW].bitcast(BF16)
        self.win = Buf(v)
        self.winv = v.rearrange("p (k n) -> p k n", n=NCOLS)

    def win_steps(self, l, stage, engs, nsplit=2):
        steps = []
        H = NCOLS // nsplit
        for k in range(8):
            for hf in range(nsplit):
                idx = nsplit * k + hf

                def step(k=k, hf=hf, idx=idx):
                    sb = stage[idx % len(stage)]
                    self.dma(sb.ap, self.win_d[l, k * 128:(k + 1) * 128, hf * H:(hf + 1) * H], writes=[sb])
                    self.copy(engs[idx % len(engs)], self.winv[:, k, hf * H:(hf + 1) * H], sb.ap, [sb], [self.win])
                steps.append(step)
        return steps

    def release(self, m):
        self.S.barrier()
        self.peaks = getattr(self, "peaks", [])
        self.peaks.append((self.peak, m))
        self.peak = m
        self.top = m

    def ps(self):
        b = self.psum[self.ps_i % 8]
        self.ps_i += 1
        return b

    def act(self, out, in_, func, reads, writes, scale=1.0, bias=0.0, accum_out=None):
        kw = dict(out=out, in_=in_, func=func, scale=scale, bias=bias)
        if accum_out is not None:
            kw["accum_out"] = accum_out
        self.S.op("act", lambda e: e.activation(**kw), reads, writes)

    def tt(self, eng, out, in0, in1, op, reads, writes):
        self.S.op(eng, lambda e: e.tensor_tensor(out=out, in0=in0, in1=in1, op=op), reads, writes)

    def ts(self, eng, out, in0, s1, s2, op0, op1, reads, writes):
        if s2 is None:
            self.S.op(eng, lambda e: e.tensor_scalar(out=out, in0=in0, scalar1=s1, scalar2=None, op0=op0), reads, writes)
        else:
            self.S.op(eng, lambda e: e.tensor_scalar(out=out, in0=in0, scalar1=s1, scalar2=s2, op0=op0, op1=op1), reads, writes)

    def stt(self, eng, out, in0, scalar, in1, op0, op1, reads, writes):
        self.S.op(eng, lambda e: e.scalar_tensor_tensor(out=out, in0=in0, scalar=scalar, in1=in1, op0=op0, op1=op1), reads, writes)

    def copy(self, eng, out, in_, reads, writes):
        if eng == "act":
            self.act(out, in_, AF.Copy, reads, writes)
        else:
            self.S.op(eng, lambda e: e.tensor_copy(out=out, in_=in_), reads, writes)

    def memset(self, eng, out, val, writes):
        self.S.op(eng, lambda e: e.memset(out, val), (), writes)

    def mm(self, out, pairs, reads, writes, start=True, stop=True):
        def fn(e):
            n = len(pairs)
            ins = None
            for i, (l, r) in enumerate(pairs):
                ins = e.matmul(out, lhsT=l, rhs=r, start=(start and i == 0), stop=(stop and i == n - 1))
            return ins
        self.S.op("pe", fn, reads, writes)

    def dma(self, out, in_, reads=(), writes=(), key=None, eng="sp"):
        self.S.dma(out, in_, reads, writes, eng=eng, key=key)

    def build(self):
        nc = self.nc
        S = self.S
        di = self.dram_in
        self.x_d = di("x", [TL, D])
        self.ctx_d = di("ctx", [TC, D])
        self.cv_d = di("cvec", [128, 16])
        self.wmod_d = di("w_mod", [2, D, 3 * D])
        self.bmod_d = di("b_mod", [2, 3 * D])
        self.win_d = di("w_in", [2, D, NCOLS])
        self.wout_d = di("w_out", [2, D, D])
        self.prm_d = di("prm", [2, 128, NPRM])
        self.agw_d = di("agw", [2, 4, 384, 384])
        self.cws_d = di("cwsT", [2, 128, 512])
        self.cbias_d = di("cbias", [2, 128, 256])
        self.fnw_d = di("fnw", [D])
        self.cst_d = di("cst", [128, NCST])
        self.out_d = nc.dram_tensor("out", [TL, D], F32, kind="ExternalOutput").ap()
        ds = self.dram_scratch
        self.AX = ds("AX", [384, T], F32)
        self.SGA = ds("SGA", [384, T], BF16)
        self.Q = ds("Q", [384, T], BF16)
        self.SGF = ds("SGF", [384, T], F32)
        self.SGB = ds("SGB", [384, T], F32)
        self.GB = ds("GB", [384, T], BF16)
        self.VT = ds("VT", [T, 384], BF16)
        self.OF = ds("OF", [384, T], F32)
        self.Y = ds("Y", [D, T], BF16)
        self.X1 = ds("X1", [T, D], F32)
        self.X0 = ds("X0", [TL, D], F32)
        self.tkX0 = Buf(None)
        self.tkAX, self.tkSGA, self.tkQ, self.tkSGF, self.tkSGB, self.tkGB, self.tkVT, self.tkOF, self.tkY, self.tkX1 = [Buf(None) for _ in range(10)]
        for b in (self.tkAX, self.tkSGA, self.tkQ, self.tkSGF, self.tkSGB, self.tkGB, self.tkVT, self.tkOF, self.tkY, self.tkX1, self.tkX0):
            b.tk.multi = True
        self.setup_arena()
        self.phase_setup()
        for l in range(2):
            if self.stop_after == ("setup", l):
                break
            S.new_epoch()
            self.phase_weights(l)
            self.phase_p1(l)
            if self.stop_after == ("p1", l):
                break
            self.phase_a(l)
            if self.halt or self.stop_after == ("a", l):
                break
            self.phase_b(l)
            if self.halt or self.stop_after == ("b", l):
                break
            self.phase_o(l)
            if self.stop_after == ("o", l):
                break
        S.barrier()
        S.emit()
        self.es.close()
        return nc

    def phase_setup(self):
        S = self.S
        al = self.alloc
        self.cst = al(NCST)
        self.dma(self.cst.ap, self.cst_d, writes=[self.cst])
        self.identb = al(128, BF16)
        self.maskf = al(128, BF16)
        self.maskb = al(128, BF16)
        self.copy("dve", self.identb.ap, self.cst[:, C_ID:C_ID + 128], [self.cst], [self.identb])
        self.copy("dve", self.maskf.ap, self.cst[:, C_MF:C_MF + 128], [self.cst], [self.maskf])
        self.copy("dve", self.maskb.ap, self.cst[:, C_MB:C_MB + 128], [self.cst], [self.maskb])
        self.ones = self.cst[:, C_ONE:C_ONE + 128]
        self.smask = self.cst[0:96, C_SMASK:C_SMASK + 2049]
        self.prm = [al(NPRM), al(NPRM)]
        for l in range(2):
            self.dma(self.prm[l].ap, self.prm_d[l], writes=[self.prm[l]])
        self.negpi = al(1)
        self.memset("pool", self.negpi.ap, -PI, [self.negpi])
        self.epsb = al(1)
        self.memset("pool", self.epsb.ap, EPS, [self.epsb])
        self.pc = al(512)
        mpos = self.mark()
        self.omega = al(512)
        self.copy("dve", self.omega.ap, self.cst[:, C_JROW:C_JROW + 512], [self.cst], [self.omega])
        self._pt = (al(512), al(512, I32), al(512))
        self.pos_half_tile(self.pc, None, rcol_ap=self.cst[:, C_CCOL:C_CCOL + 1])
        self.E_d = self.dram_scratch("Etab", [64, 512], F32)
        self.tkE = Buf(None)
        self.dma(self.E_d, self.pc[0:64, :], reads=[self.pc], writes=[self.tkE])
        self.release(mpos)
        p0 = self.prm[0]
        lg = [p0[0:96, P_LBL + 4 * i:P_LBL + 4 * i + 4] for i in range(3)]
        mx = al(4)
        self.tt("dve", mx[0:96, :], lg[0], lg[1], ALU.max, [p0], [mx])
        self.tt("dve", mx[0:96, :], mx[0:96, :], lg[2], ALU.max, [p0, mx], [mx])
        ee = al(12)
        for i in range(3):
            self.tt("dve", ee[0:96, 4 * i:4 * i + 4], lg[i], mx[0:96, :], ALU.subtract, [p0, mx], [ee])
        self.act(ee[0:96, :], ee[0:96, :], AF.Exp, [ee], [ee])
        ssum = al(4)
        self.tt("dve", ssum[0:96, :], ee[0:96, 0:4], ee[0:96, 4:8], ALU.add, [ee], [ssum])
        e01 = al(4)
        self.copy("dve", e01[0:96, :], ssum[0:96, :], [ssum], [e01])
        self.tt("dve", ssum[0:96, :], ssum[0:96, :], ee[0:96, 8:12], ALU.add, [ee, ssum], [ssum])
        S.op("dve", lambda e: e.reciprocal(out=ssum[0:96, :], in_=ssum[0:96, :]), [ssum], [ssum])
        self.lb = [al(4), al(4)]
        self.oml = [al(4), al(4)]
        self.noml = [al(4), al(4)]
        self.tt("dve", self.lb[0][0:96, :], ee[0:96, 0:4], ssum[0:96, :], ALU.mult, [ee, ssum], [self.lb[0]])
        self.tt("dve", self.lb[1][0:96, :], e01[0:96, :], ssum[0:96, :], ALU.mult, [e01, ssum], [self.lb[1]])
        for l in range(2):
            self.ts("dve", self.oml[l][0:96, :], self.lb[l][0:96, :], -1.0, 1.0, ALU.mult, ALU.add, [self.lb[l]], [self.oml[l]])
            self.ts("dve", self.noml[l][0:96, :], self.lb[l][0:96, :], 1.0, -1.0, ALU.mult, ALU.add, [self.lb[l]], [self.noml[l]])
        cv = al(16)
        self.dma(cv.ap, self.cv_d, writes=[cv])
        self.act(cv.ap, cv.ap, AF.Silu, [cv], [cv])
        crep = al(8 * 128)
        crv = crep.ap.rearrange("p (k m) -> p k m", m=128)
        self.copy("dve", crv[:, :, 0:64], bcast_last(cv[:, 0:8], 64), [cv], [crep])
        self.copy("dve", crv[:, :, 64:128], bcast_last(cv[:, 8:16], 64), [cv], [crep])
        self.scw = [[None, None], [None, None]]
        self.shT = [[None, None], [None, None]]
        self.gb = [[None, None], [None, None]]
        for l in range(2):
            for s in range(2):
                self.scw[l][s] = al(8)
                self.shT[l][s] = al(8)
                self.gb[l][s] = al(D) if (l, s) != (1, 1) else None
        m0 = self.mark()
        self.alloc_win()
        wst = [al(NCOLS // 2) for _ in range(3)]
        wsteps0 = self.win_steps(0, wst, ["act", "dve"])
        bm = al(3 * D, parts=1)
        stage = [al(3 * D) for _ in range(3)]
        modsb = al(3 * D)
        for l in range(2):
            bmv = self.bmod_d[l]
            self.dma(bm.ap, AP(bmv.tensor, bmv.offset, [[0, 1], [1, 3 * D]]), writes=[bm])
            banks = [self.ps() for _ in range(6)]
            for k in range(8):
                st = stage[k % 3]
                self.dma(st.ap, self.wmod_d[l, k * 128:(k + 1) * 128, :], writes=[st])
                for n in range(6):
                    self.mm(banks[n].ap, [(crv[:, k, :], st[:, n * 512:(n + 1) * 512])], [crep, st], [banks[n]],
                            start=(k == 0), stop=False)
                for _ in range(2):
                    if wsteps0:
                        wsteps0.pop(0)()
            for n in range(6):
                self.mm(banks[n].ap, [(self.ones[0:1, :], bm[0:1, n * 512:(n + 1) * 512])], [self.cst, bm], [banks[n]],
                        start=False, stop=True)
                self.copy("act" if n % 2 else "dve", modsb[:, n * 512:(n + 1) * 512], banks[n].ap, [banks[n]], [modsb])
            for s in range(2):
                r0 = 64 * s
                for n in range(2):
                    if self.gb[l][s] is None:
                        continue
                    pb = self.ps()
                    self.mm(pb.ap, [(self.ones[r0:r0 + 1, :], modsb[r0:r0 + 1, 2 * D + n * 512:2 * D + (n + 1) * 512])],
                            [self.cst, modsb], [pb])
                    self.copy("dve", self.gb[l][s][:, n * 512:(n + 1) * 512], pb.ap, [pb], [self.gb[l][s]])
                pt = self.ps()
                for k in range(8):
                    self.mm(pt[:, k:k + 1], [(modsb[r0:r0 + 1, D + k * 128:D + (k + 1) * 128], self.ones[r0:r0 + 1, 0:1])],
                            [modsb, self.cst], [pt])
                    self.mm(pt[:, 8 + k:9 + k], [(modsb[r0:r0 + 1, k * 128:(k + 1) * 128], self.ones[r0:r0 + 1, 0:1])],
                            [modsb, self.cst], [pt])
                self.stt("dve", self.scw[l][s].ap, pt[:, 0:8], 1.0, self.prm[l][:, P_NORMW:P_NORMW + 8], ALU.add, ALU.mult,
                         [pt, self.prm[l]], [self.scw[l][s]])
                self.copy("dve", self.shT[l][s].ap, pt[:, 8:16], [pt], [self.shT[l][s]])
        while wsteps0:
            wsteps0.pop(0)()
        self.release(m0)

    def phase_weights(self, l):
        al = self.alloc
        self.m_layer = self.mark()
        self.cws = al(512, BF16)
        self.cbias = al(256)
        m = self.mark()
        s = al(512)
        self.dma(s.ap, self.cws_d[l], writes=[s])
        self.copy("dve", self.cws.ap, s.ap, [s], [self.cws])
        self.dma(self.cbias.ap, self.cbias_d[l], writes=[self.cbias])
        self.release(m)
        self.halfc = al(6)
        prm = self.prm[l]
        self.act(self.halfc.ap, prm[:, P_LAM:P_LAM + 6], AF.Exp, [prm], [self.halfc], scale=-1.0)
        self.act(self.halfc.ap, self.halfc.ap, AF.Ln, [self.halfc], [self.halfc], bias=1.0)
        self.ts("dve", self.halfc.ap, self.halfc.ap, -4.0, None, ALU.mult, None, [self.halfc], [self.halfc])
        self.ahalf = al(18)
        self.ts("dve", self.ahalf[:, 0:6], self.halfc.ap, 0.5, None, ALU.mult, None, [self.halfc], [self.ahalf])
        self.ts("dve", self.ahalf[:, 6:12], prm[:, P_BR:P_BR + 6], 0.5, None, ALU.mult, None, [prm], [self.ahalf])
        self.ts("dve", self.ahalf[:, 12:18], prm[:, P_BI:P_BI + 6], 0.5, None, ALU.mult, None, [prm], [self.ahalf])

    def phase_p1(self, l):
        al = self.alloc
        S = self.S
        m_p1 = self.mark()
        prm = self.prm[l]
        win, winv = self.win, self.winv
        xt = [al(D) for _ in range(8)]
        xn = [al(D, BF16) for _ in range(4)]
        hT = [al(8 * 512, BF16) for _ in range(2)]
        ss = [al(4) for _ in range(2)]
        rstd = [al(4) for _ in range(2)]
        prt = [al(512) for _ in range(4 if l == 0 else 0)]
        sq_junk = al(D, BF16)
        NF32, NBF = (6, 9) if l == 0 else (8, 12)
        f32t = [al(512) for _ in range(NF32)]
        bft = [al(512, BF16) for _ in range(NBF)]
        vtt = [al(384, BF16) for _ in range(2)]
        cvt = [al(256) for _ in range(4)]
        vhat = [al(256, BF16) for _ in range(4)]
        stats = al(4 * 6)
        mv = al(8)
        cu = [al(1024, BF16) for _ in range(1)]
        cg = [al(1024, BF16) for _ in range(1)]
        ug = [al(1024, BF16) for _ in range(1)]
        ycb = [al(1024, BF16) for _ in range(2)]
        t1 = [al(128) for _ in range(2)]
        cnt = {"f": 0, "b": 0, "v": 0}
        NBLK = len(BLOCKS)

        def src_rows(t0):
            if l == 0:
                return self.ctx_d[t0:t0 + 128, :] if t0 < TC else self.x_d[t0 - TC:t0 - TC + 128, :]
            return self.X1[t0:t0 + 128, :]

        def xbuf(bi, i):
            return xt[(bi % 2) * 4 + i]

        def loads(bi):
            if bi >= NBLK:
                return
            c0, nt = BLOCKS[bi]
            for i in range(nt):
                rd = [self.tkX1] if l > 0 else []
                self.dma(xbuf(bi, i).ap, src_rows(c0 + i * 128), reads=rd, writes=[xbuf(bi, i)])
                if l == 0 and c0 >= TC:
                    tglob = (c0 - TC) // 128 + i
                    pr = prt[i]
                    for half in range(2):
                        src = AP(self.E_d.tensor, self.E_d.offset + (2 * tglob + half) * 512, [[0, 64], [1, 512]])
                        self.dma(pr[half * 64:(half + 1) * 64, :], src, reads=[self.tkE], writes=[pr])

        def part0(bi):
            if bi >= NBLK or l != 0:
                return
            c0, nt = BLOCKS[bi]
            if c0 < TC:
                return
            for i in range(nt):
                xb = xbuf(bi, i)
                pr = prt[i]
                self.tt("dve", xb[:, 0:512], xb[:, 0:512], pr.ap, ALU.add, [xb, pr], [xb])
                self.tt("dve", xb[:, 512:1024], xb[:, 512:1024], self.pc.ap, ALU.add, [xb, self.pc], [xb])
                self.dma(self.X0[c0 - TC + i * 128:c0 - TC + (i + 1) * 128, :], xb.ap, reads=[xb], writes=[self.tkX0])

        def part1(bi):
            if bi >= NBLK:
                return
            c0, nt = BLOCKS[bi]
            sb, rb = ss[bi % 2], rstd[bi % 2]
            self.memset("pool", sb.ap, 0.0, [sb])
            for i in range(nt):
                xb = xbuf(bi, i)
                self.act(sq_junk.ap, xb.ap, AF.Square, [xb], [sq_junk, sb], accum_out=sb[:, i:i + 1])
            self.act(rb[:, 0:nt], sb[:, 0:nt], AF.Sqrt, [sb, self.epsb], [rb], scale=1.0 / D, bias=self.epsb.ap)
            S.op("dve", lambda e, rb=rb, nt=nt: e.reciprocal(out=rb[:, 0:nt], in_=rb[:, 0:nt]), [rb], [rb])
            for i in range(nt):
                self.ts("dve", xn[i].ap, xbuf(bi, i).ap, rb[:, i:i + 1], None, ALU.mult, None, [xbuf(bi, i), rb], [xn[i]])

        def part2(bi):
            if bi >= NBLK:
                return
            c0, nt = BLOCKS[bi]
            s_ctx = 1 if c0 < TC else 0
            h = hT[bi % 2]
            hv = h.ap.rearrange("p (k t) -> p k t", t=512)
            scw, shT = self.scw[l][s_ctx], self.shT[l][s_ctx]
            for i in range(nt):
                xnb = xn[i]
                pb = self.ps()
                pbv = pb.ap.bitcast(BF16)
                S.op("pe", lambda e, pbv=pbv, xnb=xnb: [e.transpose(out=pbv[:, k * 128:(k + 1) * 128], in_=xnb[:, k * 128:(k + 1) * 128],
                                                                       identity=self.identb.ap) for k in range(8)][-1],
                     [xnb, self.identb], [pb])
                pv = pbv.rearrange("p (k t) -> p k t", t=128)
                for k in range(8):
                    if k % 2 == 0:
                        self.ts("dve", hv[:, k, i * 128:(i + 1) * 128], pv[:, k, :], scw[:, k:k + 1], shT[:, k:k + 1], ALU.mult, ALU.add,
                                [pb, scw, shT], [h])
                    else:
                        self.act(hv[:, k, i * 128:(i + 1) * 128], pv[:, k, :], AF.Identity, [pb, scw, shT], [h],
                                 scale=scw[:, k:k + 1], bias=shT[:, k:k + 1])
            if "hT" in self.debug and bi == 1 and l == 0:
                d = self.dram_scratch("hT", [128, 8 * 512], BF16)
                self.dma(d, h.ap, reads=[h])

        def f32tile():
            b = f32t[cnt["f"] % NF32]
            cnt["f"] += 1
            return b

        def bftile():
            b = bft[cnt["b"] % NBF]
            cnt["b"] += 1
            return b

        def inproj(bi, mid_hook):
            c0, nt = BLOCKS[bi]
            ncol = nt * 128
            h = hT[bi % 2]
            hv = h.ap.rearrange("p (k t) -> p k t", t=512)
            cub, cgb, ugb, ycbb = cu[0], cg[0], ug[0], ycb[bi % 2]

            def proj(col0, M):
                pb = self.ps()
                self.mm(pb[0:M, 0:ncol], [(winv[:, k, col0:col0 + M], hv[:, k, 0:ncol]) for k in range(8)], [win, h], [pb])
                return pb
            for j in range(3):
                pb = proj(384 + j * 128, 128)
                o = bftile()
                self.act(o[:, 0:ncol], pb[:, 0:ncol], AF.Silu, [pb], [o])
                self.dma(self.SGA[j * 128:(j + 1) * 128, c0:c0 + ncol], o[:, 0:ncol], reads=[o], writes=[self.tkSGA], eng="act")
            def bsplit(colb, dst, tk, func, tile_fn):
                for s3 in range(3):
                    pb = proj(colb + s3 * 128, 128)
                    o = tile_fn()
                    self.act(o[:, 0:ncol], pb[:, 0:ncol], func, [pb], [o])
                    self.dma(dst[s3 * 128:(s3 + 1) * 128, c0:c0 + ncol], o[:, 0:ncol], reads=[o], writes=[tk], eng="act")
            bsplit(768, self.Q, self.tkQ, AF.Silu, bftile)
            bsplit(2304, self.GB, self.tkGB, AF.Silu, bftile)
            for j in range(2):
                pb = proj(3200 + j * 128, 128)
                self.act(cgb[:, j * 512:j * 512 + ncol], pb[:, 0:ncol], AF.Silu, [pb], [cgb])
            mid_hook()
            bsplit(1152, self.SGF, self.tkSGF, AF.Sigmoid, f32tile)
            bsplit(1536, self.SGB, self.tkSGB, AF.Sigmoid, f32tile)
            for j in range(2):
                pb = proj(2688 + j * 128, 128)
                self.act(cub[:, j * 512:j * 512 + ncol], pb[:, 0:ncol], AF.Gelu_apprx_tanh, [pb], [cub])
            w3 = lambda b: b.ap.rearrange("p (j t) -> p j t", t=512)[:, :, 0:ncol]
            self.tt("dve", w3(ugb), w3(cub), w3(cgb), ALU.mult, [cub, cgb], [ugb])
            stv = stats.ap.rearrange("p (i s) -> p i s", s=6)
            mvv = mv.ap.rearrange("p (i s) -> p i s", s=2)
            for i in range(nt):
                hs = [hv[:, k, i * 128:(i + 1) * 128] for k in range(8)]
                pb = self.ps()
                self.mm(pb[:, 0:256], [(hs[k], winv[:, k, 2944:3200]) for k in range(8)], [win, h], [pb])
                cvb = cvt[i]
                self.act(cvb.ap, pb[:, 0:256], AF.Gelu_apprx_tanh, [pb], [cvb])
                S.op("dve", lambda e, i=i, cvb=cvb: e.bn_stats(out=stv[:, i, :], in_=cvb.ap), [cvb], [stats])
                S.op("dve", lambda e, i=i: e.bn_aggr(out=mvv[:, i, :], in_=stv[:, i, :]), [stats], [mv])
            for i in range(nt):
                hs = [hv[:, k, i * 128:(i + 1) * 128] for k in range(8)]
                pb = self.ps()
                self.mm(pb[:, 0:384], [(hs[k], winv[:, k, 1920:2304]) for k in range(8)], [win, h], [pb])
                vb = vtt[cnt["v"] % 2]
                cnt["v"] += 1
                self.copy("dve", vb.ap, pb[:, 0:384], [pb], [vb])
                self.dma(self.VT[c0 + i * 128:c0 + (i + 1) * 128, :], vb.ap, reads=[vb], writes=[self.tkVT])
            self.act(mvv[:, 0:nt, 1], mvv[:, 0:nt, 1], AF.Sqrt, [mv, self.epsb], [mv], bias=self.epsb.ap)
            S.op("dve", lambda e: e.reciprocal(out=mvv[:, 0:nt, 1], in_=mvv[:, 0:nt, 1]), [mv], [mv])
            for j in range(3):
                pb = proj(j * 128, 128)
                o = f32tile()
                self.act(o[:, 0:ncol], pb[:, 0:ncol], AF.Copy, [pb], [o])
                self.dma(self.AX[j * 128:(j + 1) * 128, c0:c0 + ncol], o[:, 0:ncol], reads=[o], writes=[self.tkAX], eng="act")
            cwv = self.cws.ap.rearrange("p (h s) -> p h s", s=128)
            for i in range(nt):
                self.ts("dve", vhat[i].ap, cvt[i].ap, mvv[:, i, 0:1], mvv[:, i, 1:2], ALU.subtract, ALU.mult, [cvt[i], mv], [vhat[i]])
            pcs = []
            for i in range(nt):
                vh = vhat[i]
                pc_ = self.ps()
                pcv = pc_.ap.rearrange("p (j s) -> p j s", s=256)

                def cmix(e, pcv=pcv, vh=vh):
                    ins = None
                    for j in range(2):
                        for hl in range(2):
                            hh = 2 * j + hl
                            ins = e.matmul(pcv[hl * 64:(hl + 1) * 64, j, 0:128], lhsT=vh[:, hh * 64:(hh + 1) * 64], rhs=cwv[:, hh, :],
                                           start=True, stop=True)
                    return ins
                S.op("pe", cmix, [vh, self.cws], [pc_])
                pcs.append((pc_, pcv))
            for i in range(nt):
                pc_, pcv = pcs[i]
                for j in range(2):
                    tb = t1[j]
                    self.stt("dve", tb.ap, pcv[:, j, 0:128], prm[:, P_CNW + j:P_CNW + j + 1], self.cbias[:, j * 128:(j + 1) * 128],
                             ALU.mult, ALU.add, [pc_, prm, self.cbias], [tb])
                    self.tt("dve", ycbb[:, j * 512 + i * 128:j * 512 + (i + 1) * 128], tb.ap,
                            ugb[:, j * 512 + i * 128:j * 512 + (i + 1) * 128], ALU.mult, [tb, ugb], [ycbb])
            for j in range(2):
                self.dma(self.Y[768 + j * 128:768 + (j + 1) * 128, c0:c0 + ncol], ycbb[:, j * 512:j * 512 + ncol], reads=[ycbb], writes=[self.tkY])

        loads(0)
        part0(0)
        part1(0)
        loads(1)
        part2(0)
        part0(1)
        for bi in range(NBLK):
            part1(bi + 1)
            loads(bi + 2)
            inproj(bi, lambda bi=bi: part2(bi + 1))
            part0(bi + 2)
        self.release(m_p1)
        self.top_hi = self.NW

    def pos_half_tile(self, dst, tglob, rcol_ap=None):
        arg, ni, nf = self._pt
        if rcol_ap is None:
            rcol_ap = self.cst[:, C_RCOL + tglob:C_RCOL + tglob + 1]
        self.stt("dve", arg.ap, self.omega.ap, rcol_ap, self.cst[:, C_PH:C_PH + 512],
                 ALU.mult, ALU.add, [self.omega, self.cst], [arg])
        self.ts("dve", ni.ap, arg.ap, 1.0 / (2 * PI), None, ALU.mult, None, [arg], [ni])
        self.copy("dve", nf.ap, ni.ap, [ni], [nf])
        self.stt("dve", arg.ap, nf.ap, -2 * PI, arg.ap, ALU.mult, ALU.add, [nf, arg], [arg])
        self.act(dst.ap, arg.ap, AF.Sin, [arg], [dst])

    def phase_a(self, l):
        al = self.alloc
        S = self.S
        m_a = self.mark()
        prm = self.prm[l]
        NP = 2 + TC + 1 + 2 + TL + 1
        xcb = al(3 * T, BF16)
        xcbv = xcb.ap.rearrange("p (j t) -> p j t", t=T)
        XC = self.dram_scratch("XC%d" % l, [384, T], F32)
        tkXC = Buf(None)
        tkXC.tk.multi = True
        self.agw = al(4 * 3 * 384, BF16)
        self.agwv = self.agw.ap.rearrange("p (g k n) -> p g k n", g=4, k=3)
        m0 = self.mark()
        agst = [al(3 * 384), al(3 * 384)]
        for g in range(4):
            sg_ = agst[g % 2]
            self.dma(sg_.ap.rearrange("p (k n) -> p k n", k=3), self.agw_d[l, g].rearrange("(k p) n -> p k n", p=128), writes=[sg_])
            self.copy("act", self.agwv[:, g, :, :], sg_.ap.rearrange("p (k n) -> p k n", k=3), [sg_], [self.agw])
        axp = [al(NP), al(NP)]
        xcj0 = [al(T), al(T)]
        for b in axp:
            self.memset("pool", b[:, 0:2], 0.0, [b])
            self.memset("pool", b[:, 2 + TC:2 + TC + 3], 0.0, [b])
            self.memset("pool", b[:, NP - 1:NP], 0.0, [b])
        for j in range(3):
            ax, xj = axp[j % 2], xcj0[j % 2]
            self.dma(ax[:, 2:2 + TC], self.AX[j * 128:(j + 1) * 128, 0:TC], reads=[self.tkAX], writes=[ax])
            self.dma(ax[:, 5 + TC:5 + TC + TL], self.AX[j * 128:(j + 1) * 128, TC:T], reads=[self.tkAX], writes=[ax])
            eng = "dve"
            cw = [prm[:, P_CONVW + 4 * j + k:P_CONVW + 4 * j + k + 1] for k in range(4)]
            for (o0, n, b0) in ((0, TC, 0), (TC, TL, 3 + TC)):
                self.ts(eng, xj[:, o0:o0 + n], ax[:, b0:b0 + n], cw[0], prm[:, P_CONVB + j:P_CONVB + j + 1], ALU.mult, ALU.add,
                        [ax, prm], [xj])
                for k in range(1, 4):
                    self.stt(eng, xj[:, o0:o0 + n], ax[:, b0 + k:b0 + k + n], cw[k], xj[:, o0:o0 + n], ALU.mult, ALU.add,
                             [ax, prm, xj], [xj])
            self.copy("act", xcbv[:, j, :], xj.ap, [xj], [xcb])
            self.dma(XC[j * 128:(j + 1) * 128, :], xj.ap, reads=[xj], writes=[tkXC])
        self.release(m0)
        if self.stop_after == ("a0", l):
            self.halt = True
            return
        xcj = al(T)
        a_row = al(T)
        nw_row = al(T)
        ix_row = al(T)
        hf_j = al(T)
        sga_row = al(T, BF16)
        y_row = a_row.ap.bitcast(BF16)[:, 0:T]
        NR = 5
        rt = [al(512) for _ in range(NR)]
        it = [al(512) for _ in range(NR)]
        tt_ = [al(512) for _ in range(NR)]
        dt_ = [al(512) for _ in range(NR)]
        kch = {0: (0, 1), 1: (0, 1, 2), 2: (1, 2)}
        itn = 0
        for j in range(3):
            self.dma(xcj.ap, XC[j * 128:(j + 1) * 128, :], reads=[tkXC], writes=[xcj])
            self.dma(sga_row.ap, self.SGA[j * 128:(j + 1) * 128, :], reads=[self.tkSGA], writes=[sga_row])
            for d in range(2):
                for (c0, nt) in BLOCKS:
                    n = nt * 128
                    q = itn % NR
                    itn += 1
                    r_, i_, t_, d_ = rt[q], it[q], tt_[q], dt_[q]
                    pr, pi = self.ps(), self.ps()
                    self.mm(pr[:, 0:n], [(self.agwv[:, 2 * d, k, j * 128:(j + 1) * 128], xcbv[:, k, c0:c0 + n]) for k in kch[j]], [self.agw, xcb], [pr])
                    self.mm(pi[:, 0:n], [(self.agwv[:, 2 * d + 1, k, j * 128:(j + 1) * 128], xcbv[:, k, c0:c0 + n]) for k in kch[j]], [self.agw, xcb], [pi])
                    cidx = 3 * d + j
                    ah = self.ahalf
                    self.act(r_[:, 0:n], pr[:, 0:n], AF.Tanh, [pr, ah], [r_], scale=0.5, bias=ah[:, 6 + cidx:7 + cidx])
                    self.act(i_[:, 0:n], pi[:, 0:n], AF.Tanh, [pi, ah], [i_], scale=0.5, bias=ah[:, 12 + cidx:13 + cidx])
                    self.act(t_[:, 0:n], r_[:, 0:n], AF.Tanh, [r_, ah], [t_], scale=ah[:, cidx:cidx + 1], bias=ah[:, cidx:cidx + 1])
                    self.act(d_[:, 0:n], r_[:, 0:n], AF.Exp, [r_, self.halfc], [d_], scale=self.halfc[:, cidx:cidx + 1],
                             bias=self.halfc[:, cidx:cidx + 1])
                    self.stt("dve", d_[:, 0:n], d_[:, 0:n], 1.0, t_[:, 0:n], ALU.add, ALU.mult, [d_, t_], [d_])
                    self.ts("dve", a_row[:, c0:c0 + n], d_[:, 0:n], 1.0, None, ALU.add, None, [d_], [a_row])
                    self.stt("dve", ix_row[:, c0:c0 + n], i_[:, 0:n], 1.0, xcj[:, c0:c0 + n], ALU.add, ALU.mult, [i_, xcj], [ix_row])
                if self.stop_after == ("a1", l):
                    self.halt = True
                    return
                self.act(nw_row.ap, a_row.ap, AF.Square, [a_row], [nw_row])
                self.act(nw_row.ap, nw_row.ap, AF.Sqrt, [nw_row], [nw_row], scale=-0.25, bias=0.25)
                self.tt("dve", ix_row.ap, ix_row.ap, nw_row.ap, ALU.mult, [ix_row, nw_row], [ix_row])
                if self.stop_after == ("a2", l):
                    self.halt = True
                    return
                if d == 0:
                    S.op("dve", lambda e: e.tensor_tensor_scan(out=hf_j.ap, data0=a_row.ap, data1=ix_row.ap, initial=0.0,
                                                               op0=ALU.mult, op1=ALU.add), [a_row, ix_row], [hf_j])
                    if self.stop_after == ("a3", l):
                        self.halt = True
                        return
                else:
                    S.op("dve", lambda e: e.tensor_tensor_scan(out=rev(nw_row[:, 0:TC]), data0=rev(a_row[:, 0:TC]), data1=rev(ix_row[:, 0:TC]),
                                                               initial=0.0, op0=ALU.mult, op1=ALU.add), [a_row, ix_row], [nw_row])
                    S.op("dve", lambda e: e.tensor_tensor_scan(out=rev(nw_row[:, TC:T]), data0=rev(a_row[:, TC:T]), data1=rev(ix_row[:, TC:T]),
                                                               initial=nw_row[:, 0:1], op0=ALU.mult, op1=ALU.add), [a_row, ix_row, nw_row], [nw_row])
                    if self.stop_after == ("a4", l):
                        self.halt = True
                        return
                    self.tt("dve", nw_row.ap, nw_row.ap, hf_j.ap, ALU.add, [nw_row, hf_j], [nw_row])
                    self.tt("dve", y_row, nw_row.ap, sga_row.ap, ALU.mult, [nw_row, sga_row, a_row], [a_row])
                    if self.stop_after == ("a5", l):
                        self.halt = True
                        return
                    self.dma(self.Y[j * 128:(j + 1) * 128, :], y_row, reads=[a_row], writes=[self.tkY])
                    if self.stop_after == ("a6", l):
                        self.halt = True
                        return
                if self.stop_after == ("aj%dd%d" % (j, d), l):
                    self.halt = True
                    return
        self.release(m_a)

    def phase_b(self, l):
        al = self.alloc
        S = self.S
        m_b = self.mark()
        prm = self.prm[l]
        lb, oml, noml = self.lb[l], self.oml[l], self.noml[l]
        P = 96
        W = 2048
        Sst = al(384, parts=P)
        Sbf2 = [al(384, BF16, parts=P) for _ in range(3)]
        st = {"cur": 0}
        qb = [al(W, BF16, parts=P) for _ in range(2)]
        sgb = [al(W, parts=P) for _ in range(2)]
        vtb = [al(4 * 384, BF16) for _ in range(2)]
        ofb = [al(W, parts=P) for _ in range(2)]
        gbb = [al(W, BF16, parts=P) for _ in range(2)]
        lf = al(W, parts=P)
        G = al(W, parts=P)
        eG = al(W, BF16, parts=P)
        enG = al(W, BF16, parts=P)
        kk = al(W, BF16, parts=P)
        kgs = [al(W, BF16, parts=P) for _ in range(2)]
        qgs = [al(W, BF16, parts=P) for _ in range(2)]
        decs = [al(32, parts=P) for _ in range(2)]
        kets = [al(4 * 384, BF16) for _ in range(2)]
        scm = [al(512, BF16) for _ in range(2)]
        ob = [al(W, parts=P) for _ in range(2)]
        o2 = al(512, parts=P)
        rs = al(512, parts=P)
        yb = [al(W, BF16, parts=P) for _ in range(2)]
        tmpS = al(384, parts=P)
        xtk = {id(b): [Buf(None) for _ in range(3)] for b in qb + sgb + gbb}
        Qv = self.Q.rearrange("(h d) t -> d h t", d=96)
        GBv = self.GB.rearrange("(h d) t -> d h t", d=96)
        OFv = self.OF.rearrange("(h d) t -> d h t", d=96)
        Yv = self.Y[384:768, :].rearrange("(h d) t -> d h t", d=96)
        v3 = lambda b: b.ap.rearrange("p (h t) -> p h t", t=512)
        S3 = Sst.ap.rearrange("p (h v) -> p h v", v=96)
        T3 = tmpS.ap.rearrange("p (h v) -> p h v", v=96)
        for d in range(2):
            SG = self.SGF if d == 0 else self.SGB
            tkSG = self.tkSGF if d == 0 else self.tkSGB
            SGv = SG.rearrange("(h d) t -> d h t", d=96)
            order = list(range(len(BLOCKS))) if d == 0 else [0] + list(range(len(BLOCKS) - 1, 0, -1))
            N = len(order)
            mask = self.maskf if d == 0 else self.maskb
            self.memset("pool", Sst.ap, 0.0, [Sst])
            self.memset("pool", Sbf2[0].ap, 0.0, [Sbf2[0]])
            self.memset("pool", Sbf2[1].ap, 0.0, [Sbf2[1]])
            self.memset("pool", Sbf2[2].ap, 0.0, [Sbf2[2]])

            def geom(oi):
                c0, nt = BLOCKS[order[oi]]
                return c0, nt, nt * 128

            def load_heads(buf, src, c0, n, tk):
                comp = src.rearrange("(s p) t -> p s t", p=128)
                self.dma(v3(buf)[:, 0:3, 0:n], comp[0:96, :, c0:c0 + n], reads=[tk], writes=[buf])
                for s3 in range(3):
                    self.dma(v3(buf)[32 * s3:32 * (s3 + 1), 3, 0:n], comp[96:128, s3, c0:c0 + n], reads=[tk], writes=[xtk[id(buf)][s3]])

            def RD(buf):
                return [buf] + xtk[id(buf)]

            def loadsA(oi):
                if oi >= N:
                    return
                c0, nt, n = geom(oi)
                q_, s_ = qb[oi % 2], sgb[oi % 2]
                load_heads(q_, self.Q, c0, n, self.tkQ)
                load_heads(s_, SG, c0, n, tkSG)

            def loadsB(oi):
                if oi >= N:
                    return
                c0, nt, n = geom(oi)
                v_ = vtb[oi % 2]
                self.dma(v_.ap.rearrange("p (n c) -> p n c", c=384)[:, 0:nt, :],
                         self.VT[c0:c0 + n, :].rearrange("(n p) c -> p n c", p=128), reads=[self.tkVT], writes=[v_])
                if d == 1:
                    self.dma(v3(ofb[oi % 2])[:, :, 0:n], OFv[:, :, c0:c0 + n], reads=[self.tkOF], writes=[ofb[oi % 2]])
                    load_heads(gbb[oi % 2], self.GB, c0, n, self.tkGB)

            def pw_stages(oi):
                if oi >= N:
                    return []
                c0, nt, n = geom(oi)
                nch = nt * 2
                q_, s_ = qb[oi % 2], sgb[oi % 2]
                kg, qg, dec = kgs[oi % 2], qgs[oi % 2], decs[oi % 2]
                full = (n == 512)
                sl = (lambda b: b.ap) if full else (lambda b: v3(b)[:, :, 0:n])
                dv = dec.ap.rearrange("p (h c) -> p h c", c=8)

                def stA():
                    for hh in range(4):
                        self.act(v3(lf)[:, hh, 0:n], v3(s_)[:, hh, 0:n], AF.Ln, RD(s_) + [oml, lb], [lf],
                                 scale=oml[0:P, hh:hh + 1], bias=lb[0:P, hh:hh + 1])
                        self.act(v3(kk)[:, hh, 0:n], v3(s_)[:, hh, 0:n], AF.Identity, RD(s_) + [noml, oml], [kk],
                                 scale=noml[0:P, hh:hh + 1], bias=oml[0:P, hh:hh + 1])

                def stB():
                    if full:
                        views = [(lf.ap, G.ap, W)]
                    else:
                        views = [(v3(lf)[:, hh, 0:n], v3(G)[:, hh, 0:n], n) for hh in range(4)]
                    for (src, dst, mlen) in views:
                        if d == 0:
                            S.op("dve", lambda e, src=src, dst=dst, mlen=mlen: e.tensor_tensor_scan(
                                out=dst, data0=self.smask[:, 0:mlen], data1=src, initial=0.0, op0=ALU.mult, op1=ALU.add), [lf, self.cst], [G])
                        else:
                            S.op("dve", lambda e, src=src, dst=dst, mlen=mlen: e.tensor_tensor_scan(
                                out=rev(dst), data0=rev(self.smask[:, 1:mlen + 1]), data1=rev(src), initial=0.0, op0=ALU.mult, op1=ALU.add),
                                [lf, self.cst], [G])

                def stC():
                    self.act(sl(eG), sl(G), AF.Exp, [G], [eG])
                    self.act(sl(enG), sl(G), AF.Exp, [G], [enG], scale=-1.0)
                    g4 = G.ap.rearrange("p (h c s) -> p h c s", h=4, s=64)
                    pos = 63 if d == 0 else 0
                    self.act(dv[:, :, 0:nch], g4[:, :, 0:nch, pos], AF.Exp, [G], [dec])

                def stD():
                    self.tt("dve", sl(kg), sl(kk), sl(enG), ALU.mult, [kk, enG], [kg])
                    self.tt("dve", sl(qg), sl(q_), sl(eG), ALU.mult, RD(q_) + [eG], [qg])
                return [stA, stB, stC, stD]

            def transposes(oi):
                if oi >= N:
                    return
                c0, nt, n = geom(oi)
                ke, ket = kgs[oi % 2], kets[oi % 2]
                kev = ket.ap.rearrange("p (n c) -> p n c", c=384)
                for i in range(nt):
                    pb = self.ps()
                    pbv = pb.ap.bitcast(BF16)
                    S.op("pe", lambda e, pbv=pbv, i=i, ke=ke: [e.transpose(out=pbv[:, hh * 96:(hh + 1) * 96], in_=v3(ke)[:, hh, i * 128:(i + 1) * 128],
                                                                              identity=self.identb[0:96, 0:96]) for hh in range(4)][-1],
                         [ke, self.identb], [pb])
                    self.copy("act", kev[:, i, :], pbv[:, 0:384], [pb], [ket])

            def core_tile(oi, i):
                c0, nt, n = geom(oi)
                kg, qg, dec, ket, v_ = kgs[oi % 2], qgs[oi % 2], decs[oi % 2], kets[oi % 2], vtb[oi % 2]
                o_ = ob[oi % 2]
                vv = v_.ap.rearrange("p (n c) -> p n c", c=384)
                kev = ket.ap.rearrange("p (n c) -> p n c", c=384)
                dv = dec.ap.rearrange("p (h c) -> p h c", c=8)
                psc = self.ps()
                pscv = psc.ap.rearrange("p (h c) -> p h c", c=128)
                S.op("pe", lambda e: [e.matmul(pscv[:, hh, :], lhsT=v3(kg)[:, hh, i * 128:(i + 1) * 128],
                                               rhs=v3(qg)[:, hh, i * 128:(i + 1) * 128], start=True, stop=True)
                                      for hh in range(4)][-1], [kg, qg], [psc])
                sc = scm[i % 2]
                scv = sc.ap.rearrange("p (h c) -> p h c", c=128)
                self.tt("dve", scv, pscv, bcast_mid(mask.ap, 4), ALU.mult, [psc, mask], [sc])
                po = self.ps()
                pov = po.ap.rearrange("p (h c) -> p h c", c=128)
                chunks = (0, 1) if d == 0 else (1, 0)
                pks = []
                for c in chunks:
                    pk = self.ps()
                    pkv = pk.ap.rearrange("p (h v) -> p h v", v=128)
                    S.op("pe", lambda e, pkv=pkv, c=c: [e.matmul(
                        pkv[0:P, hh, 0:96], lhsT=kev[c * 64:(c + 1) * 64, i, hh * 96:(hh + 1) * 96],
                        rhs=vv[c * 64:(c + 1) * 64, i, hh * 96:(hh + 1) * 96], start=True, stop=True) for hh in range(4)][-1],
                        [ket, v_], [pk])
                    pks.append((pk, pkv))

                def update(ci):
                    c = chunks[ci]
                    pk, pkv = pks[ci]
                    dcol = dv[:, :, i * 2 + c]
                    dcb = AP(dcol.tensor, dcol.offset, [list(dcol.ap[0]), list(dcol.ap[1]), [0, 96]])
                    self.tt("dve", T3, S3, pkv[0:P, :, 0:96], ALU.add, [Sst, pk], [tmpS])
                    self.tt("dve", S3, T3, dcb, ALU.mult, [tmpS, dec], [Sst])
                    st["cur"] = (st["cur"] + 1) % 3
                    self.copy("dve", Sbf2[st["cur"]].ap, Sst.ap, [Sst], [Sbf2[st["cur"]]])
                s_in0 = Sbf2[st["cur"]]
                update(0)
                s_in1 = Sbf2[st["cur"]]
                sins = {chunks[0]: s_in0, chunks[1]: s_in1}

                def ogroup(e):
                    ins = None
                    for hh in range(4):
                        e.matmul(pov[0:P, hh, :], lhsT=vv[:, i, hh * 96:(hh + 1) * 96], rhs=scv[:, hh, :], start=True, stop=False)
                        for ci2, c in enumerate(chunks):
                            col = i * 128 + c * 64
                            ins = e.matmul(pov[0:P, hh, c * 64:(c + 1) * 64], lhsT=sins[c][:, hh * 96:(hh + 1) * 96],
                                           rhs=v3(qg)[:, hh, col:col + 64], start=False, stop=(ci2 == 1))
                    return ins
                S.op("pe", ogroup, [v_, sc, s_in0, s_in1, qg], [po])
                update(1)
                def fin():
                    if d == 0:
                        self.copy("act", v3(o_)[:, :, i * 128:(i + 1) * 128], pov[0:P, :, :], [po], [o_])
                    else:
                        self.tt("dve", v3(o_)[:, :, i * 128:(i + 1) * 128], pov[0:P, :, :], v3(ofb[oi % 2])[:, :, i * 128:(i + 1) * 128], ALU.add,
                                [po, ofb[oi % 2]], [o_])
                return fin

            def finish(oi):
                c0, nt, n = geom(oi)
                o_ = ob[oi % 2]
                if d == 0:
                    self.dma(OFv[:, :, c0:c0 + n], v3(o_)[:, :, 0:n], reads=[o_], writes=[self.tkOF])
                    return
                y_ = yb[oi % 2]
                g_ = gbb[oi % 2]
                for hh in range(4):
                    self.act(o2[:, 0:n], v3(o_)[:, hh, 0:n], AF.Square, [o_], [o2])
                    pn = self.ps()
                    self.mm(pn[0:P, 0:n], [(self.ones[0:P, 0:P], o2[:, 0:n])], [self.cst, o2], [pn])
                    self.act(rs[:, 0:n], pn[0:P, 0:n], AF.Ln, [pn, self.epsb], [rs], scale=1.0 / 96.0, bias=self.epsb[0:P, :])
                    self.act(rs[:, 0:n], rs[:, 0:n], AF.Exp, [rs], [rs], scale=-0.5)
                    self.tt("dve", rs[:, 0:n], rs[:, 0:n], v3(o_)[:, hh, 0:n], ALU.mult, [rs, o_], [rs])
                    self.stt("dve", v3(y_)[:, hh, 0:n], rs[:, 0:n], prm[0:P, P_BNW + hh:P_BNW + hh + 1], v3(g_)[:, hh, 0:n], ALU.mult, ALU.mult,
                             [rs, prm] + RD(g_), [y_])
                self.dma(Yv[:, :, c0:c0 + n], v3(y_)[:, :, 0:n], reads=[y_], writes=[self.tkY])

            loadsA(0)
            loadsB(0)
            for f in pw_stages(0):
                f()
            transposes(0)
            loadsA(1)
            for oi in range(N):
                loadsA(oi + 2)
                loadsB(oi + 1)
                stages = pw_stages(oi + 1)
                c0, nt, n = geom(oi)
                tiles = list(range(nt)) if d == 0 else list(range(nt - 1, -1, -1))
                per = (len(stages) + nt - 1) // nt if stages else 0
                pend = None
                for ti, i in enumerate(tiles):
                    fin = core_tile(oi, i)
                    if pend is not None:
                        pend()
                    pend = fin
                    for f in stages[ti * per:(ti + 1) * per]:
                        f()
                pend()
                transposes(oi + 1)
                finish(oi)
        self.release(m_b)

    def phase_o(self, l):
        al = self.alloc
        S = self.S
        m_o = self.mark()
        last = (l == 1)
        self.wout = al(9 * D, BF16)
        self.woutv = self.wout.ap.rearrange("p (c n) -> p c n", n=D)
        if not last:
            self.woutc = al(9 * D, BF16)
            self.woutcv = self.woutc.ap.rearrange("p (c n) -> p c n", n=D)
        else:
            self.fnw = al(D)
            fn = self.fnw_d
            self.dma(self.fnw.ap, AP(fn.tensor, fn.offset, [[0, 128], [1, D]]), writes=[self.fnw])
        rows = [(j * 128, 128) for j in range(3)] + [(384 + h * 96, 96) for h in range(4)] + [(768 + j * 128, 128) for j in range(2)]
        self.wout_rows = rows
        m = self.mark()
        st = [al(D), al(D)]
        for i, (r0, n) in enumerate(rows):
            s = st[i % 2]
            self.dma(s[0:n, :], self.wout_d[l, r0:r0 + n, :], writes=[s])
            self.tt("dve", self.woutv[0:n, i, :], s[0:n, :], self.gb[l][0][0:n, :], ALU.mult, [s, self.gb[l][0]], [self.wout])
            if not last:
                self.tt("dve", self.woutcv[0:n, i, :], s[0:n, :], self.gb[l][1][0:n, :], ALU.mult, [s, self.gb[l][1]], [self.woutc])
        self.release(m)
        yA = [al(3 * 512, BF16) for _ in range(2)]
        yB = [al(4 * 512, BF16, parts=96) for _ in range(2)]
        yC = [al(2 * 512, BF16) for _ in range(2)]
        xt = [al(D) for _ in range(8)]
        xo = [al(D) for _ in range(2 if not last else 6)]
        wsteps = []
        if not last:
            self.alloc_win()
            wst = [al(NCOLS // 4) for _ in range(2)]
            wsteps = self.win_steps(1, wst, ["act"], nsplit=4)
        sq_junk = al(D, BF16)
        ssb = [al(1) for _ in range(4)]
        blocks = BLOCKS[1:] if last else BLOCKS
        cnt = {"x": 0}

        def src_rows(t0):
            if l == 0:
                return self.ctx_d[t0:t0 + 128, :] if t0 < TC else self.X0[t0 - TC:t0 - TC + 128, :]
            return self.X1[t0:t0 + 128, :]

        def load(oi):
            c0, nt = blocks[oi]
            n = nt * 128
            a_, b_, c_ = yA[oi % 2], yB[oi % 2], yC[oi % 2]
            self.dma(a_.ap.rearrange("p (j t) -> p j t", t=512)[:, :, 0:n], self.Y[0:384, :].rearrange("(j p) t -> p j t", p=128)[:, :, c0:c0 + n],
                     reads=[self.tkY], writes=[a_])
            self.dma(b_.ap.rearrange("p (j t) -> p j t", t=512)[:, :, 0:n], self.Y[384:768, :].rearrange("(j p) t -> p j t", p=96)[:, :, c0:c0 + n],
                     reads=[self.tkY], writes=[b_])
            self.dma(c_.ap.rearrange("p (j t) -> p j t", t=512)[:, :, 0:n], self.Y[768:1024, :].rearrange("(j p) t -> p j t", p=128)[:, :, c0:c0 + n],
                     reads=[self.tkY], writes=[c_])
            xs = []
            for i in range(nt):
                b = xt[cnt["x"] % 8]
                cnt["x"] += 1
                rd = [self.tkX1] if l > 0 else [self.tkX0]
                self.dma(b.ap, src_rows(c0 + i * 128), reads=rd, writes=[b])
                xs.append(b)
            return xs
        nxt = load(0)
        oc = 0
        for oi, (c0, nt) in enumerate(blocks):
            xs = nxt
            if oi + 1 < len(blocks):
                nxt = load(oi + 1)
            s_ctx = 1 if c0 < TC else 0
            a_, b_, c_ = yA[oi % 2], yB[oi % 2], yC[oi % 2]
            av = a_.ap.rearrange("p (j t) -> p j t", t=512)
            bv = b_.ap.rearrange("p (j t) -> p j t", t=512)
            cv = c_.ap.rearrange("p (j t) -> p j t", t=512)
            for i in range(nt):
                xb = xs[i]
                lhs = [(av[:, j, i * 128:(i + 1) * 128], 128) for j in range(3)] + [(bv[:, j, i * 128:(i + 1) * 128], 96) for j in range(4)] + \
                      [(cv[:, j, i * 128:(i + 1) * 128], 128) for j in range(2)]
                o_ = xo[oc % len(xo)]
                oc += 1
                if wsteps:
                    wsteps.pop(0)()
                wv, wb = (self.woutcv, self.woutc) if s_ctx else (self.woutv, self.wout)
                for nb in range(2):
                    pb = self.ps()
                    self.mm(pb.ap, [(lh, wv[0:kn, ci, nb * 512:(nb + 1) * 512]) for ci, (lh, kn) in enumerate(lhs)], [a_, b_, c_, wb], [pb])
                    self.tt("dve", o_[:, nb * 512:(nb + 1) * 512], pb.ap, xb[:, nb * 512:(nb + 1) * 512], ALU.add, [pb, xb], [o_])
                t0 = c0 + i * 128
                if not last:
                    self.dma(self.X1[t0:t0 + 128, :], o_.ap, reads=[o_], writes=[self.tkX1])
                else:
                    sb = ssb[oc % 4]
                    self.memset("pool", sb.ap, 0.0, [sb])
                    self.act(sq_junk.ap, o_.ap, AF.Square, [o_], [sq_junk, sb], accum_out=sb.ap)
                    self.act(sb.ap, sb.ap, AF.Sqrt, [sb, self.epsb], [sb], scale=1.0 / D, bias=self.epsb.ap)
                    S.op("dve", lambda e, sb=sb: e.reciprocal(out=sb.ap, in_=sb.ap), [sb], [sb])
                    self.stt("dve", o_.ap, o_.ap, sb.ap, self.fnw.ap, ALU.mult, ALU.mult, [o_, sb, self.fnw], [o_])
                    self.dma(self.out_d[t0 - TC:t0 - TC + 128, :], o_.ap, reads=[o_])
        while wsteps:
            wsteps.pop(0)()
        self.release(m_o)
        self.release(self.m_layer)


def _consts():
    c = np.zeros((128, NCST), np.float32)
    c[:, C_ID:C_ID + 128] = np.eye(128)
    s = np.arange(128)[:, None]
    t = np.arange(128)[None, :]
    same = (s // 64) == (t // 64)
    c[:, C_MF:C_MF + 128] = (same & (s <= t))
    c[:, C_MB:C_MB + 128] = (same & (s >= t))
    c[:, C_ONE:C_ONE + 128] = 1.0
    om = (1.0 / (10000.0 ** (np.arange(256, dtype=np.float64) / 256.0))).astype(np.float32)
    c[:, C_JROW:C_JROW + 256] = om
    c[:, C_JROW + 256:C_JROW + 512] = om
    c[:, C_PH:C_PH + 256] = 0.0
    c[:, C_PH + 256:C_PH + 512] = np.pi / 2
    p = np.arange(128)
    for tile in range(32):
        c[:, C_RCOL + tile] = 2 * tile + p // 64
    c[:, C_CCOL] = p % 64
    m = np.ones(2049, np.float32)
    m[::64] = 0.0
    c[:, C_SMASK:C_SMASK + 2049] = m
    return c


def _pack_params(inp):
    prm = np.zeros((2, 128, NPRM), np.float32)
    for l in range(2):
        prm[l, :, P_NORMW:P_NORMW + 8] = inp["norm_w"][l].reshape(8, 128).T
        cw = inp["a_conv_w"][l]
        prm[l, :, P_CONVW:P_CONVW + 12] = cw.reshape(4, 3, 128).transpose(2, 1, 0).reshape(128, 12)
        prm[l, :, P_CONVB:P_CONVB + 3] = inp["a_conv_b"][l].reshape(3, 128).T
        for name, col in (("a_br", P_BR), ("a_bi", P_BI), ("a_lambda", P_LAM)):
            v = inp[name][l]
            prm[l, :, col:col + 6] = v.reshape(2, 3, 128).transpose(2, 0, 1).reshape(128, 6)
        prm[l, :, P_CNW:P_CNW + 2] = inp["c_norm_w"][l].reshape(2, 128).T
        prm[l, 0:96, P_BNW:P_BNW + 4] = inp["b_norm_w"][l].reshape(4, 96).T
        lg = inp["b_lb_logits"]
        prm[l, 0:96, P_LBL:P_LBL + 12] = lg.reshape(3, 4, 96).transpose(2, 0, 1).reshape(96, 12)
    agw = np.zeros((2, 4, 384, 384), np.float32)
    for l in range(2):
        for d in range(2):
            for gi, name in enumerate(("a_wr", "a_wi")):
                w = inp[name][l, d]
                for h in range(8):
                    agw[l, 2 * d + gi, h * 48:(h + 1) * 48, h * 48:(h + 1) * 48] = w[h]
    cwsT = np.ascontiguousarray(inp["c_ws"].transpose(0, 3, 1, 2)).reshape(2, 128, 512)
    cb = inp["c_bs"]
    cbias = np.zeros((2, 128, 2, 128), np.float32)
    for j in range(2):
        for hl in range(2):
            cbias[:, hl * 64:(hl + 1) * 64, j, :] = cb[:, 2 * j + hl, None, :]
    return prm, agw, cwsT, cbias.reshape(2, 128, 256)


def _win_col_perm():
    perm = np.arange(NCOLS)
    for base in (768, 1152, 1536, 2304):
        for s3 in range(3):
            perm[base + s3 * 128:base + s3 * 128 + 96] = base + s3 * 96 + np.arange(96)
            perm[base + s3 * 128 + 96:base + (s3 + 1) * 128] = base + 288 + 32 * s3 + np.arange(32)
    return perm


def make_in_maps(inp, cores):
    inp = {k: np.ascontiguousarray(np.asarray(v, dtype=np.float32)) for k, v in inp.items()}
    inp["w_in"] = np.ascontiguousarray(inp["w_in"][:, :, _win_col_perm()])
    prm, agw, cwsT, cbias = _pack_params(inp)
    cst = _consts()
    maps = []
    for b in cores:
        cv = np.concatenate([inp["c"][b].reshape(8, 128).T, inp["c_ctx"].reshape(8, 128).T], axis=1)
        maps.append({
            "x": inp["x"][b], "ctx": inp["ctx"][b], "cvec": np.ascontiguousarray(cv),
            "w_mod": inp["w_mod"], "b_mod": inp["b_mod"], "w_in": inp["w_in"], "w_out": inp["w_out"],
            "prm": prm, "agw": agw, "cwsT": cwsT, "cbias": cbias, "fnw": inp["final_norm_w"], "cst": cst,
        })
    return maps


def kernel(**inputs):
    bld = Builder()
    nc = bld.build()
    maps = make_in_maps(inputs, list(range(8)))
    res = run_bass_kernel_spmd(nc, maps, core_ids=list(range(8)))
    return np.stack([np.asarray(r["out"], dtype=np.float32) for r in res.results], axis=0)
```

```python
import numpy as np
from contextlib import ExitStack
import concourse.bass as bass
import concourse.mybir as mybir
from concourse.ap import AP
from concourse.bass_utils import run_bass_kernel_spmd

F32 = mybir.dt.float32
BF16 = mybir.dt.bfloat16
I32 = mybir.dt.int32
AF = mybir.ActivationFunctionType
ALU = mybir.AluOpType

D = 1024
TC = 256
TL = 4096
T = TC + TL
NCOLS = 3456
EPS = 1e-6
PI = float(np.pi)
BLOCKS = [(0, 2)] + [(256 + 512 * i, 4) for i in range(8)]

P_NORMW, P_CONVW, P_CONVB, P_BR, P_BI, P_LAM, P_CNW, P_BNW, P_LBL = 0, 8, 20, 23, 29, 35, 41, 43, 47
NPRM = 64
C_ID, C_MF, C_MB, C_ONE, C_JROW, C_PH, C_RCOL, C_CCOL, C_SMASK = 0, 128, 256, 384, 512, 1024, 1536, 1568, 1569
NCST = C_SMASK + 2049


class Tk:
    __slots__ = ("w", "r", "multi", "ws")

    def __init__(self):
        self.w = None
        self.r = []
        self.multi = False
        self.ws = []


class Sched:
    ENGS = ("pe", "act", "dve", "pool", "sp")

    def __init__(self, nc):
        self.nc = nc
        self.ops = {e: [] for e in self.ENGS}
        self.semcount = {}
        self.known = {e: {} for e in self.ENGS}
        self.epoch = 0
        self.rr = 0
        self.rrq = {}

    def new_epoch(self):
        self.epoch += 1

    def _bump(self, key, inc):
        v = self.semcount.get(key, 0) + inc
        self.semcount[key] = v
        return (key, v)

    def _collect(self, eng, reads, writes):
        waits = {}
        kn = self.known[eng]

        def need(ev):
            if ev is None:
                return
            k, v = ev
            if kn.get(k, 0) >= v:
                return
            if waits.get(k, 0) < v:
                waits[k] = v
        for t in reads:
            if t.multi:
                for ev in t.ws:
                    need(ev)
            else:
                need(t.w)
        for t in writes:
            if not t.multi:
                need(t.w)
            for ev in t.r:
                need(ev)
        for k, v in waits.items():
            kn[k] = v
        return waits

    def _commit(self, ev, reads, writes):
        for t in reads:
            t.r.append(ev)
            if len(t.r) > 16:
                best = {}
                for k, v in t.r:
                    if best.get(k, 0) < v:
                        best[k] = v
                t.r = list(best.items())
        for t in writes:
            if t.multi:
                t.ws.append(ev)
                if len(t.ws) > 16:
                    best = {}
                    for k, v in t.ws:
                        if best.get(k, 0) < v:
                            best[k] = v
                    t.ws = list(best.items())
            else:
                t.w = ev
                t.r = []

    def op(self, eng, fn, reads=(), writes=()):
        reads = [b.tk for b in reads]
        writes = [b.tk for b in writes]
        waits = self._collect(eng, reads, writes)
        ev = self._bump(("E", eng, self.epoch), 1)
        self.ops[eng].append((list(waits.items()), fn, ev, 1))
        self._commit(ev, reads, writes)

    def dma(self, out, in_, reads=(), writes=(), eng="sp", key=None):
        reads = [b.tk for b in reads]
        writes = [b.tk for b in writes]
        if key is None:
            n = self.rrq.get(eng, 0)
            self.rrq[eng] = n + 1
            key = ("D", eng, n % (32 if eng == "sp" else 8))
        else:
            key = ("D", key)
        waits = self._collect(eng, reads, writes)
        prev = self.semcount.get(key, 0)
        if prev and self.known[eng].get(key, 0) < prev:
            waits[key] = max(waits.get(key, 0), prev)
            self.known[eng][key] = prev
        ev = self._bump(key, 16)

        def fn(e, out=out, in_=in_):
            return e.dma_start(out=out, in_=in_)
        self.ops[eng].append((list(waits.items()), fn, ev, 16))
        self._commit(ev, reads, writes)

    def barrier(self):
        evs = list(self.semcount.items())
        for e in self.ENGS:
            waits = []
            for k, v in evs:
                if self.known[e].get(k, 0) < v:
                    waits.append((k, v))
                    self.known[e][k] = v
            if waits:
                self.ops[e].append((waits, None, None, 0))

    def emit(self):
        nc = self.nc
        keys = list(self.semcount.keys())
        with ExitStack() as es:
            sems = {}
            for i, k in enumerate(keys):
                sems[k] = es.enter_context(nc.semaphore("s%d" % i))
            block = es.enter_context(nc.Block())

            def run(engname):
                def body(e):
                    for waits, fn, ev, inc in self.ops[engname]:
                        for k, v in waits:
                            e.wait_ge(sems[k], v)
                        if fn is not None:
                            fn(e).then_inc(sems[ev[0]], inc)
                return body
            block.tensor(run("pe"))
            block.scalar(run("act"))
            block.vector(run("dve"))
            block.gpsimd(run("pool"))
            block.sync(run("sp"))


class Buf:
    __slots__ = ("ap", "tk")

    def __init__(self, ap):
        self.ap = ap
        self.tk = Tk()

    def __getitem__(self, key):
        return self.ap[key]


def rev(ap):
    a = [list(x) for x in ap.ap]
    off = ap.offset + a[-1][0] * (a[-1][1] - 1)
    a[-1][0] = -a[-1][0]
    return AP(ap.tensor, off, a)


def bcast_mid(ap2, n):
    a = [list(x) for x in ap2.ap]
    return AP(ap2.tensor, ap2.offset, [a[0], [0, n], a[1]])


def bcast_last(ap2, n):
    a = [list(x) for x in ap2.ap]
    return AP(ap2.tensor, ap2.offset, [a[0], a[1], [0, n]])


class Builder:
    def __init__(self, debug=None, stop_after=None):
        self.debug = debug or ()
        self.stop_after = stop_after
        self.nc = nc = bass.Bass("TRN2", target_bir_lowering=False)
        self.S = Sched(nc)
        self.es = ExitStack()
        self.dbg_out = {}
        self.halt = False

    def dram_in(self, name, shape, dt=F32):
        return self.nc.dram_tensor(name, list(shape), dt, kind="ExternalInput").ap()

    def dram_scratch(self, name, shape, dt):
        kind = "ExternalOutput" if name in self.debug else "Internal"
        ap = self.nc.dram_tensor(name, list(shape), dt, kind=kind).ap()
        if name in self.debug:
            self.dbg_out[name] = ap
        return ap

    def setup_arena(self):
        nc = self.nc
        self.NW = 52500
        ar = self.es.enter_context(nc.sbuf_tensor("arena", [128, self.NW], F32))
        self.arena = ar
        self.top = 0
        self.top_hi = self.NW
        self.psum = []
        for i in range(8):
            p = self.es.enter_context(nc.psum_tensor("ps%d" % i, [128, 512], F32))
            self.psum.append(Buf(p[:]))
        self.ps_i = 0

    def alloc(self, nelem, dt=F32, parts=128):
        size = 4 if dt in (F32, I32) else 2
        words = (nelem * size + 3) // 4
        assert self.top + words <= self.top_hi, "SBUF arena overflow %d > %d" % (self.top + words, self.top_hi)
        v = self.arena[0:parts, self.top:self.top + words]
        self.top += words
        self.peak = max(getattr(self, "peak", 0), self.top)
        if dt != F32:
            v = v.bitcast(dt)
            v = v[:, 0:nelem]
        return Buf(v)

    def mark(self):
        return self.top

    def alloc_win(self):
        words = 8 * NCOLS // 2
        self.top_hi = self.NW - words
        assert self.top <= self.top_hi, "SBUF arena overflow (win)"
        v = self.arena[:, self.top_hi:self.NW].bitcast(BF16)
        self.win = Buf(v)
        self.winv = v.rearrange("p (k n) -> p k n", n=NCOLS)

    def win_steps(self, l, stage, engs, nsplit=2):
        steps = []
        H = NCOLS // nsplit
        for k in range(8):
            for hf in range(nsplit):
                idx = nsplit * k + hf

                def step(k=k, hf=hf, idx=idx):
                    sb = stage[idx % len(stage)]
                    self.dma(sb.ap, self.win_d[l, k * 128:(k + 1) * 128, hf * H:(hf + 1) * H], writes=[sb])
                    self.copy(engs[idx % len(engs)], self.winv[:, k, hf * H:(hf + 1) * H], sb.ap, [sb], [self.win])
                steps.append(step)
        return steps

    def release(self, m):
        self.S.barrier()
        self.peaks = getattr(self, "peaks", [])
        self.peaks.append((self.peak, m))
        self.peak = m
        self.top = m

    def ps(self):
        b = self.psum[self.ps_i % 8]
        self.ps_i += 1
        return b

    def act(self, out, in_, func, reads, writes, scale=1.0, bias=0.0, accum_out=None):
        kw = dict(out=out, in_=in_, func=func, scale=scale, bias=bias)
        if accum_out is not None:
            kw["accum_out"] = accum_out
        self.S.op("act", lambda e: e.activation(**kw), reads, writes)

    def tt(self, eng, out, in0, in1, op, reads, writes):
        self.S.op(eng, lambda e: e.tensor_tensor(out=out, in0=in0, in1=in1, op=op), reads, writes)

    def ts(self, eng, out, in0, s1, s2, op0, op1, reads, writes):
        if s2 is None:
            self.S.op(eng, lambda e: e.tensor_scalar(out=out, in0=in0, scalar1=s1, scalar2=None, op0=op0), reads, writes)
        else:
            self.S.op(eng, lambda e: e.tensor_scalar(out=out, in0=in0, scalar1=s1, scalar2=s2, op0=op0, op1=op1), reads, writes)

    def stt(self, eng, out, in0, scalar, in1, op0, op1, reads, writes):
        self.S.op(eng, lambda e: e.scalar_tensor_tensor(out=out, in0=in0, scalar=scalar, in1=in1, op0=op0, op1=op1), reads, writes)

    def copy(self, eng, out, in_, reads, writes):
        if eng == "act":
            self.act(out, in_, AF.Copy, reads, writes)
        else:
            self.S.op(eng, lambda e: e.tensor_copy(out=out, in_=in_), reads, writes)

    def memset(self, eng, out, val, writes):
        self.S.op(eng, lambda e: e.memset(out, val), (), writes)

    def mm(self, out, pairs, reads, writes, start=True, stop=True):
        def fn(e):
            n = len(pairs)
            ins = None
            for i, (l, r) in enumerate(pairs):
                ins = e.matmul(out, lhsT=l, rhs=r, start=(start and i == 0), stop=(stop and i == n - 1))
            return ins
        self.S.op("pe", fn, reads, writes)

    def dma(self, out, in_, reads=(), writes=(), key=None, eng="sp"):
        self.S.dma(out, in_, reads, writes, eng=eng, key=key)

    def build(self):
        nc = self.nc
        S = self.S
        di = self.dram_in
        self.x_d = di("x", [TL, D])
        self.ctx_d = di("ctx", [TC, D])
        self.cv_d = di("cvec", [128, 16])
        self.wmod_d = di("w_mod", [2, D, 3 * D])
        self.bmod_d = di("b_mod", [2, 3 * D])
        self.win_d = di("w_in", [2, D, NCOLS])
        self.wout_d = di("w_out", [2, D, D])
        self.prm_d = di("prm", [2, 128, NPRM])
        self.agw_d = di("agw", [2, 4, 384, 384])
        self.cws_d = di("cwsT", [2, 128, 512])
        self.cbias_d = di("cbias", [2, 128, 256])
        self.fnw_d = di("fnw", [D])
        self.cst_d = di("cst", [128, NCST])
        self.out_d = nc.dram_tensor("out", [TL, D], F32, kind="ExternalOutput").ap()
        ds = self.dram_scratch
        self.AX = ds("AX", [384, T], F32)
        self.SGA = ds("SGA", [384, T], BF16)
        self.Q = ds("Q", [384, T], BF16)
        self.SGF = ds("SGF", [384, T], F32)
        self.SGB = ds("SGB", [384, T], F32)
        self.GB = ds("GB", [384, T], BF16)
        self.VT = ds("VT", [T, 384], BF16)
        self.OF = ds("OF", [384, T], F32)
        self.Y = ds("Y", [D, T], BF16)
        self.X1 = ds("X1", [T, D], F32)
        self.X0 = ds("X0", [TL, D], F32)
        self.tkX0 = Buf(None)
        self.tkAX, self.tkSGA, self.tkQ, self.tkSGF, self.tkSGB, self.tkGB, self.tkVT, self.tkOF, self.tkY, self.tkX1 = [Buf(None) for _ in range(10)]
        for b in (self.tkAX, self.tkSGA, self.tkQ, self.tkSGF, self.tkSGB, self.tkGB, self.tkVT, self.tkOF, self.tkY, self.tkX1, self.tkX0):
            b.tk.multi = True
        self.setup_arena()
        self.phase_setup()
        for l in range(2):
            if self.stop_after == ("setup", l):
                break
            S.new_epoch()
            self.phase_weights(l)
            self.phase_p1(l)
            if self.stop_after == ("p1", l):
                break
            self.phase_a(l)
            if self.halt or self.stop_after == ("a", l):
                break
            self.phase_b(l)
            if self.halt or self.stop_after == ("b", l):
                break
            self.phase_o(l)
            if self.stop_after == ("o", l):
                break
        S.barrier()
        S.emit()
        self.es.close()
        return nc

    def phase_setup(self):
        S = self.S
        al = self.alloc
        self.cst = al(NCST)
        self.dma(self.cst.ap, self.cst_d, writes=[self.cst])
        self.identb = al(128, BF16)
        self.maskf = al(128, BF16)
        self.maskb = al(128, BF16)
        self.copy("dve", self.identb.ap, self.cst[:, C_ID:C_ID + 128], [self.cst], [self.identb])
        self.copy("dve", self.maskf.ap, self.cst[:, C_MF:C_MF + 128], [self.cst], [self.maskf])
        self.copy("dve", self.maskb.ap, self.cst[:, C_MB:C_MB + 128], [self.cst], [self.maskb])
        self.ones = self.cst[:, C_ONE:C_ONE + 128]
        self.smask = self.cst[0:96, C_SMASK:C_SMASK + 2049]
        self.prm = [al(NPRM), al(NPRM)]
        for l in range(2):
            self.dma(self.prm[l].ap, self.prm_d[l], writes=[self.prm[l]])
        self.negpi = al(1)
        self.memset("pool", self.negpi.ap, -PI, [self.negpi])
        self.epsb = al(1)
        self.memset("pool", self.epsb.ap, EPS, [self.epsb])
        self.pc = al(512)
        mpos = self.mark()
        self.omega = al(512)
        self.copy("dve", self.omega.ap, self.cst[:, C_JROW:C_JROW + 512], [self.cst], [self.omega])
        self._pt = (al(512), al(512, I32), al(512))
        self.pos_half_tile(self.pc, None, rcol_ap=self.cst[:, C_CCOL:C_CCOL + 1])
        self.E_d = self.dram_scratch("Etab", [64, 512], F32)
        self.tkE = Buf(None)
        self.dma(self.E_d, self.pc[0:64, :], reads=[self.pc], writes=[self.tkE])
        self.release(mpos)
        p0 = self.prm[0]
        lg = [p0[0:96, P_LBL + 4 * i:P_LBL + 4 * i + 4] for i in range(3)]
        mx = al(4)
        self.tt("dve", mx[0:96, :], lg[0], lg[1], ALU.max, [p0], [mx])
        self.tt("dve", mx[0:96, :], mx[0:96, :], lg[2], ALU.max, [p0, mx], [mx])
        ee = al(12)
        for i in range(3):
            self.tt("dve", ee[0:96, 4 * i:4 * i + 4], lg[i], mx[0:96, :], ALU.subtract, [p0, mx], [ee])
        self.act(ee[0:96, :], ee[0:96, :], AF.Exp, [ee], [ee])
        ssum = al(4)
        self.tt("dve", ssum[0:96, :], ee[0:96, 0:4], ee[0:96, 4:8], ALU.add, [ee], [ssum])
        e01 = al(4)
        self.copy("dve", e01[0:96, :], ssum[0:96, :], [ssum], [e01])
        self.tt("dve", ssum[0:96, :], ssum[0:96, :], ee[0:96, 8:12], ALU.add, [ee, ssum], [ssum])
        S.op("dve", lambda e: e.reciprocal(out=ssum[0:96, :], in_=ssum[0:96, :]), [ssum], [ssum])
        self.lb = [al(4), al(4)]
        self.oml = [al(4), al(4)]
        self.noml = [al(4), al(4)]
        self.tt("dve", self.lb[0][0:96, :], ee[0:96, 0:4], ssum[0:96, :], ALU.mult, [ee, ssum], [self.lb[0]])
        self.tt("dve", self.lb[1][0:96, :], e01[0:96, :], ssum[0:96, :], ALU.mult, [e01, ssum], [self.lb[1]])
        for l in range(2):
            self.ts("dve", self.oml[l][0:96, :], self.lb[l][0:96, :], -1.0, 1.0, ALU.mult, ALU.add, [self.lb[l]], [self.oml[l]])
            self.ts("dve", self.noml[l][0:96, :], self.lb[l][0:96, :], 1.0, -1.0, ALU.mult, ALU.add, [self.lb[l]], [self.noml[l]])
        cv = al(16)
        self.dma(cv.ap, self.cv_d, writes=[cv])
        self.act(cv.ap, cv.ap, AF.Silu, [cv], [cv])
        crep = al(8 * 128)
        crv = crep.ap.rearrange("p (k m) -> p k m", m=128)
        self.copy("dve", crv[:, :, 0:64], bcast_last(cv[:, 0:8], 64), [cv], [crep])
        self.copy("dve", crv[:, :, 64:128], bcast_last(cv[:, 8:16], 64), [cv], [crep])
        self.scw = [[None, None], [None, None]]
        self.shT = [[None, None], [None, None]]
        self.gb = [[None, None], [None, None]]
        for l in range(2):
            for s in range(2):
                self.scw[l][s] = al(8)
                self.shT[l][s] = al(8)
                self.gb[l][s] = al(D) if (l, s) != (1, 1) else None
        m0 = self.mark()
        self.alloc_win()
        wst = [al(NCOLS // 2) for _ in range(3)]
        wsteps0 = self.win_steps(0, wst, ["act", "dve"])
        bm = al(3 * D, parts=1)
        stage = [al(3 * D) for _ in range(3)]
        modsb = al(3 * D)
        for l in range(2):
            bmv = self.bmod_d[l]
            self.dma(bm.ap, AP(bmv.tensor, bmv.offset, [[0, 1], [1, 3 * D]]), writes=[bm])
            banks = [self.ps() for _ in range(6)]
            for k in range(8):
                st = stage[k % 3]
                self.dma(st.ap, self.wmod_d[l, k * 128:(k + 1) * 128, :], writes=[st])
                for n in range(6):
                    self.mm(banks[n].ap, [(crv[:, k, :], st[:, n * 512:(n + 1) * 512])], [crep, st], [banks[n]],
                            start=(k == 0), stop=False)
                for _ in range(2):
                    if wsteps0:
                        wsteps0.pop(0)()
            for n in range(6):
                self.mm(banks[n].ap, [(self.ones[0:1, :], bm[0:1, n * 512:(n + 1) * 512])], [self.cst, bm], [banks[n]],
                        start=False, stop=True)
                self.copy("act" if n % 2 else "dve", modsb[:, n * 512:(n + 1) * 512], banks[n].ap, [banks[n]], [modsb])
            for s in range(2):
                r0 = 64 * s
                for n in range(2):
                    if self.gb[l][s] is None:
                        continue
                    pb = self.ps()
                    self.mm(pb.ap, [(self.ones[r0:r0 + 1, :], modsb[r0:r0 + 1, 2 * D + n * 512:2 * D + (n + 1) * 512])],
                            [self.cst, modsb], [pb])
                    self.copy("dve", self.gb[l][s][:, n * 512:(n + 1) * 512], pb.ap, [pb], [self.gb[l][s]])
                pt = self.ps()
                for k in range(8):
                    self.mm(pt[:, k:k + 1], [(modsb[r0:r0 + 1, D + k * 128:D + (k + 1) * 128], self.ones[r0:r0 + 1, 0:1])],
                            [modsb, self.cst], [pt])
                    self.mm(pt[:, 8 + k:9 + k], [(modsb[r0:r0 + 1, k * 128:(k + 1) * 128], self.ones[r0:r0 + 1, 0:1])],
                            [modsb, self.cst], [pt])
                self.stt("dve", self.scw[l][s].ap, pt[:, 0:8], 1.0, self.prm[l][:, P_NORMW:P_NORMW + 8], ALU.add, ALU.mult,
                         [pt, self.prm[l]], [self.scw[l][s]])
                self.copy("dve", self.shT[l][s].ap, pt[:, 8:16], [pt], [self.shT[l][s]])
        while wsteps0:
            wsteps0.pop(0)()
        self.release(m0)

    def phase_weights(self, l):
        al = self.alloc
        self.m_layer = self.mark()
        self.cws = al(512, BF16)
        self.cbias = al(256)
        self.dma(self.cbias.ap, self.cbias_d[l], writes=[self.cbias])
        self.halfc = al(6)
        prm = self.prm[l]
        self.act(self.halfc.ap, prm[:, P_LAM:P_LAM + 6], AF.Exp, [prm], [self.halfc], scale=-1.0)
        self.act(self.halfc.ap, self.halfc.ap, AF.Ln, [self.halfc], [self.halfc], bias=1.0)
        self.ts("dve", self.halfc.ap, self.halfc.ap, -4.0, None, ALU.mult, None, [self.halfc], [self.halfc])
        self.ahalf = al(18)
        self.ts("dve", self.ahalf[:, 0:6], self.halfc.ap, 0.5, None, ALU.mult, None, [self.halfc], [self.ahalf])
        self.ts("dve", self.ahalf[:, 6:12], prm[:, P_BR:P_BR + 6], 0.5, None, ALU.mult, None, [prm], [self.ahalf])
        self.ts("dve", self.ahalf[:, 12:18], prm[:, P_BI:P_BI + 6], 0.5, None, ALU.mult, None, [prm], [self.ahalf])

    def phase_p1(self, l):
        al = self.alloc
        S = self.S
        m_p1 = self.mark()
        prm = self.prm[l]
        win, winv = self.win, self.winv
        cst_ = al(512)
        self.dma(cst_.ap, self.cws_d[l], writes=[cst_])
        self.copy("dve", self.cws.ap, cst_.ap, [cst_], [self.cws])
        xt = [al(D) for _ in range(8)]
        xn = [al(D, BF16) for _ in range(4)]
        hT = [al(8 * 512, BF16) for _ in range(2)]
        ss = [al(4) for _ in range(2)]
        rstd = [al(4) for _ in range(2)]
        prt = [al(512) for _ in range(4 if l == 0 else 0)]
        sq_junk = al(D, BF16)
        NF32, NBF = (5, 9) if l == 0 else (8, 12)
        f32t = [al(512) for _ in range(NF32)]
        bft = [al(512, BF16) for _ in range(NBF)]
        vtt = [al(384, BF16) for _ in range(2)]
        cvt = [al(256) for _ in range(4)]
        vhat = [al(256, BF16) for _ in range(4)]
        stats = al(4 * 6)
        mv = al(8)
        cu = [al(1024, BF16) for _ in range(1)]
        cg = [al(1024, BF16) for _ in range(1)]
        ug = [al(1024, BF16) for _ in range(1)]
        ycb = [al(1024, BF16) for _ in range(2)]
        t1 = [al(128) for _ in range(2)]
        cnt = {"f": 0, "b": 0, "v": 0}
        NBLK = len(BLOCKS)

        def src_rows(t0):
            if l == 0:
                return self.ctx_d[t0:t0 + 128, :] if t0 < TC else self.x_d[t0 - TC:t0 - TC + 128, :]
            return self.X1[t0:t0 + 128, :]

        def xbuf(bi, i):
            return xt[(bi % 2) * 4 + i]

        def loads(bi):
            if bi >= NBLK:
                return
            c0, nt = BLOCKS[bi]
            for i in range(nt):
                rd = [self.tkX1] if l > 0 else []
                self.dma(xbuf(bi, i).ap, src_rows(c0 + i * 128), reads=rd, writes=[xbuf(bi, i)])
                if l == 0 and c0 >= TC:
                    tglob = (c0 - TC) // 128 + i
                    pr = prt[i]
                    for half in range(2):
                        src = AP(self.E_d.tensor, self.E_d.offset + (2 * tglob + half) * 512, [[0, 64], [1, 512]])
                        self.dma(pr[half * 64:(half + 1) * 64, :], src, reads=[self.tkE], writes=[pr])

        def part0(bi):
            if bi >= NBLK or l != 0:
                return
            c0, nt = BLOCKS[bi]
            if c0 < TC:
                return
            for i in range(nt):
                xb = xbuf(bi, i)
                pr = prt[i]
                self.tt("dve", xb[:, 0:512], xb[:, 0:512], pr.ap, ALU.add, [xb, pr], [xb])
                self.tt("dve", xb[:, 512:1024], xb[:, 512:1024], self.pc.ap, ALU.add, [xb, self.pc], [xb])
                self.dma(self.X0[c0 - TC + i * 128:c0 - TC + (i + 1) * 128, :], xb.ap, reads=[xb], writes=[self.tkX0])

        def part1(bi):
            if bi >= NBLK:
                return
            c0, nt = BLOCKS[bi]
            sb, rb = ss[bi % 2], rstd[bi % 2]
            self.memset("pool", sb.ap, 0.0, [sb])
            for i in range(nt):
                xb = xbuf(bi, i)
                self.act(sq_junk.ap, xb.ap, AF.Square, [xb], [sq_junk, sb], accum_out=sb[:, i:i + 1])
            self.act(rb[:, 0:nt], sb[:, 0:nt], AF.Sqrt, [sb, self.epsb], [rb], scale=1.0 / D, bias=self.epsb.ap)
            S.op("dve", lambda e, rb=rb, nt=nt: e.reciprocal(out=rb[:, 0:nt], in_=rb[:, 0:nt]), [rb], [rb])
            for i in range(nt):
                self.ts("dve", xn[i].ap, xbuf(bi, i).ap, rb[:, i:i + 1], None, ALU.mult, None, [xbuf(bi, i), rb], [xn[i]])

        def part2(bi):
            if bi >= NBLK:
                return
            c0, nt = BLOCKS[bi]
            s_ctx = 1 if c0 < TC else 0
            h = hT[bi % 2]
            hv = h.ap.rearrange("p (k t) -> p k t", t=512)
            scw, shT = self.scw[l][s_ctx], self.shT[l][s_ctx]
            for i in range(nt):
                xnb = xn[i]
                pb = self.ps()
                pbv = pb.ap.bitcast(BF16)
                S.op("pe", lambda e, pbv=pbv, xnb=xnb: [e.transpose(out=pbv[:, k * 128:(k + 1) * 128], in_=xnb[:, k * 128:(k + 1) * 128],
                                                                       identity=self.identb.ap) for k in range(8)][-1],
                     [xnb, self.identb], [pb])
                pv = pbv.rearrange("p (k t) -> p k t", t=128)
                for k in range(8):
                    if k % 2 == 0:
                        self.ts("dve", hv[:, k, i * 128:(i + 1) * 128], pv[:, k, :], scw[:, k:k + 1], shT[:, k:k + 1], ALU.mult, ALU.add,
                                [pb, scw, shT], [h])
                    else:
                        self.act(hv[:, k, i * 128:(i + 1) * 128], pv[:, k, :], AF.Identity, [pb, scw, shT], [h],
                                 scale=scw[:, k:k + 1], bias=shT[:, k:k + 1])
            if "hT" in self.debug and bi == 1 and l == 0:
                d = self.dram_scratch("hT", [128, 8 * 512], BF16)
                self.dma(d, h.ap, reads=[h])

        def f32tile():
            b = f32t[cnt["f"] % NF32]
            cnt["f"] += 1
            return b

        def bftile():
            b = bft[cnt["b"] % NBF]
            cnt["b"] += 1
            return b

        def inproj(bi, mid_hook):
            c0, nt = BLOCKS[bi]
            ncol = nt * 128
            h = hT[bi % 2]
            hv = h.ap.rearrange("p (k t) -> p k t", t=512)
            cub, cgb, ugb, ycbb = cu[0], cg[0], ug[0], ycb[bi % 2]

            def proj(col0, M):
                pb = self.ps()
                self.mm(pb[0:M, 0:ncol], [(winv[:, k, col0:col0 + M], hv[:, k, 0:ncol]) for k in range(8)], [win, h], [pb])
                return pb
            for j in range(3):
                pb = proj(384 + j * 128, 128)
                o = bftile()
                self.act(o[:, 0:ncol], pb[:, 0:ncol], AF.Silu, [pb], [o])
                self.dma(self.SGA[j * 128:(j + 1) * 128, c0:c0 + ncol], o[:, 0:ncol], reads=[o], writes=[self.tkSGA], eng="act")
            def bsplit(colb, dst, tk, func, tile_fn):
                for s3 in range(3):
                    pb = proj(colb + s3 * 128, 128)
                    o = tile_fn()
                    self.act(o[:, 0:ncol], pb[:, 0:ncol], func, [pb], [o])
                    self.dma(dst[s3 * 128:(s3 + 1) * 128, c0:c0 + ncol], o[:, 0:ncol], reads=[o], writes=[tk], eng="act")
            bsplit(768, self.Q, self.tkQ, AF.Silu, bftile)
            bsplit(2304, self.GB, self.tkGB, AF.Silu, bftile)
            for j in range(2):
                pb = proj(3200 + j * 128, 128)
                self.act(cgb[:, j * 512:j * 512 + ncol], pb[:, 0:ncol], AF.Silu, [pb], [cgb])
            mid_hook()
            bsplit(1152, self.SGF, self.tkSGF, AF.Sigmoid, f32tile)
            bsplit(1536, self.SGB, self.tkSGB, AF.Sigmoid, f32tile)
            for j in range(2):
                pb = proj(2688 + j * 128, 128)
                self.act(cub[:, j * 512:j * 512 + ncol], pb[:, 0:ncol], AF.Gelu_apprx_tanh, [pb], [cub])
            w3 = lambda b: b.ap.rearrange("p (j t) -> p j t", t=512)[:, :, 0:ncol]
            self.tt("dve", w3(ugb), w3(cub), w3(cgb), ALU.mult, [cub, cgb], [ugb])
            stv = stats.ap.rearrange("p (i s) -> p i s", s=6)
            mvv = mv.ap.rearrange("p (i s) -> p i s", s=2)
            for i in range(nt):
                hs = [hv[:, k, i * 128:(i + 1) * 128] for k in range(8)]
                pb = self.ps()
                self.mm(pb[:, 0:256], [(hs[k], winv[:, k, 2944:3200]) for k in range(8)], [win, h], [pb])
                cvb = cvt[i]
                self.act(cvb.ap, pb[:, 0:256], AF.Gelu_apprx_tanh, [pb], [cvb])
                S.op("dve", lambda e, i=i, cvb=cvb: e.bn_stats(out=stv[:, i, :], in_=cvb.ap), [cvb], [stats])
                S.op("dve", lambda e, i=i: e.bn_aggr(out=mvv[:, i, :], in_=stv[:, i, :]), [stats], [mv])
            for i in range(nt):
                hs = [hv[:, k, i * 128:(i + 1) * 128] for k in range(8)]
                pb = self.ps()
                self.mm(pb[:, 0:384], [(hs[k], winv[:, k, 1920:2304]) for k in range(8)], [win, h], [pb])
                vb = vtt[cnt["v"] % 2]
                cnt["v"] += 1
                self.copy("dve", vb.ap, pb[:, 0:384], [pb], [vb])
                self.dma(self.VT[c0 + i * 128:c0 + (i + 1) * 128, :], vb.ap, reads=[vb], writes=[self.tkVT])
            self.act(mvv[:, 0:nt, 1], mvv[:, 0:nt, 1], AF.Sqrt, [mv, self.epsb], [mv], bias=self.epsb.ap)
            S.op("dve", lambda e: e.reciprocal(out=mvv[:, 0:nt, 1], in_=mvv[:, 0:nt, 1]), [mv], [mv])
            for j in range(3):
                pb = proj(j * 128, 128)
                o = f32tile()
                self.act(o[:, 0:ncol], pb[:, 0:ncol], AF.Copy, [pb], [o])
                self.dma(self.AX[j * 128:(j + 1) * 128, c0:c0 + ncol], o[:, 0:ncol], reads=[o], writes=[self.tkAX], eng="act")
            cwv = self.cws.ap.rearrange("p (h s) -> p h s", s=128)
            for i in range(nt):
                self.ts("dve", vhat[i].ap, cvt[i].ap, mvv[:, i, 0:1], mvv[:, i, 1:2], ALU.subtract, ALU.mult, [cvt[i], mv], [vhat[i]])
            pcs = []
            for i in range(nt):
                vh = vhat[i]
                pc_ = self.ps()
                pcv = pc_.ap.rearrange("p (j s) -> p j s", s=256)

                def cmix(e, pcv=pcv, vh=vh):
                    ins = None
                    for j in range(2):
                        for hl in range(2):
                            hh = 2 * j + hl
                            ins = e.matmul(pcv[hl * 64:(hl + 1) * 64, j, 0:128], lhsT=vh[:, hh * 64:(hh + 1) * 64], rhs=cwv[:, hh, :],
                                           start=True, stop=True)
                    return ins
                S.op("pe", cmix, [vh, self.cws], [pc_])
                pcs.append((pc_, pcv))
            for i in range(nt):
                pc_, pcv = pcs[i]
                for j in range(2):
                    tb = t1[j]
                    self.stt("dve", tb.ap, pcv[:, j, 0:128], prm[:, P_CNW + j:P_CNW + j + 1], self.cbias[:, j * 128:(j + 1) * 128],
                             ALU.mult, ALU.add, [pc_, prm, self.cbias], [tb])
                    self.tt("dve", ycbb[:, j * 512 + i * 128:j * 512 + (i + 1) * 128], tb.ap,
                            ugb[:, j * 512 + i * 128:j * 512 + (i + 1) * 128], ALU.mult, [tb, ugb], [ycbb])
            for j in range(2):
                self.dma(self.Y[768 + j * 128:768 + (j + 1) * 128, c0:c0 + ncol], ycbb[:, j * 512:j * 512 + ncol], reads=[ycbb], writes=[self.tkY])

        loads(0)
        part0(0)
        part1(0)
        loads(1)
        part2(0)
        part0(1)
        for bi in range(NBLK):
            part1(bi + 1)
            loads(bi + 2)
            inproj(bi, lambda bi=bi: part2(bi + 1))
            part0(bi + 2)
        self.release(m_p1)
        self.top_hi = self.NW

    def pos_half_tile(self, dst, tglob, rcol_ap=None):
        arg, ni, nf = self._pt
        if rcol_ap is None:
            rcol_ap = self.cst[:, C_RCOL + tglob:C_RCOL + tglob + 1]
        self.stt("dve", arg.ap, self.omega.ap, rcol_ap, self.cst[:, C_PH:C_PH + 512],
                 ALU.mult, ALU.add, [self.omega, self.cst], [arg])
        self.ts("dve", ni.ap, arg.ap, 1.0 / (2 * PI), None, ALU.mult, None, [arg], [ni])
        self.copy("dve", nf.ap, ni.ap, [ni], [nf])
        self.stt("dve", arg.ap, nf.ap, -2 * PI, arg.ap, ALU.mult, ALU.add, [nf, arg], [arg])
        self.act(dst.ap, arg.ap, AF.Sin, [arg], [dst])

    def phase_a(self, l):
        al = self.alloc
        S = self.S
        m_a = self.mark()
        prm = self.prm[l]
        NP = 2 + TC + 1 + 2 + TL + 1
        xcb = al(3 * T, BF16)
        xcbv = xcb.ap.rearrange("p (j t) -> p j t", t=T)
        XC = self.dram_scratch("XC%d" % l, [384, T], F32)
        tkXC = Buf(None)
        tkXC.tk.multi = True
        self.agw = al(4 * 3 * 384, BF16)
        self.agwv = self.agw.ap.rearrange("p (g k n) -> p g k n", g=4, k=3)
        m0 = self.mark()
        agst = [al(3 * 384), al(3 * 384)]
        for g in range(4):
            sg_ = agst[g % 2]
            self.dma(sg_.ap.rearrange("p (k n) -> p k n", k=3), self.agw_d[l, g].rearrange("(k p) n -> p k n", p=128), writes=[sg_])
            self.copy("act", self.agwv[:, g, :, :], sg_.ap.rearrange("p (k n) -> p k n", k=3), [sg_], [self.agw])
        axp = [al(NP), al(NP)]
        xcj0 = [al(T), al(T)]
        for b in axp:
            self.memset("pool", b[:, 0:2], 0.0, [b])
            self.memset("pool", b[:, 2 + TC:2 + TC + 3], 0.0, [b])
            self.memset("pool", b[:, NP - 1:NP], 0.0, [b])
        for j in range(3):
            ax, xj = axp[j % 2], xcj0[j % 2]
            self.dma(ax[:, 2:2 + TC], self.AX[j * 128:(j + 1) * 128, 0:TC], reads=[self.tkAX], writes=[ax])
            self.dma(ax[:, 5 + TC:5 + TC + TL], self.AX[j * 128:(j + 1) * 128, TC:T], reads=[self.tkAX], writes=[ax])
            eng = "dve"
            cw = [prm[:, P_CONVW + 4 * j + k:P_CONVW + 4 * j + k + 1] for k in range(4)]
            for (o0, n, b0) in ((0, TC, 0), (TC, TL, 3 + TC)):
                self.ts(eng, xj[:, o0:o0 + n], ax[:, b0:b0 + n], cw[0], prm[:, P_CONVB + j:P_CONVB + j + 1], ALU.mult, ALU.add,
                        [ax, prm], [xj])
                for k in range(1, 4):
                    self.stt(eng, xj[:, o0:o0 + n], ax[:, b0 + k:b0 + k + n], cw[k], xj[:, o0:o0 + n], ALU.mult, ALU.add,
                             [ax, prm, xj], [xj])
            self.copy("act", xcbv[:, j, :], xj.ap, [xj], [xcb])
            self.dma(XC[j * 128:(j + 1) * 128, :], xj.ap, reads=[xj], writes=[tkXC])
        self.release(m0)
        if self.stop_after == ("a0", l):
            self.halt = True
            return
        xcj = al(T)
        a_row = al(T)
        nw_row = al(T)
        ix_row = al(T)
        hf_j = al(T)
        sga_row = al(T, BF16)
        y_row = a_row.ap.bitcast(BF16)[:, 0:T]
        NR = 5
        rt = [al(512) for _ in range(NR)]
        it = [al(512) for _ in range(NR)]
        tt_ = [al(512) for _ in range(NR)]
        dt_ = [al(512) for _ in range(NR)]
        kch = {0: (0, 1), 1: (0, 1, 2), 2: (1, 2)}
        itn = 0
        for j in range(3):
            self.dma(xcj.ap, XC[j * 128:(j + 1) * 128, :], reads=[tkXC], writes=[xcj])
            self.dma(sga_row.ap, self.SGA[j * 128:(j + 1) * 128, :], reads=[self.tkSGA], writes=[sga_row])
            for d in range(2):
                for (c0, nt) in BLOCKS:
                    n = nt * 128
                    q = itn % NR
                    itn += 1
                    r_, i_, t_, d_ = rt[q], it[q], tt_[q], dt_[q]
                    pr, pi = self.ps(), self.ps()
                    self.mm(pr[:, 0:n], [(self.agwv[:, 2 * d, k, j * 128:(j + 1) * 128], xcbv[:, k, c0:c0 + n]) for k in kch[j]], [self.agw, xcb], [pr])
                    self.mm(pi[:, 0:n], [(self.agwv[:, 2 * d + 1, k, j * 128:(j + 1) * 128], xcbv[:, k, c0:c0 + n]) for k in kch[j]], [self.agw, xcb], [pi])
                    cidx = 3 * d + j
                    ah = self.ahalf
                    self.act(r_[:, 0:n], pr[:, 0:n], AF.Tanh, [pr, ah], [r_], scale=0.5, bias=ah[:, 6 + cidx:7 + cidx])
                    self.act(i_[:, 0:n], pi[:, 0:n], AF.Tanh, [pi, ah], [i_], scale=0.5, bias=ah[:, 12 + cidx:13 + cidx])
                    self.act(t_[:, 0:n], r_[:, 0:n], AF.Tanh, [r_, ah], [t_], scale=ah[:, cidx:cidx + 1], bias=ah[:, cidx:cidx + 1])
                    self.act(d_[:, 0:n], r_[:, 0:n], AF.Exp, [r_, self.halfc], [d_], scale=self.halfc[:, cidx:cidx + 1],
                             bias=self.halfc[:, cidx:cidx + 1])
                    self.stt("dve", d_[:, 0:n], d_[:, 0:n], 1.0, t_[:, 0:n], ALU.add, ALU.mult, [d_, t_], [d_])
                    self.ts("dve", a_row[:, c0:c0 + n], d_[:, 0:n], 1.0, None, ALU.add, None, [d_], [a_row])
                    self.stt("dve", ix_row[:, c0:c0 + n], i_[:, 0:n], 1.0, xcj[:, c0:c0 + n], ALU.add, ALU.mult, [i_, xcj], [ix_row])
                if self.stop_after == ("a1", l):
                    self.halt = True
                    return
                self.act(nw_row.ap, a_row.ap, AF.Square, [a_row], [nw_row])
                self.act(nw_row.ap, nw_row.ap, AF.Sqrt, [nw_row], [nw_row], scale=-0.25, bias=0.25)
                self.tt("dve", ix_row.ap, ix_row.ap, nw_row.ap, ALU.mult, [ix_row, nw_row], [ix_row])
                if self.stop_after == ("a2", l):
                    self.halt = True
                    return
                if d == 0:
                    S.op("dve", lambda e: e.tensor_tensor_scan(out=hf_j.ap, data0=a_row.ap, data1=ix_row.ap, initial=0.0,
                                                               op0=ALU.mult, op1=ALU.add), [a_row, ix_row], [hf_j])
                    if self.stop_after == ("a3", l):
                        self.halt = True
                        return
                else:
                    S.op("dve", lambda e: e.tensor_tensor_scan(out=rev(nw_row[:, 0:TC]), data0=rev(a_row[:, 0:TC]), data1=rev(ix_row[:, 0:TC]),
                                                               initial=0.0, op0=ALU.mult, op1=ALU.add), [a_row, ix_row], [nw_row])
                    S.op("dve", lambda e: e.tensor_tensor_scan(out=rev(nw_row[:, TC:T]), data0=rev(a_row[:, TC:T]), data1=rev(ix_row[:, TC:T]),
                                                               initial=nw_row[:, 0:1], op0=ALU.mult, op1=ALU.add), [a_row, ix_row, nw_row], [nw_row])
                    if self.stop_after == ("a4", l):
                        self.halt = True
                        return
                    self.tt("dve", nw_row.ap, nw_row.ap, hf_j.ap, ALU.add, [nw_row, hf_j], [nw_row])
                    self.tt("dve", y_row, nw_row.ap, sga_row.ap, ALU.mult, [nw_row, sga_row, a_row], [a_row])
                    if self.stop_after == ("a5", l):
                        self.halt = True
                        return
                    self.dma(self.Y[j * 128:(j + 1) * 128, :], y_row, reads=[a_row], writes=[self.tkY])
                    if self.stop_after == ("a6", l):
                        self.halt = True
                        return
                if self.stop_after == ("aj%dd%d" % (j, d), l):
                    self.halt = True
                    return
        self.release(m_a)

    def phase_b(self, l):
        al = self.alloc
        S = self.S
        m_b = self.mark()
        prm = self.prm[l]
        lb, oml, noml = self.lb[l], self.oml[l], self.noml[l]
        P = 96
        W = 2048
        Sst = al(384, parts=P)
        Sbf2 = [al(384, BF16, parts=P) for _ in range(3)]
        st = {"cur": 0}
        qb = [al(W, BF16, parts=P) for _ in range(2)]
        sgb = [al(W, parts=P) for _ in range(2)]
        vtb = [al(4 * 384, BF16) for _ in range(2)]
        ofb = [al(W, parts=P) for _ in range(2)]
        gbb = [al(W, BF16, parts=P) for _ in range(2)]
        lf = al(W, parts=P)
        G = al(W, parts=P)
        eG = al(W, BF16, parts=P)
        enG = al(W, BF16, parts=P)
        kk = al(W, BF16, parts=P)
        kgs = [al(W, BF16, parts=P) for _ in range(2)]
        qgs = [al(W, BF16, parts=P) for _ in range(2)]
        decs = [al(32, parts=P) for _ in range(2)]
        kets = [al(4 * 384, BF16) for _ in range(2)]
        scm = [al(512, BF16) for _ in range(2)]
        ob = [al(W, parts=P) for _ in range(2)]
        o2 = al(512, parts=P)
        rs = al(512, parts=P)
        yb = [al(W, BF16, parts=P) for _ in range(2)]
        tmpS = al(384, parts=P)
        xtk = {id(b): [Buf(None) for _ in range(3)] for b in qb + sgb + gbb}
        Qv = self.Q.rearrange("(h d) t -> d h t", d=96)
        GBv = self.GB.rearrange("(h d) t -> d h t", d=96)
        OFv = self.OF.rearrange("(h d) t -> d h t", d=96)
        Yv = self.Y[384:768, :].rearrange("(h d) t -> d h t", d=96)
        v3 = lambda b: b.ap.rearrange("p (h t) -> p h t", t=512)
        S3 = Sst.ap.rearrange("p (h v) -> p h v", v=96)
        T3 = tmpS.ap.rearrange("p (h v) -> p h v", v=96)
        for d in range(2):
            SG = self.SGF if d == 0 else self.SGB
            tkSG = self.tkSGF if d == 0 else self.tkSGB
            SGv = SG.rearrange("(h d) t -> d h t", d=96)
            order = list(range(len(BLOCKS))) if d == 0 else [0] + list(range(len(BLOCKS) - 1, 0, -1))
            N = len(order)
            mask = self.maskf if d == 0 else self.maskb
            self.memset("pool", Sst.ap, 0.0, [Sst])
            self.memset("pool", Sbf2[0].ap, 0.0, [Sbf2[0]])
            self.memset("pool", Sbf2[1].ap, 0.0, [Sbf2[1]])
            self.memset("pool", Sbf2[2].ap, 0.0, [Sbf2[2]])

            def geom(oi):
                c0, nt = BLOCKS[order[oi]]
                return c0, nt, nt * 128

            def load_heads(buf, src, c0, n, tk):
                comp = src.rearrange("(s p) t -> p s t", p=128)
                self.dma(v3(buf)[:, 0:3, 0:n], comp[0:96, :, c0:c0 + n], reads=[tk], writes=[buf])
                for s3 in range(3):
                    self.dma(v3(buf)[32 * s3:32 * (s3 + 1), 3, 0:n], comp[96:128, s3, c0:c0 + n], reads=[tk], writes=[xtk[id(buf)][s3]])

            def RD(buf):
                return [buf] + xtk[id(buf)]

            def loadsA(oi):
                if oi >= N:
                    return
                c0, nt, n = geom(oi)
                q_, s_ = qb[oi % 2], sgb[oi % 2]
                load_heads(q_, self.Q, c0, n, self.tkQ)
                load_heads(s_, SG, c0, n, tkSG)

            def loadsB(oi):
                if oi >= N:
                    return
                c0, nt, n = geom(oi)
                v_ = vtb[oi % 2]
                self.dma(v_.ap.rearrange("p (n c) -> p n c", c=384)[:, 0:nt, :],
                         self.VT[c0:c0 + n, :].rearrange("(n p) c -> p n c", p=128), reads=[self.tkVT], writes=[v_])
                if d == 1:
                    self.dma(v3(ofb[oi % 2])[:, :, 0:n], OFv[:, :, c0:c0 + n], reads=[self.tkOF], writes=[ofb[oi % 2]])
                    load_heads(gbb[oi % 2], self.GB, c0, n, self.tkGB)

            def pw_stages(oi):
                if oi >= N:
                    return []
                c0, nt, n = geom(oi)
                nch = nt * 2
                q_, s_ = qb[oi % 2], sgb[oi % 2]
                kg, qg, dec = kgs[oi % 2], qgs[oi % 2], decs[oi % 2]
                full = (n == 512)
                sl = (lambda b: b.ap) if full else (lambda b: v3(b)[:, :, 0:n])
                dv = dec.ap.rearrange("p (h c) -> p h c", c=8)

                def stA():
                    for hh in range(4):
                        self.act(v3(lf)[:, hh, 0:n], v3(s_)[:, hh, 0:n], AF.Ln, RD(s_) + [oml, lb], [lf],
                                 scale=oml[0:P, hh:hh + 1], bias=lb[0:P, hh:hh + 1])
                        self.act(v3(kk)[:, hh, 0:n], v3(s_)[:, hh, 0:n], AF.Identity, RD(s_) + [noml, oml], [kk],
                                 scale=noml[0:P, hh:hh + 1], bias=oml[0:P, hh:hh + 1])

                def stB():
                    if full:
                        views = [(lf.ap, G.ap, W)]
                    else:
                        views = [(v3(lf)[:, hh, 0:n], v3(G)[:, hh, 0:n], n) for hh in range(4)]
                    for (src, dst, mlen) in views:
                        if d == 0:
                            S.op("dve", lambda e, src=src, dst=dst, mlen=mlen: e.tensor_tensor_scan(
                                out=dst, data0=self.smask[:, 0:mlen], data1=src, initial=0.0, op0=ALU.mult, op1=ALU.add), [lf, self.cst], [G])
                        else:
                            S.op("dve", lambda e, src=src, dst=dst, mlen=mlen: e.tensor_tensor_scan(
                                out=rev(dst), data0=rev(self.smask[:, 1:mlen + 1]), data1=rev(src), initial=0.0, op0=ALU.mult, op1=ALU.add),
                                [lf, self.cst], [G])

                def stC():
                    self.act(sl(eG), sl(G), AF.Exp, [G], [eG])
                    self.act(sl(enG), sl(G), AF.Exp, [G], [enG], scale=-1.0)
                    g4 = G.ap.rearrange("p (h c s) -> p h c s", h=4, s=64)
                    pos = 63 if d == 0 else 0
                    self.act(dv[:, :, 0:nch], g4[:, :, 0:nch, pos], AF.Exp, [G], [dec])

                def stD():
                    self.tt("dve", sl(kg), sl(kk), sl(enG), ALU.mult, [kk, enG], [kg])
                    self.tt("dve", sl(qg), sl(q_), sl(eG), ALU.mult, RD(q_) + [eG], [qg])
                return [stA, stB, stC, stD]

            def transposes(oi):
                if oi >= N:
                    return
                c0, nt, n = geom(oi)
                ke, ket = kgs[oi % 2], kets[oi % 2]
                kev = ket.ap.rearrange("p (n c) -> p n c", c=384)
                for i in range(nt):
                    pb = self.ps()
                    pbv = pb.ap.bitcast(BF16)
                    S.op("pe", lambda e, pbv=pbv, i=i, ke=ke: [e.transpose(out=pbv[:, hh * 96:(hh + 1) * 96], in_=v3(ke)[:, hh, i * 128:(i + 1) * 128],
                                                                              identity=self.identb[0:96, 0:96]) for hh in range(4)][-1],
                         [ke, self.identb], [pb])
                    self.copy("act", kev[:, i, :], pbv[:, 0:384], [pb], [ket])

            def core_tile(oi, i):
                c0, nt, n = geom(oi)
                kg, qg, dec, ket, v_ = kgs[oi % 2], qgs[oi % 2], decs[oi % 2], kets[oi % 2], vtb[oi % 2]
                o_ = ob[oi % 2]
                vv = v_.ap.rearrange("p (n c) -> p n c", c=384)
                kev = ket.ap.rearrange("p (n c) -> p n c", c=384)
                dv = dec.ap.rearrange("p (h c) -> p h c", c=8)
                psc = self.ps()
                pscv = psc.ap.rearrange("p (h c) -> p h c", c=128)
                S.op("pe", lambda e: [e.matmul(pscv[:, hh, :], lhsT=v3(kg)[:, hh, i * 128:(i + 1) * 128],
                                               rhs=v3(qg)[:, hh, i * 128:(i + 1) * 128], start=True, stop=True)
                                      for hh in range(4)][-1], [kg, qg], [psc])
                sc = scm[i % 2]
                scv = sc.ap.rearrange("p (h c) -> p h c", c=128)
                self.tt("dve", scv, pscv, bcast_mid(mask.ap, 4), ALU.mult, [psc, mask], [sc])
                po = self.ps()
                pov = po.ap.rearrange("p (h c) -> p h c", c=128)
                chunks = (0, 1) if d == 0 else (1, 0)
                pks = []
                for c in chunks:
                    pk = self.ps()
                    pkv = pk.ap.rearrange("p (h v) -> p h v", v=128)
                    S.op("pe", lambda e, pkv=pkv, c=c: [e.matmul(
                        pkv[0:P, hh, 0:96], lhsT=kev[c * 64:(c + 1) * 64, i, hh * 96:(hh + 1) * 96],
                        rhs=vv[c * 64:(c + 1) * 64, i, hh * 96:(hh + 1) * 96], start=True, stop=True) for hh in range(4)][-1],
                        [ket, v_], [pk])
                    pks.append((pk, pkv))

                def update(ci):
                    c = chunks[ci]
                    pk, pkv = pks[ci]
                    dcol = dv[:, :, i * 2 + c]
                    dcb = AP(dcol.tensor, dcol.offset, [list(dcol.ap[0]), list(dcol.ap[1]), [0, 96]])
                    self.tt("dve", T3, S3, pkv[0:P, :, 0:96], ALU.add, [Sst, pk], [tmpS])
                    self.tt("dve", S3, T3, dcb, ALU.mult, [tmpS, dec], [Sst])
                    st["cur"] = (st["cur"] + 1) % 3
                    self.copy("dve", Sbf2[st["cur"]].ap, Sst.ap, [Sst], [Sbf2[st["cur"]]])
                s_in0 = Sbf2[st["cur"]]
                update(0)
                s_in1 = Sbf2[st["cur"]]
                sins = {chunks[0]: s_in0, chunks[1]: s_in1}

                def ogroup(e):
                    ins = None
                    for hh in range(4):
                        e.matmul(pov[0:P, hh, :], lhsT=vv[:, i, hh * 96:(hh + 1) * 96], rhs=scv[:, hh, :], start=True, stop=False)
                        for ci2, c in enumerate(chunks):
                            col = i * 128 + c * 64
                            ins = e.matmul(pov[0:P, hh, c * 64:(c + 1) * 64], lhsT=sins[c][:, hh * 96:(hh + 1) * 96],
                                           rhs=v3(qg)[:, hh, col:col + 64], start=False, stop=(ci2 == 1))
                    return ins
                S.op("pe", ogroup, [v_, sc, s_in0, s_in1, qg], [po])
                update(1)
                def fin():
                    if d == 0:
                        self.copy("act", v3(o_)[:, :, i * 128:(i + 1) * 128], pov[0:P, :, :], [po], [o_])
                    else:
                        self.tt("dve", v3(o_)[:, :, i * 128:(i + 1) * 128], pov[0:P, :, :], v3(ofb[oi % 2])[:, :, i * 128:(i + 1) * 128], ALU.add,
                                [po, ofb[oi % 2]], [o_])
                return fin

            def finish(oi):
                c0, nt, n = geom(oi)
                o_ = ob[oi % 2]
                if d == 0:
                    self.dma(OFv[:, :, c0:c0 + n], v3(o_)[:, :, 0:n], reads=[o_], writes=[self.tkOF])
                    return
                y_ = yb[oi % 2]
                g_ = gbb[oi % 2]
                for hh in range(4):
                    self.act(o2[:, 0:n], v3(o_)[:, hh, 0:n], AF.Square, [o_], [o2])
                    pn = self.ps()
                    self.mm(pn[0:P, 0:n], [(self.ones[0:P, 0:P], o2[:, 0:n])], [self.cst, o2], [pn])
                    self.act(rs[:, 0:n], pn[0:P, 0:n], AF.Ln, [pn, self.epsb], [rs], scale=1.0 / 96.0, bias=self.epsb[0:P, :])
                    self.act(rs[:, 0:n], rs[:, 0:n], AF.Exp, [rs], [rs], scale=-0.5)
                    self.tt("dve", rs[:, 0:n], rs[:, 0:n], v3(o_)[:, hh, 0:n], ALU.mult, [rs, o_], [rs])
                    self.stt("dve", v3(y_)[:, hh, 0:n], rs[:, 0:n], prm[0:P, P_BNW + hh:P_BNW + hh + 1], v3(g_)[:, hh, 0:n], ALU.mult, ALU.mult,
                             [rs, prm] + RD(g_), [y_])
                self.dma(Yv[:, :, c0:c0 + n], v3(y_)[:, :, 0:n], reads=[y_], writes=[self.tkY])

            loadsA(0)
            loadsB(0)
            for f in pw_stages(0):
                f()
            transposes(0)
            loadsA(1)
            for oi in range(N):
                loadsA(oi + 2)
                loadsB(oi + 1)
                stages = pw_stages(oi + 1)
                c0, nt, n = geom(oi)
                tiles = list(range(nt)) if d == 0 else list(range(nt - 1, -1, -1))
                per = (len(stages) + nt - 1) // nt if stages else 0
                pend = None
                for ti, i in enumerate(tiles):
                    fin = core_tile(oi, i)
                    if pend is not None:
                        pend()
                    pend = fin
                    for f in stages[ti * per:(ti + 1) * per]:
                        f()
                pend()
                transposes(oi + 1)
                finish(oi)
        self.release(m_b)

    def phase_o(self, l):
        al = self.alloc
        S = self.S
        m_o = self.mark()
        last = (l == 1)
        self.wout = al(9 * D, BF16)
        self.woutv = self.wout.ap.rearrange("p (c n) -> p c n", n=D)
        if not last:
            self.woutc = al(9 * D, BF16)
            self.woutcv = self.woutc.ap.rearrange("p (c n) -> p c n", n=D)
        else:
            self.fnw = al(D)
            fn = self.fnw_d
            self.dma(self.fnw.ap, AP(fn.tensor, fn.offset, [[0, 128], [1, D]]), writes=[self.fnw])
        rows = [(j * 128, 128) for j in range(3)] + [(384 + h * 96, 96) for h in range(4)] + [(768 + j * 128, 128) for j in range(2)]
        self.wout_rows = rows
        m = self.mark()
        st = [al(D), al(D)]
        for i, (r0, n) in enumerate(rows):
            s = st[i % 2]
            self.dma(s[0:n, :], self.wout_d[l, r0:r0 + n, :], writes=[s])
            self.tt("dve", self.woutv[0:n, i, :], s[0:n, :], self.gb[l][0][0:n, :], ALU.mult, [s, self.gb[l][0]], [self.wout])
            if not last:
                self.tt("dve", self.woutcv[0:n, i, :], s[0:n, :], self.gb[l][1][0:n, :], ALU.mult, [s, self.gb[l][1]], [self.woutc])
        self.release(m)
        yA = [al(3 * 512, BF16) for _ in range(2)]
        yB = [al(4 * 512, BF16, parts=96) for _ in range(2)]
        yC = [al(2 * 512, BF16) for _ in range(2)]
        xt = [al(D) for _ in range(8)]
        xo = [al(D) for _ in range(2 if not last else 6)]
        wsteps = []
        if not last:
            self.alloc_win()
            wst = [al(NCOLS // 4) for _ in range(2)]
            wsteps = self.win_steps(1, wst, ["act"], nsplit=4)
        sq_junk = al(D, BF16)
        ssb = [al(1) for _ in range(4)]
        blocks = BLOCKS[1:] if last else BLOCKS
        cnt = {"x": 0}

        def src_rows(t0):
            if l == 0:
                return self.ctx_d[t0:t0 + 128, :] if t0 < TC else self.X0[t0 - TC:t0 - TC + 128, :]
            return self.X1[t0:t0 + 128, :]

        def load(oi):
            c0, nt = blocks[oi]
            n = nt * 128
            a_, b_, c_ = yA[oi % 2], yB[oi % 2], yC[oi % 2]
            self.dma(a_.ap.rearrange("p (j t) -> p j t", t=512)[:, :, 0:n], self.Y[0:384, :].rearrange("(j p) t -> p j t", p=128)[:, :, c0:c0 + n],
                     reads=[self.tkY], writes=[a_])
            self.dma(b_.ap.rearrange("p (j t) -> p j t", t=512)[:, :, 0:n], self.Y[384:768, :].rearrange("(j p) t -> p j t", p=96)[:, :, c0:c0 + n],
                     reads=[self.tkY], writes=[b_])
            self.dma(c_.ap.rearrange("p (j t) -> p j t", t=512)[:, :, 0:n], self.Y[768:1024, :].rearrange("(j p) t -> p j t", p=128)[:, :, c0:c0 + n],
                     reads=[self.tkY], writes=[c_])
            xs = []
            for i in range(nt):
                b = xt[cnt["x"] % 8]
                cnt["x"] += 1
                rd = [self.tkX1] if l > 0 else [self.tkX0]
                self.dma(b.ap, src_rows(c0 + i * 128), reads=rd, writes=[b])
                xs.append(b)
            return xs
        nxt = load(0)
        oc = 0
        for oi, (c0, nt) in enumerate(blocks):
            xs = nxt
            if oi + 1 < len(blocks):
                nxt = load(oi + 1)
            s_ctx = 1 if c0 < TC else 0
            a_, b_, c_ = yA[oi % 2], yB[oi % 2], yC[oi % 2]
            av = a_.ap.rearrange("p (j t) -> p j t", t=512)
            bv = b_.ap.rearrange("p (j t) -> p j t", t=512)
            cv = c_.ap.rearrange("p (j t) -> p j t", t=512)
            for i in range(nt):
                xb = xs[i]
                lhs = [(av[:, j, i * 128:(i + 1) * 128], 128) for j in range(3)] + [(bv[:, j, i * 128:(i + 1) * 128], 96) for j in range(4)] + \
                      [(cv[:, j, i * 128:(i + 1) * 128], 128) for j in range(2)]
                o_ = xo[oc % len(xo)]
                oc += 1
                if wsteps:
                    wsteps.pop(0)()
                wv, wb = (self.woutcv, self.woutc) if s_ctx else (self.woutv, self.wout)
                for nb in range(2):
                    pb = self.ps()
                    self.mm(pb.ap, [(lh, wv[0:kn, ci, nb * 512:(nb + 1) * 512]) for ci, (lh, kn) in enumerate(lhs)], [a_, b_, c_, wb], [pb])
                    self.tt("dve", o_[:, nb * 512:(nb + 1) * 512], pb.ap, xb[:, nb * 512:(nb + 1) * 512], ALU.add, [pb, xb], [o_])
                t0 = c0 + i * 128
                if not last:
                    self.dma(self.X1[t0:t0 + 128, :], o_.ap, reads=[o_], writes=[self.tkX1])
                else:
                    sb = ssb[oc % 4]
                    self.memset("pool", sb.ap, 0.0, [sb])
                    self.act(sq_junk.ap, o_.ap, AF.Square, [o_], [sq_junk, sb], accum_out=sb.ap)
                    self.act(sb.ap, sb.ap, AF.Sqrt, [sb, self.epsb], [sb], scale=1.0 / D, bias=self.epsb.ap)
                    S.op("dve", lambda e, sb=sb: e.reciprocal(out=sb.ap, in_=sb.ap), [sb], [sb])
                    self.stt("dve", o_.ap, o_.ap, sb.ap, self.fnw.ap, ALU.mult, ALU.mult, [o_, sb, self.fnw], [o_])
                    self.dma(self.out_d[t0 - TC:t0 - TC + 128, :], o_.ap, reads=[o_])
        while wsteps:
            wsteps.pop(0)()
        self.release(m_o)
        self.release(self.m_layer)


def _consts():
    c = np.zeros((128, NCST), np.float32)
    c[:, C_ID:C_ID + 128] = np.eye(128)
    s = np.arange(128)[:, None]
    t = np.arange(128)[None, :]
    same = (s // 64) == (t // 64)
    c[:, C_MF:C_MF + 128] = (same & (s <= t))
    c[:, C_MB:C_MB + 128] = (same & (s >= t))
    c[:, C_ONE:C_ONE + 128] = 1.0
    om = (1.0 / (10000.0 ** (np.arange(256, dtype=np.float64) / 256.0))).astype(np.float32)
    c[:, C_JROW:C_JROW + 256] = om
    c[:, C_JROW + 256:C_JROW + 512] = om
    c[:, C_PH:C_PH + 256] = 0.0
    c[:, C_PH + 256:C_PH + 512] = np.pi / 2
    p = np.arange(128)
    for tile in range(32):
        c[:, C_RCOL + tile] = 2 * tile + p // 64
    c[:, C_CCOL] = p % 64
    m = np.ones(2049, np.float32)
    m[::64] = 0.0
    c[:, C_SMASK:C_SMASK + 2049] = m
    return c


def _pack_params(inp):
    prm = np.zeros((2, 128, NPRM), np.float32)
    for l in range(2):
        prm[l, :, P_NORMW:P_NORMW + 8] = inp["norm_w"][l].reshape(8, 128).T
        cw = inp["a_conv_w"][l]
        prm[l, :, P_CONVW:P_CONVW + 12] = cw.reshape(4, 3, 128).transpose(2, 1, 0).reshape(128, 12)
        prm[l, :, P_CONVB:P_CONVB + 3] = inp["a_conv_b"][l].reshape(3, 128).T
        for name, col in (("a_br", P_BR), ("a_bi", P_BI), ("a_lambda", P_LAM)):
            v = inp[name][l]
            prm[l, :, col:col + 6] = v.reshape(2, 3, 128).transpose(2, 0, 1).reshape(128, 6)
        prm[l, :, P_CNW:P_CNW + 2] = inp["c_norm_w"][l].reshape(2, 128).T
        prm[l, 0:96, P_BNW:P_BNW + 4] = inp["b_norm_w"][l].reshape(4, 96).T
        lg = inp["b_lb_logits"]
        prm[l, 0:96, P_LBL:P_LBL + 12] = lg.reshape(3, 4, 96).transpose(2, 0, 1).reshape(96, 12)
    agw = np.zeros((2, 4, 384, 384), np.float32)
    for l in range(2):
        for d in range(2):
            for gi, name in enumerate(("a_wr", "a_wi")):
                w = inp[name][l, d]
                for h in range(8):
                    agw[l, 2 * d + gi, h * 48:(h + 1) * 48, h * 48:(h + 1) * 48] = w[h]
    cwsT = np.ascontiguousarray(inp["c_ws"].transpose(0, 3, 1, 2)).reshape(2, 128, 512)
    cb = inp["c_bs"]
    cbias = np.zeros((2, 128, 2, 128), np.float32)
    for j in range(2):
        for hl in range(2):
            cbias[:, hl * 64:(hl + 1) * 64, j, :] = cb[:, 2 * j + hl, None, :]
    return prm, agw, cwsT, cbias.reshape(2, 128, 256)


def _win_col_perm():
    perm = np.arange(NCOLS)
    for base in (768, 1152, 1536, 2304):
        for s3 in range(3):
            perm[base + s3 * 128:base + s3 * 128 + 96] = base + s3 * 96 + np.arange(96)
            perm[base + s3 * 128 + 96:base + (s3 + 1) * 128] = base + 288 + 32 * s3 + np.arange(32)
    return perm


def make_in_maps(inp, cores):
    inp = {k: np.ascontiguousarray(np.asarray(v, dtype=np.float32)) for k, v in inp.items()}
    inp["w_in"] = np.ascontiguousarray(inp["w_in"][:, :, _win_col_perm()])
    prm, agw, cwsT, cbias = _pack_params(inp)
    cst = _consts()
    maps = []
    for b in cores:
        cv = np.concatenate([inp["c"][b].reshape(8, 128).T, inp["c_ctx"].reshape(8, 128).T], axis=1)
        maps.append({
            "x": inp["x"][b], "ctx": inp["ctx"][b], "cvec": np.ascontiguousarray(cv),
            "w_mod": inp["w_mod"], "b_mod": inp["b_mod"], "w_in": inp["w_in"], "w_out": inp["w_out"],
            "prm": prm, "agw": agw, "cwsT": cwsT, "cbias": cbias, "fnw": inp["final_norm_w"], "cst": cst,
        })
    return maps


def kernel(**inputs):
    bld = Builder()
    nc = bld.build()
    maps = make_in_maps(inputs, list(range(8)))
    res = run_bass_kernel_spmd(nc, maps, core_ids=list(range(8)))
    return np.stack([np.asarray(r["out"], dtype=np.float32) for r in res.results], axis=0)
```
